# Optimizing a Trainium2 kernel written in Bass

```python
import math
import jax
import jax.numpy as jnp
from jax import lax
import numpy as np

D_MODEL = 1024
BATCH = 4
SEQ = 8192
DEPTH = 2
DEC_BATCH = 16
DEC_SEQ = 2048
PAST_LEN = 128

GRID_W = 64
WIN_R = 8
WIN_C = 16
N_HEADS = 16
HEAD_DIM = D_MODEL // N_HEADS
ATT_W = N_HEADS * HEAD_DIM
EXPAND = 2
D_INNER = EXPAND * D_MODEL
SSM_HEAD_DIM = 64
SSM_HEADS = D_INNER // SSM_HEAD_DIM
N_GROUPS = 4
D_STATE = 128
CONV_W = 5
CONV_DIM = D_INNER + 2 * N_GROUPS * D_STATE
CHUNK = 128
DT_MIN = 0.001
DT_MAX = 0.1
D_FF = 2816
IN_COLS = 3 * ATT_W + D_INNER + CONV_DIM + 2 * SSM_HEADS + 2 * D_MODEL
RMS_EPS = 1e-6

kernel_name = 'hybrid_natten_ssd_macaron_encoder'


def _in_proj_splits():
    sizes = (ATT_W, ATT_W, ATT_W, D_INNER, CONV_DIM, SSM_HEADS, SSM_HEADS, D_MODEL, D_MODEL)
    return [int(s) for s in np.cumsum(sizes)[:-1]]


def rms_norm(x, g):
    x32 = x.astype(jnp.float32)
    y = x32 * lax.rsqrt(jnp.mean(x32 * x32, axis=-1, keepdims=True) + RMS_EPS)
    return (y * g.astype(jnp.float32)).astype(x.dtype)


def swiglu(h, w_up, w_down):
    a, b = jnp.split(h @ w_up, 2, axis=-1)
    return (jax.nn.silu(a) * b) @ w_down


def neighbourhood_attention(q, k, v, rel_bias):
    b, l, h, dh = q.shape
    rows = l // GRID_W
    wr = min(WIN_R, rows)
    qg = q.reshape(b, rows, GRID_W, h, dh)
    kg = k.reshape(b, rows, GRID_W, h, dh)
    vg = v.reshape(b, rows, GRID_W, h, dh)
    cols = jnp.arange(GRID_W)
    col_start = jnp.clip(cols - WIN_C // 2, 0, GRID_W - WIN_C)
    col_valid = (cols[None, :] >= col_start[:, None]) & (cols[None, :] < col_start[:, None] + WIN_C)
    col_idx = jnp.clip(cols[None, :] - cols[:, None] + WIN_C - 1, 0, 2 * WIN_C - 2)

    def row_block(r):
        r0 = jnp.clip(r - WIN_R // 2, 0, rows - wr)
        q_r = lax.dynamic_index_in_dim(qg, r, axis=1, keepdims=False)
        k_r = lax.dynamic_slice_in_dim(kg, r0, wr, axis=1)
        v_r = lax.dynamic_slice_in_dim(vg, r0, wr, axis=1)
        row_idx = r0 + jnp.arange(wr) - r + WIN_R - 1
        bias = rel_bias[:, row_idx[None, :, None], col_idx[:, None, :]]
        s = jnp.einsum('bqhd,bikhd->bhqik', q_r, k_r, preferred_element_type=jnp.float32)
        s = jnp.where(col_valid[:, None, :], s + bias.astype(jnp.float32), -jnp.inf)
        p = jax.nn.softmax(s.reshape(b, h, GRID_W, wr * GRID_W), axis=-1).reshape(s.shape)
        return jnp.einsum('bhqik,bikhd->bqhd', p.astype(v.dtype), v_r)

    out = lax.map(row_block, jnp.arange(rows))
    return jnp.moveaxis(out, 0, 1).reshape(b, l, h * dh)


def centred_depthwise_conv(u, w, bias):
    out = lax.conv_general_dilated(
        u, w[:, None, :].astype(u.dtype), window_strides=(1,),
        padding=[(CONV_W // 2, CONV_W // 2)],
        dimension_numbers=('NWC', 'WIO', 'NWC'),
        feature_group_count=u.shape[-1])
    return out + bias.astype(u.dtype)


def ssd_chunked(x, dt, a, bm, cm):
    b, l, h, p = x.shape
    g, n = bm.shape[2], bm.shape[3]
    j = h // g
    c = l // CHUNK
    q = CHUNK
    xd = (x * dt[..., None].astype(x.dtype)).reshape(b, c, q, g, j, p)
    adt = jnp.moveaxis((dt * a).reshape(b, c, q, g, j), 2, -1)
    a_cum = jnp.cumsum(adt, axis=-1)
    bc = bm.reshape(b, c, q, g, n)
    cc = cm.reshape(b, c, q, g, n)
    causal = jnp.tril(jnp.ones((q, q), dtype=bool))
    diff = a_cum[..., :, None] - a_cum[..., None, :]
    lmat = jnp.exp(jnp.where(causal, diff, -jnp.inf)).astype(x.dtype)
    cb = jnp.einsum('bcqgn,bcsgn->bcgqs', cc, bc)
    y_diag = jnp.einsum('bcgjqs,bcsgjp->bcqgjp', cb[:, :, :, None] * lmat, xd)
    decay_s = jnp.exp(a_cum[..., -1:] - a_cum).astype(x.dtype)
    states = jnp.einsum('bcsgn,bcgjs,bcsgjp->bcgjpn', bc, decay_s, xd)
    chunk_decay = jnp.exp(a_cum[..., -1]).astype(states.dtype)

    def step(carry, inp):
        st, dec = inp
        return carry * dec[..., None, None] + st, carry

    init = jnp.zeros((b, g, j, p, n), states.dtype)
    _, prev = lax.scan(step, init, (jnp.moveaxis(states, 1, 0), jnp.moveaxis(chunk_decay, 1, 0)))
    prev = jnp.moveaxis(prev, 0, 1)
    y_off = jnp.einsum('bcqgn,bcgjpn,bcgjq->bcqgjp', cc, prev, jnp.exp(a_cum).astype(x.dtype))
    return (y_diag + y_off).reshape(b, l, h, p)


def bidir_ssd_mixer(z, xbc, dt_f_raw, dt_b_raw, conv_w, conv_b, dt_bias_f, dt_bias_b,
                    a_log_f, a_log_b, d_skip, ssm_norm):
    b, l, _ = z.shape
    xbc = jax.nn.silu(centred_depthwise_conv(xbc, conv_w, conv_b))
    xs, bm, cm = jnp.split(xbc, [D_INNER, D_INNER + N_GROUPS * D_STATE], axis=-1)
    xh = xs.reshape(b, l, SSM_HEADS, SSM_HEAD_DIM)
    bm = bm.reshape(b, l, N_GROUPS, D_STATE)
    cm = cm.reshape(b, l, N_GROUPS, D_STATE)
    dt_f = jax.nn.softplus(dt_f_raw.astype(jnp.float32) + dt_bias_f.astype(jnp.float32))
    dt_b = jax.nn.softplus(dt_b_raw.astype(jnp.float32) + dt_bias_b.astype(jnp.float32))
    a_f = -jnp.exp(a_log_f.astype(jnp.float32))
    a_b = -jnp.exp(a_log_b.astype(jnp.float32))
    y_f = ssd_chunked(xh, dt_f, a_f, bm, cm)
    flip = lambda t: jnp.flip(t, axis=1)
    y_b = flip(ssd_chunked(flip(xh), flip(dt_b), a_b, flip(bm), flip(cm)))
    y = y_f + y_b + d_skip[:, None].astype(xh.dtype) * xh
    y = y.reshape(b, l, D_INNER) * jax.nn.silu(z)
    y = rms_norm(y.reshape(b, l, N_GROUPS, D_INNER // N_GROUPS),
                 ssm_norm.reshape(N_GROUPS, D_INNER // N_GROUPS))
    return y.reshape(b, l, D_INNER)


def encoder_layer(x, ln_ffn1, w_ffn1_up, w_ffn1_down, ln_mix, w_in, q_norm, k_norm, rel_bias,
                  conv_w, conv_b, dt_bias_fwd, dt_bias_bwd, a_log_fwd, a_log_bwd, d_skip,
                  ssm_norm, w_attn_proj, w_ssm_proj, w_out, ln_ffn2, w_ffn2_up, w_ffn2_down):
    b, l, _ = x.shape
    x = x + 0.5 * swiglu(rms_norm(x, ln_ffn1), w_ffn1_up, w_ffn1_down)
    h = rms_norm(x, ln_mix)
    proj = h @ w_in
    q, k, v, z, xbc, dt_f, dt_b, g_attn, g_ssm = jnp.split(proj, _in_proj_splits(), axis=-1)
    q = rms_norm(q.reshape(b, l, N_HEADS, HEAD_DIM), q_norm) * (HEAD_DIM ** -0.5)
    k = rms_norm(k.reshape(b, l, N_HEADS, HEAD_DIM), k_norm)
    v = v.reshape(b, l, N_HEADS, HEAD_DIM)
    attn = neighbourhood_attention(q, k, v, rel_bias)
    ssm = bidir_ssd_mixer(z, xbc, dt_f, dt_b, conv_w, conv_b, dt_bias_fwd, dt_bias_bwd,
                          a_log_fwd, a_log_bwd, d_skip, ssm_norm)
    merged = jax.nn.sigmoid(g_attn) * (attn @ w_attn_proj) + jax.nn.sigmoid(g_ssm) * (ssm @ w_ssm_proj)
    x = x + merged @ w_out
    x = x + 0.5 * swiglu(rms_norm(x, ln_ffn2), w_ffn2_up, w_ffn2_down)
    return x


def setup_inputs(seed: int = 0) -> dict:
    key = jax.random.key(seed)
    ks = jax.random.split(key, 24)
    f32 = jnp.float32

    def nrm(k, shape, fan_in):
        return jax.random.normal(k, shape, f32) * (fan_in ** -0.5)

    def gain(k, shape):
        return 1.0 + 0.01 * jax.random.normal(k, shape, f32)

    def dt_bias(k):
        u = jax.random.uniform(k, (DEPTH, SSM_HEADS), f32)
        dt = jnp.exp(u * (math.log(DT_MAX) - math.log(DT_MIN)) + math.log(DT_MIN))
        return dt + jnp.log(-jnp.expm1(-dt))

    def a_log(k):
        return jnp.log(jax.random.uniform(k, (DEPTH, SSM_HEADS), f32, minval=1.0, maxval=16.0))

    return {
        'x_prompt': jax.random.normal(ks[0], (BATCH, SEQ, D_MODEL), f32),
        'x_sample': jax.random.normal(ks[1], (DEC_BATCH, DEC_SEQ, D_MODEL), f32),
        'ln_ffn1': gain(ks[2], (DEPTH, D_MODEL)),
        'w_ffn1_up': nrm(ks[3], (DEPTH, D_MODEL, 2 * D_FF), D_MODEL),
        'w_ffn1_down': nrm(ks[4], (DEPTH, D_FF, D_MODEL), D_FF),
        'ln_mix': gain(ks[5], (DEPTH, D_MODEL)),
        'w_in': nrm(ks[6], (DEPTH, D_MODEL, IN_COLS), D_MODEL),
        'q_norm': gain(ks[7], (DEPTH, HEAD_DIM)),
        'k_norm': gain(ks[8], (DEPTH, HEAD_DIM)),
        'rel_bias': 0.1 * jax.random.normal(ks[9], (DEPTH, N_HEADS, 2 * WIN_R - 1, 2 * WIN_C - 1), f32),
        'conv_w': nrm(ks[10], (DEPTH, CONV_W, CONV_DIM), CONV_W),
        'conv_b': 0.01 * jax.random.normal(ks[11], (DEPTH, CONV_DIM), f32),
        'dt_bias_fwd': dt_bias(ks[12]),
        'dt_bias_bwd': dt_bias(ks[13]),
        'a_log_fwd': a_log(ks[14]),
        'a_log_bwd': a_log(ks[15]),
        'd_skip': 1.0 + 0.1 * jax.random.normal(ks[16], (DEPTH, SSM_HEADS), f32),
        'ssm_norm': gain(ks[17], (DEPTH, D_INNER)),
        'w_attn_proj': nrm(ks[18], (DEPTH, ATT_W, D_MODEL), ATT_W),
        'w_ssm_proj': nrm(ks[19], (DEPTH, D_INNER, D_MODEL), D_INNER),
        'w_out': nrm(ks[20], (DEPTH, D_MODEL, D_MODEL), D_MODEL),
        'ln_ffn2': gain(ks[21], (DEPTH, D_MODEL)),
        'w_ffn2_up': nrm(ks[22], (DEPTH, D_MODEL, 2 * D_FF), D_MODEL),
        'w_ffn2_down': nrm(ks[23], (DEPTH, D_FF, D_MODEL), D_FF),
    }


def reference(x_prompt, x_sample, ln_ffn1, w_ffn1_up, w_ffn1_down, ln_mix, w_in, q_norm, k_norm,
              rel_bias, conv_w, conv_b, dt_bias_fwd, dt_bias_bwd, a_log_fwd, a_log_bwd, d_skip,
              ssm_norm, w_attn_proj, w_ssm_proj, w_out, ln_ffn2, w_ffn2_up, w_ffn2_down):
    params = (ln_ffn1, w_ffn1_up, w_ffn1_down, ln_mix, w_in, q_norm, k_norm, rel_bias,
              conv_w, conv_b, dt_bias_fwd, dt_bias_bwd, a_log_fwd, a_log_bwd, d_skip,
              ssm_norm, w_attn_proj, w_ssm_proj, w_out, ln_ffn2, w_ffn2_up, w_ffn2_down)

    def run_trunk(x):
        for layer in range(DEPTH):
            x = encoder_layer(x, *[w[layer] for w in params])
        return x

    y_prompt = run_trunk(x_prompt)
    y_sample = run_trunk(x_sample)
    return (y_prompt, y_sample)
```

```python
import numpy as np
import ml_dtypes
import concourse.bass as bass
import concourse.mybir as mybir
from concourse.bass_utils import run_bass_kernel_spmd

F32 = mybir.dt.float32
BF16 = mybir.dt.bfloat16
AF = mybir.ActivationFunctionType
ALU = mybir.AluOpType
AX = mybir.AxisListType


class Trk:
    __slots__ = ("name", "lw", "rd", "dsem", "dgen")

    def __init__(self, name):
        self.name = name
        self.lw = None
        self.rd = {}
        self.dsem = None
        self.dgen = -1


class V:
    __slots__ = ("t", "ap")

    def __init__(self, t, ap):
        self.t = t
        self.ap = ap

    def __getitem__(self, idx):
        return V(self.t, self.ap[idx])

    def re(self, s, **kw):
        return V(self.t, self.ap.rearrange(s, **kw))

    def bc(self, shape):
        return V(self.t, self.ap.broadcast_to(shape))


class DSem:
    __slots__ = ("idx", "count", "last")

    def __init__(self, idx):
        self.idx = idx
        self.count = 0
        self.last = None


class Prog:
    STREAMS = ("sp", "act", "dve", "pool", "pe")

    def __init__(self, nc):
        self.nc = nc
        self.ops = {s: [] for s in self.STREAMS}
        self.dsems = []
        self.free_ds = []
        self.gen = 0
        self.rr = 0
        self.nbuf = 0

    def sb(self, shape, dt, name=None):
        self.nbuf += 1
        name = name or f"sb{self.nbuf}"
        h = self.nc.alloc_sbuf_tensor(name, list(shape), dt)
        return V(Trk(name), h.ap() if hasattr(h, "ap") else h[:])

    def ps(self, name):
        h = self.nc.alloc_psum_tensor(name, [128, 512], F32)
        return V(Trk(name), h.ap() if hasattr(h, "ap") else h[:])

    def dram(self, name, shape, dt, kind="Internal"):
        h = self.nc.dram_tensor(name, list(shape), dt, kind=kind)
        return h.ap()

    def _deps(self, stream, reads, writes):
        deps = set()
        for r in reads:
            if r.lw is not None:
                deps.add(r.lw)
        for w in writes:
            if w.lw is not None:
                deps.add(w.lw)
            for ev in w.rd.values():
                deps.add(ev)
        return deps

    def _commit(self, ev, key, reads, writes):
        for w in writes:
            w.lw = ev
            w.rd = {}
        for r in reads:
            if r not in writes:
                r.rd[key] = ev

    def op(self, stream, fn, reads, writes):
        reads = [x.t if isinstance(x, V) else x for x in reads if x is not None]
        writes = [x.t if isinstance(x, V) else x for x in writes if x is not None]
        deps = self._deps(stream, reads, writes)
        lst = self.ops[stream]
        ev = ("E", stream, len(lst))
        if stream == "pe":
            deps = {d for d in deps if not (d[0] == "E" and d[1] == "pe")}
        else:
            raw = {r.lw for r in reads if r.lw is not None}
            deps = {d for d in deps if not (d[0] == "E" and d[1] == stream and d not in raw)}
        lst.append(["c", fn, deps, ev, False])
        self._commit(ev, stream, reads, writes)

    def dma(self, stream, out, in_, reads, writes, sem_owner=None):
        reads = [x.t if isinstance(x, V) else x for x in reads if x is not None]
        writes = [x.t if isinstance(x, V) else x for x in writes if x is not None]
        owner = sem_owner if sem_owner is not None else (writes[0] if writes else reads[0])
        if isinstance(owner, V):
            owner = owner.t
        if owner.dsem is None or owner.dgen != self.gen:
            if self.free_ds:
                owner.dsem = self.free_ds.pop()
            elif len(self.dsems) < 80:
                owner.dsem = DSem(len(self.dsems))
                self.dsems.append(owner.dsem)
            else:
                owner.dsem = self.dsems[self.rr % len(self.dsems)]
                self.rr += 1
            owner.dgen = self.gen
        ds = owner.dsem
        deps = self._deps(stream, reads, writes)
        if ds.last is not None:
            deps.add(ds.last)
        ds.count += 16
        ev = ("D", ds.idx, ds.count)
        ds.last = ev
        self.ops[stream].append(["d", (out, in_), deps, ev, True])
        self._commit(ev, ("D", ds.idx), reads, writes)
        return ev

    def emit(self, final_events):
        nc = self.nc
        for s in self.STREAMS:
            for o in self.ops[s]:
                for d in o[2]:
                    if d[0] == "E":
                        self.ops[d[1]][d[2]][4] = True
        for d in final_events:
            if d[0] == "E":
                self.ops[d[1]][d[2]][4] = True
        cnt = {}
        for s in self.STREAMS:
            c = 0
            arr = []
            for o in self.ops[s]:
                if o[0] == "c" and o[4]:
                    c += 1
                arr.append(c)
            cnt[s] = arr
        esem = {s: nc.alloc_semaphore(f"es_{s}") for s in self.STREAMS}
        dsem = [nc.alloc_semaphore(f"ds_{i}") for i in range(len(self.dsems))]

        def resolve(d):
            if d[0] == "E":
                return ("E", d[1]), esem[d[1]], cnt[d[1]][d[2]]
            return ("D", d[1]), dsem[d[1]], d[2]

        engmap = {"sp": "sync", "act": "scalar", "dve": "vector", "pool": "gpsimd", "pe": "tensor"}

        def run_stream(s, eng, extra_final=None):
            waited = {}
            def do_waits(deps):
                need = {}
                for d in deps:
                    k, sem, val = resolve(d)
                    if val > need.get(k, (None, 0))[1]:
                        need[k] = (sem, val)
                for k, (sem, val) in need.items():
                    if waited.get(k, 0) >= val:
                        continue
                    eng.wait_ge(sem, val)
                    waited[k] = val
            for o in self.ops[s]:
                do_waits(o[2])
                if o[0] == "c":
                    ins = o[1](eng)
                    if o[4]:
                        ins.then_inc(esem[s], 1)
                else:
                    out, in_ = o[1]
                    eng.dma_start(out=out, in_=in_).then_inc(dsem[o[3][1]], 16)
            if extra_final:
                do_waits(extra_final)

        with nc.Block() as block:
            @block.sync
            def _(e):
                run_stream("sp", e, final_events)

            @block.scalar
            def _(e):
                run_stream("act", e)

            @block.vector
            def _(e):
                run_stream("dve", e)

            @block.gpsimd
            def _(e):
                run_stream("pool", e)

            @block.tensor
            def _(e):
                run_stream("pe", e)

    def barrier(self):
        evs = set()
        for s in self.STREAMS:
            if self.ops[s]:
                evs.add(self.ops[s][-1][3])
        for ds in self.dsems:
            if ds.last is not None:
                evs.add(ds.last)
        for s in self.STREAMS:
            deps = {d for d in evs if not (d[0] == "E" and d[1] == s)}
            lst = self.ops[s]
            lst.append(["c", (lambda e: e.nop()), deps, ("E", s, len(lst)), False])
        self.gen += 1
        self.free_ds = list(self.dsems)

    @staticmethod
    def _a(x):
        return x.ap if isinstance(x, V) else x

    def _eng(self, e, name):
        return e

    def mm(self, out, lhsT, rhs, start=True, stop=True):
        o, l, r = out.ap, lhsT.ap, rhs.ap
        self.op("pe", lambda e: e.matmul(o, lhsT=l, rhs=r, start=start, stop=stop), [lhsT, rhs], [out])

    def tr(self, out, in_, ident):
        o, i, d = out.ap, in_.ap, ident.ap
        self.op("pe", lambda e: e.transpose(o, i, d), [in_, ident], [out])

    def act(self, out, in_, func, bias=0.0, scale=1.0, accum=None):
        o, i, b, s = out.ap, in_.ap, self._a(bias), self._a(scale)
        ac = accum.ap if accum is not None else None
        rd = [in_] + [x for x in (bias, scale) if isinstance(x, V)]
        wr = [out] + ([accum] if accum is not None else [])
        if ac is None:
            self.op("act", lambda e: e.activation(out=o, in_=i, func=func, bias=b, scale=s), rd, wr)
        else:
            self.op("act", lambda e: e.activation(out=o, in_=i, func=func, bias=b, scale=s, accum_out=ac), rd, wr)

    def tt(self, eng, out, in0, in1, op):
        o, a, b = out.ap, in0.ap, in1.ap
        self.op(eng, lambda e: e.tensor_tensor(out=o, in0=a, in1=b, op=op), [in0, in1], [out])

    def ts(self, eng, out, in0, s1, s2, op0, op1=None):
        o, a, x1, x2 = out.ap, in0.ap, self._a(s1), self._a(s2)
        rd = [in0] + [x for x in (s1, s2) if isinstance(x, V)]
        if op1 is None:
            self.op(eng, lambda e: e.tensor_scalar(out=o, in0=a, scalar1=x1, scalar2=None, op0=op0), rd, [out])
        else:
            self.op(eng, lambda e: e.tensor_scalar(out=o, in0=a, scalar1=x1, scalar2=x2, op0=op0, op1=op1), rd, [out])

    def stt(self, eng, out, in0, scalar, in1, op0, op1):
        o, a, sc, b = out.ap, in0.ap, self._a(scalar), in1.ap
        rd = [in0, in1] + ([scalar] if isinstance(scalar, V) else [])
        self.op(eng, lambda e: e.scalar_tensor_tensor(out=o, in0=a, scalar=sc, in1=b, op0=op0, op1=op1), rd, [out])

    def cp(self, eng, out, in_):
        o, i = out.ap, in_.ap
        if eng == "act":
            self.op("act", lambda e: e.activation(out=o, in_=i, func=AF.Copy), [in_], [out])
        else:
            self.op(eng, lambda e: e.tensor_copy(out=o, in_=i), [in_], [out])

    def recip(self, out, in_):
        o, i = out.ap, in_.ap
        self.op("dve", lambda e: e.reciprocal(out=o, in_=i), [in_], [out])

    def memset(self, eng, out, val):
        o = out.ap
        self.op(eng, lambda e: e.memset(o, val), [], [out])

    def ld(self, q, dst, src_ap):
        return self.dma(q, dst.ap, src_ap, [], [dst])

    def st(self, q, dst_ap, src):
        return self.dma(q, dst_ap, src.ap, [src], [], sem_owner=src)


class Arena:
    def __init__(self, nc, nbytes):
        self.h = nc.alloc_sbuf_tensor("arena", [128, nbytes // 2], BF16)
        self.ap = self.h.ap()
        self.nbytes = nbytes
        self.off = 0
        self.n = 0

    def alloc(self, shape, dt, name=None):
        esz = mybir.dt.size(dt)
        n = int(np.prod(shape[1:]))
        nb = (n * esz + 31) // 32 * 32
        assert self.off + nb <= self.nbytes, f"arena overflow {self.off}+{nb} > {self.nbytes}"
        a = self.ap[0:shape[0], self.off // 2:(self.off + n * esz) // 2]
        if dt != BF16:
            a = a.bitcast(dt)
        if len(shape) > 2:
            names = " ".join(f"d{i}" for i in range(len(shape) - 1))
            kw = {f"d{i}": shape[i + 1] for i in range(len(shape) - 2)}
            a = a.rearrange(f"p ({names}) -> p {names}", **kw)
        self.off += nb
        self.n += 1
        return V(Trk(name or f"a{self.n}"), a)

    def mark(self):
        return self.off

    def reset(self, m):
        self.off = m


D = 1024
DFF = 2816
NH = 16
SH = 32
INC = 10304
EPS = 1e-6
NEG = -30000.0
NBLK = 67
B_UP1, B_DN1, B_WIN, B_ATT, B_SSM, B_OUT, B_UP2, B_DN2 = 0, 11, 19, 40, 42, 46, 48, 59
NTAB = 22
LN8 = -2.0794415416798357
import os
PIPE = os.environ.get("K_PIPE", "1") == "1"
EMBF = os.environ.get("K_EMBF", "1") == "1"


class Builder:
    def __init__(self, NT, nlayers=2, dbg=None, stop_after=None):
        self.NT = NT
        self.NTL = NT // 512
        self.NCH = NT // 128
        self.NG = NT // 256
        self.R = NT // 64
        self.nlayers = nlayers
        self.dbg = dbg or []
        self.stop_after = stop_after
        nc = self.nc = bass.Bass("TRN2", target_bir_lowering=False)
        P = self.P = Prog(nc)
        dr = P.dram
        EI = "ExternalInput"
        self.x_in = dr("x", [NT, D], F32, EI)
        self.w_up = [dr("w_ffn1_up", [2, D, 2 * DFF], F32, EI), dr("w_ffn2_up", [2, D, 2 * DFF], F32, EI)]
        self.w_dn = [dr("w_ffn1_down", [2, DFF, D], F32, EI), dr("w_ffn2_down", [2, DFF, D], F32, EI)]
        self.w_in = dr("w_in", [2, D, INC], F32, EI)
        self.w_ap = dr("w_attn_proj", [2, D, D], F32, EI)
        self.w_sp = dr("w_ssm_proj", [2, 2 * D, D], F32, EI)
        self.w_o = dr("w_out", [2, D, D], F32, EI)
        self.lnp = dr("lnp", [128, 2 * 3 * 8], F32, EI)
        self.qkn = dr("qkn", [128, 4], F32, EI)
        self.tabf = dr("tabf", [2, 128, NH * NTAB * 64], F32, EI)
        self.relrow = dr("relrow", [2, 1, NH * 15 * 31], F32, EI)
        self.qkrow = dr("qkrow", [2, 1, 128], F32, EI)
        self.convw = dr("convw", [2, 128, 120], F32, EI)
        self.convbc = dr("convbc", [2, 128, 24], F32, EI)
        self.convbr = dr("convbr", [2, 1, 3072], F32, EI)
        self.dtb = dr("dtb", [2, 1, 64], F32, EI)
        self.alog = dr("alog", [2, 1, 64], F32, EI)
        self.dsk = dr("dsk", [2, 1, 32], F32, EI)
        self.snw = dr("snw", [2, 1, 2048], F32, EI)
        self.cst = dr("cst", [128, 5 * 128], F32, EI)
        self.indm = dr("indm", [12, 768], F32, EI)
        self.rowm = dr("rowm", [12, self.NG * 256], F32, EI)
        self.flag = dr("flag", [128, 1], F32, EI)
        self.y_out = dr("y", [NT, D], F32, "ExternalOutput")
        kind = "Internal"
        self.Wb = dr("Wb", [2 * NBLK, 128, 4096], BF16, kind)
        self.X1 = dr("X1", [self.NTL, 128, 8 * 512], F32, kind)
        self.XL = dr("XL", [self.NTL, 128, 8 * 512], F32, kind)
        self.Qs = dr("Qs", [8, 128, NT], BF16, kind)
        self.Ks = dr("Ks", [8, 128, NT], BF16, kind)
        self.VAs = dr("VAs", [NT, 1040], BF16, kind)
        self.Zs = dr("Zs", [NT, 2048], BF16, kind)
        self.Us = dr("Us", [24, 128, NT + 4], BF16, kind)
        self.Gs = dr("Gs", [16, 128, NT], BF16, kind)
        self.DTs = dr("DTs", [NT, 64], F32, kind)
        self.ACSs = dr("ACSs", [NT, 64], F32, kind)
        self.ACSF = dr("ACSF", [64, NT], F32, kind)
        self.ACSE = dr("ACSE", [self.NCH, 64], F32, kind)
        self.XSs = dr("XSs", [NT, 2048], BF16, kind)
        self.BTs = dr("BTs", [NT, 512], BF16, kind)
        self.BFs = dr("BFs", [4, 128, NT], BF16, kind)
        self.CFs = dr("CFs", [4, 128, NT], BF16, kind)
        self.PBs = dr("PBs", [self.NCH, 128, 2048], BF16, kind)
        self.SSMs = dr("SSMs", [16, 128, NT], BF16, kind)
        self.ATTs = dr("ATTs", [8, 128, NT], BF16, kind)
        self.dbg_out = {}
        self.A = Arena(nc, 204800)
        self.psb = [P.ps(f"psb{i}") for i in range(8)]
        self.psi = 0
        self.final = []

    def nextps(self):
        p = self.psb[self.psi % 8]
        self.psi += 1
        return p

    def persistent(self):
        P, A = self.P, self.A
        c32 = A.alloc([128, 640], F32, "c32")
        P.ld("sp", c32, self.cst)
        self.identf = c32[:, 0:128]
        self.MF = c32[:, 128:256]
        self.MB = c32[:, 256:384]
        cb = A.alloc([128, 640], BF16, "cbf")
        P.cp("dve", cb, c32)
        self.identb = cb[:, 0:128]
        self.blkones = cb[:, 384:512]
        self.onesb = cb[:, 512:640]
        ind32 = A.alloc([128, 768], F32)
        self.ind = A.alloc([128, 768], BF16, "ind")
        for p0 in (0, 64):
            P.ld("sp", ind32[p0:p0 + 12, :], self.indm)
            P.cp("dve", self.ind[p0:p0 + 12, :], ind32[p0:p0 + 12, :])
        self.lnpc = A.alloc([128, 48], F32, "lnp")
        P.ld("sp", self.lnpc, self.lnp)
        self.qknc = A.alloc([128, 4], F32, "qkn")
        P.ld("sp", self.qknc, self.qkn)
        self.flagc = A.alloc([128, 1], F32, "flag")
        P.ld("sp", self.flagc, self.flag)
        self.rowmb = A.alloc([128, self.NG * 256], BF16, "rowm")
        m = A.mark()
        for g0 in range(0, self.NG, 8):
            n = min(8, self.NG - g0)
            t = A.alloc([128, n * 256], F32)
            for p0 in (0, 64):
                P.ld("sp", t[p0:p0 + 12, :], self.rowm[:, g0 * 256:(g0 + n) * 256])
                P.cp("dve", self.rowmb[p0:p0 + 12, g0 * 256:(g0 + n) * 256], t[p0:p0 + 12, :])
        z = A.alloc([128, 24, 2], BF16)
        P.memset("pool", z, 0.0)
        P.st("sp", self.Us[:, :, 0:2].rearrange("k p t -> p k t"), z)
        P.st("sp", self.Us[:, :, self.NT + 2:self.NT + 4].rearrange("k p t -> p k t"), z)
        P.barrier()
        A.reset(m)
        self.base = A.mark()

    def cast_weights(self):
        P, A = self.P, self.A
        m = A.mark()
        NBW = 4
        st32 = [A.alloc([128, 4096], F32) for _ in range(NBW)]
        stb = [A.alloc([128, 4096], BF16) for _ in range(NBW)]
        engs = ["dve", "act", "pool"]
        n = 0
        for L in range(self.nlayers):
            jobs = []
            for f, (bu, bd) in enumerate(((B_UP1, B_DN1), (B_UP2, B_DN2))):
                wu = self.w_up[f][L].rearrange("(kc p) n -> p kc n", p=128)
                wd = self.w_dn[f][L].rearrange("(kc p) n -> p kc n", p=128)
                for b in range(11):
                    jobs.append((bu + b, 8, 512, [(0, 256, wu[:, :, b * 256:(b + 1) * 256]),
                                                  (256, 512, wu[:, :, DFF + b * 256:DFF + (b + 1) * 256])]))
                for mm_ in range(8):
                    jobs.append((bd + mm_, 22, 128, [(0, 128, wd[:, :, mm_ * 128:(mm_ + 1) * 128])]))
            wi = self.w_in[L].rearrange("(kc p) n -> p kc n", p=128)
            for b in range(16):
                jobs.append((B_WIN + b, 8, 512, [(0, 512, wi[:, :, b * 512:(b + 1) * 512])]))
            for b in range(4):
                jobs.append((B_WIN + 16 + b, 8, 512, [(0, 512, wi[:, :, 8256 + b * 512:8256 + (b + 1) * 512])]))
            jobs.append((B_WIN + 20, 8, 64, [(0, 64, wi[:, :, 8192:8256])]))
            wa = self.w_ap[L].rearrange("(kc p) n -> p kc n", p=128)
            ws_ = self.w_sp[L].rearrange("(kc p) n -> p kc n", p=128)
            wo = self.w_o[L].rearrange("(kc p) n -> p kc n", p=128)
            for b in range(2):
                jobs.append((B_ATT + b, 8, 512, [(0, 512, wa[:, :, b * 512:(b + 1) * 512])]))
            for b in range(4):
                jobs.append((B_SSM + b, 16, 256, [(0, 256, ws_[:, :, b * 256:(b + 1) * 256])]))
            for b in range(2):
                jobs.append((B_OUT + b, 8, 512, [(0, 512, wo[:, :, b * 512:(b + 1) * 512])]))
            for (bi, kc, nb, parts) in jobs:
                s32 = st32[n % NBW]
                sb_ = stb[n % NBW]
                v32 = s32[:, 0:kc * nb].re("p (k n) -> p k n", k=kc)
                for (c0, c1, src) in parts:
                    P.dma("sp" if n % 2 == 0 else "act", v32.ap[:, :, c0:c1], src, [], [s32])
                P.cp(engs[n % 3], sb_[:, 0:kc * nb], s32[:, 0:kc * nb])
                P.dma("pool", self.Wb[L * NBLK + bi][:, 0:kc * nb], sb_.ap[:, 0:kc * nb], [sb_], [], sem_owner=sb_)
                n += 1
        P.barrier()
        A.reset(m)

    class WStream:
        def __init__(self, bld, plan, R=4, q="sp"):
            self.b = bld
            self.plan = plan
            self.R = R
            self.ahead = R - 1
            self.q = q
            self.slots = [bld.A.alloc([128, 4096], BF16, f"wslot{i}") for i in range(R)]
            self.issued = 0
            self.cur = 0

        def next(self):
            i = self.cur
            lim = min(len(self.plan), i + self.ahead + 1)
            while self.issued < lim:
                j = self.issued
                ap, n = self.plan[j]
                s = self.slots[j % self.R]
                self.b.P.dma(self.q, s.ap[:, 0:n], ap[:, 0:n], [], [s])
                self.issued += 1
            self.cur += 1
            return self.slots[i % self.R]

    def wblk(self, L, bi, n=4096):
        return (self.Wb[L * NBLK + bi], n)

    def rmsnorm(self, xT, gcol, h, sq, tmp):
        P = self.P
        P.act(sq, xT, AF.Square)
        ps = self.nextps()
        for kc in range(8):
            P.mm(ps, self.onesb, sq[:, kc, :], start=(kc == 0), stop=(kc == 7))
        P.act(tmp, ps, AF.Ln, bias=EPS, scale=1.0 / D)
        P.act(tmp, tmp, AF.Exp, scale=-0.5)
        for kc in range(8):
            P.stt("dve", h[:, kc, :], xT[:, kc, :], gcol[:, kc:kc + 1], tmp, ALU.mult, ALU.mult)

    def ffn(self, ws, xT, h, actb, tmps):
        P = self.P
        for j in range(22):
            if j % 2 == 0:
                blk = ws.next()[:, 0:4096].re("p (k n) -> p k n", k=8)
            ca = (j % 2) * 128
            pa, pb = self.nextps(), self.nextps()
            for kc in range(8):
                P.mm(pa, blk[:, kc, ca:ca + 128], h[:, kc, :], start=(kc == 0), stop=(kc == 7))
            for kc in range(8):
                P.mm(pb, blk[:, kc, 256 + ca:256 + ca + 128], h[:, kc, :], start=(kc == 0), stop=(kc == 7))
            t = tmps[j % len(tmps)]
            P.act(t, pa, AF.Silu)
            P.tt("dve", actb[:, j, :], t, pb, ALU.mult)
        for m in range(8):
            blk = ws.next()[:, 0:22 * 128].re("p (k n) -> p k n", k=22)
            po = self.nextps()
            for j in range(22):
                P.mm(po, blk[:, j, :], actb[:, j, :], start=(j == 0), stop=(j == 21))
            P.stt("dve", xT[:, m, :], po, 0.5, xT[:, m, :], ALU.mult, ALU.add)

    def stage1(self, L):
        P, A, NT = self.P, self.A, self.NT
        m0 = A.mark()
        qa = "pool"
        plan = []
        for i in range(self.NTL):
            plan += [self.wblk(L, B_UP1 + b) for b in range(11)]
            plan += [self.wblk(L, B_DN1 + b, 22 * 128) for b in range(8)]
            plan += [self.wblk(L, B_WIN + b) for b in range(20)]
            plan += [self.wblk(L, B_WIN + 20, 512)]
        ws = self.WStream(self, plan)
        xT = A.alloc([128, 8, 512], F32, "xT")
        xtok = A.alloc([128, 4, 1024], F32, "xtok") if L == 0 else None
        h = A.alloc([128, 8, 512], BF16, "h")
        sq = A.alloc([128, 8, 512], BF16, "sq")
        actb = A.alloc([128, 22, 512], BF16, "actb")
        tmps = [A.alloc([128, 512], F32, f"tmp{i}") for i in range(4)]
        rst = A.alloc([128, 512], F32, "rst")
        sqt = [A.alloc([128, 512], BF16, f"sqt{i}") for i in range(2)]
        stg = [A.alloc([128, 4, 512], BF16, f"stg{i}") for i in range(4)]
        vst = A.alloc([128, 4, NH * 65], BF16, "vst")
        dtst = A.alloc([128, 4, 64], F32, "dtst")
        acst = A.alloc([128, 4, 64], F32, "acst")
        acsf = A.alloc([64, 512], F32, "acsf")
        dsm = [A.alloc([128, 64], F32, f"dsm{i}") for i in range(4)]
        dtbb = A.alloc([128, 64], F32, "dtbb")
        ab = A.alloc([128, 64], F32, "ab")
        P.ld(qa, dtbb, self.dtb[L].broadcast_to([128, 64]))
        P.ld(qa, ab, self.alog[L].broadcast_to([128, 64]))
        P.act(ab, ab, AF.Exp)
        P.ts("dve", ab, ab, -1.0, None, ALU.mult)
        P.memset("pool", vst, 1.0)
        g1 = self.lnpc[:, (L * 3 + 0) * 8:(L * 3 + 0) * 8 + 8]
        g2 = self.lnpc[:, (L * 3 + 1) * 8:(L * 3 + 1) * 8 + 8]
        gq = self.qknc[:, L * 2:L * 2 + 1]
        gk = self.qknc[:, L * 2 + 1:L * 2 + 2]
        si = 0
        for i in range(self.NTL):
            t0 = i * 512
            if L == 0:
                P.ld(qa, xtok, self.x_in[t0:t0 + 512, :].rearrange("(s p) d -> p s d", p=128))
                for sub in range(4):
                    for kc in range(8):
                        if kc % 4 == 0:
                            ps = self.nextps()
                        P.tr(ps[:, (kc % 4) * 128:(kc % 4) * 128 + 128], xtok[:, sub, kc * 128:(kc + 1) * 128], self.identf)
                        if kc % 4 == 3:
                            P.cp("act" if (kc // 4) % 2 else "dve",
                                 xT[:, kc - 3:kc + 1, sub * 128:(sub + 1) * 128],
                                 ps.re("p (k t) -> p k t", k=4))
            else:
                P.ld(qa, xT, self.XL[i].rearrange("p (k t) -> p k t", k=8))
            self.rmsnorm(xT, g1, h, sq, rst)
            self.ffn(ws, xT, h, actb, tmps)
            P.st(qa, self.X1[i].rearrange("p (k t) -> p k t", k=8), xT)
            self.rmsnorm(xT, g2, h, sq, rst)
            pend = None

            def finish(pd_):
                pq_, s__, c_, isq_, sg_, b_ = pd_
                pst = self.nextps()
                P.mm(pst, self.blkones, s__)
                t = tmps[c_ % 4]
                P.act(t, pst, AF.Ln, bias=EPS, scale=1.0 / 64.0)
                P.act(t, t, AF.Exp, bias=(LN8 if isq_ else 0.0), scale=-0.5)
                P.stt("dve", sg_[:, c_, :], pq_, gq if isq_ else gk, t, ALU.mult, ALU.mult)
                if c_ == 3:
                    dst = (self.Qs if isq_ else self.Ks)[(b_ % 2) * 4:(b_ % 2) * 4 + 4, :, t0:t0 + 512].rearrange("c p t -> p c t")
                    P.st(qa, dst, sg_)

            for b in range(4):
                blk = ws.next().re("p (k n) -> p k n", k=8)
                isq = b < 2
                sg = stg[si % 4]; si += 1
                for c in range(4):
                    pq = self.nextps()
                    for kc in range(8):
                        P.mm(pq, blk[:, kc, c * 128:(c + 1) * 128], h[:, kc, :], start=(kc == 0), stop=(kc == 7))
                    s_ = sqt[c % 2]
                    P.act(s_, pq, AF.Square)
                    if pend is not None:
                        finish(pend)
                    pend = (pq, s_, c, isq, sg, b)
            finish(pend)
            for b in range(2):
                blk = ws.next().re("p (k n) -> p k n", k=8)
                for sub in range(4):
                    pv = self.nextps()
                    for kc in range(8):
                        P.mm(pv, h[:, kc, sub * 128:(sub + 1) * 128], blk[:, kc, :], start=(kc == 0), stop=(kc == 7))
                    dstv = vst[:, sub, :].re("p (h e) -> p h e", e=65)[:, b * 8:(b + 1) * 8, 0:64]
                    P.cp("act" if sub % 2 else "dve", dstv, pv.re("p (h e) -> p h e", e=64))
            P.st(qa, self.VAs[t0:t0 + 512, :].rearrange("(s p) f -> p s f", p=128), vst)
            for b in range(4):
                blk = ws.next().re("p (k n) -> p k n", k=8)
                sg = stg[si % 4]; si += 1
                for sub in range(4):
                    pz = self.nextps()
                    for kc in range(8):
                        P.mm(pz, h[:, kc, sub * 128:(sub + 1) * 128], blk[:, kc, :], start=(kc == 0), stop=(kc == 7))
                    P.act(sg[:, sub, :], pz, AF.Silu)
                P.st(qa, self.Zs[t0:t0 + 512, b * 512:(b + 1) * 512].rearrange("(s p) f -> p s f", p=128), sg)
            for b in range(6):
                blk = ws.next().re("p (k n) -> p k n", k=8)
                sg = stg[si % 4]; si += 1
                for c in range(4):
                    pu = self.nextps()
                    for kc in range(8):
                        P.mm(pu, blk[:, kc, c * 128:(c + 1) * 128], h[:, kc, :], start=(kc == 0), stop=(kc == 7))
                    P.cp("act" if c % 2 else "dve", sg[:, c, :], pu)
                P.st(qa, self.Us[b * 4:b * 4 + 4, :, 2 + t0:2 + t0 + 512].rearrange("c p t -> p c t"), sg)
            for b in range(4):
                blk = ws.next().re("p (k n) -> p k n", k=8)
                sg = stg[si % 4]; si += 1
                for c in range(4):
                    pg = self.nextps()
                    for kc in range(8):
                        P.mm(pg, blk[:, kc, c * 128:(c + 1) * 128], h[:, kc, :], start=(kc == 0), stop=(kc == 7))
                    P.act(sg[:, c, :], pg, AF.Sigmoid)
                P.st(qa, self.Gs[b * 4:b * 4 + 4, :, t0:t0 + 512].rearrange("c p t -> p c t"), sg)
            blk = ws.next()[:, 0:512].re("p (k n) -> p k n", k=8)
            for sub in range(4):
                pd = self.nextps()
                for kc in range(8):
                    P.mm(pd[:, 0:64], h[:, kc, sub * 128:(sub + 1) * 128], blk[:, kc, :], start=(kc == 0), stop=(kc == 7))
                t1, t2 = dsm[0], dsm[1]
                P.tt("dve", t1, pd[:, 0:64], dtbb, ALU.add)
                P.act(t1, t1, AF.Exp)
                P.act(dtst[:, sub, :], t1, AF.Ln, bias=1.0)
                P.tt("dve", t2, dtst[:, sub, :], ab, ALU.mult)
                pc = self.nextps()
                P.mm(pc[:, 0:32], self.MF, t2[:, 0:32])
                P.mm(pc[:, 32:64], self.MB, t2[:, 32:64])
                P.cp("dve", acst[:, sub, :], pc[:, 0:64])
                pf = self.nextps()
                P.mm(pf[0:32, 0:128], t2[:, 0:32], self.MF)
                P.mm(pf[32:64, 0:128], t2[:, 32:64], self.MB)
                P.cp("act", acsf[:, sub * 128:(sub + 1) * 128], pf[0:64, 0:128])
            P.st(qa, self.DTs[t0:t0 + 512, :].rearrange("(s p) f -> p s f", p=128), dtst)
            P.st(qa, self.ACSs[t0:t0 + 512, :].rearrange("(s p) f -> p s f", p=128), acst)
            P.st(qa, self.ACSF[:, t0:t0 + 512], acsf)
            c0 = i * 4
            P.st(qa, self.ACSE[c0:c0 + 4, 0:32].unsqueeze(0), acst[127:128, :, 0:32])
            P.st(qa, self.ACSE[c0:c0 + 4, 32:64].unsqueeze(0), acst[0:1, :, 32:64])
        P.barrier()
        A.reset(m0)

    def stage2a(self, L):
        P, A, NT = self.P, self.A, self.NT
        m0 = A.mark()
        q = "sp"
        cw = A.alloc([128, 120], F32)
        P.ld(q, cw, self.convw[L])
        cbc = A.alloc([128, 24], F32, "cbc")
        P.ld(q, cbc, self.convbc[L])
        cbr32 = A.alloc([1, 3072], F32)
        P.ld(q, cbr32, self.convbr[L])
        cbr = A.alloc([128, 2560], BF16, "cbr")
        P.memset("pool", cbr, 0.0)
        P.cp("dve", cbr[0:1, :], cbr32[0:1, 0:2560])
        diag = A.alloc([128, 5, 24, 128], BF16, "diag")
        for j in range(5):
            for c in range(24):
                P.ts("pool" if (j * 24 + c) % 2 else "dve", diag[:, j, c, :], self.identf,
                     cw[:, j * 24 + c:j * 24 + c + 1], None, ALU.mult)
        NB = 2
        uw = [A.alloc([128, 24, 132], BF16, f"uw{i}") for i in range(NB)]
        dtt = [A.alloc([128, 64], F32, f"dtt{i}") for i in range(NB)]
        acs = [A.alloc([128, 64], F32, f"acs{i}") for i in range(NB)]
        ace = [A.alloc([128, 64], F32, f"ace{i}") for i in range(NB)]
        xtk = [A.alloc([128, 2048], BF16, f"xtk{i}") for i in range(NB)]
        btk = [A.alloc([128, 512], BF16, f"btk{i}") for i in range(NB)]
        bfm = [A.alloc([128, 4, 128], BF16, f"bfm{i}") for i in range(NB)]
        cfm = [A.alloc([128, 4, 128], BF16, f"cfm{i}") for i in range(NB)]
        xdd = [A.alloc([128, 2048], BF16, f"xdd{i}") for i in range(NB)]
        pbb = [A.alloc([128, 2048], BF16, f"pbb{i}") for i in range(NB)]
        sm = [A.alloc([128, 32], F32, f"sm{i}") for i in range(4)]
        Pb = A.alloc([128, 2048], F32, "Pb")
        P.memset("pool", Pb, 0.0)

        def loads(c):
            k = c % NB
            P.ld(q, uw[k], self.Us[:, :, c * 128:c * 128 + 132].rearrange("k p t -> p k t"))
            P.ld(q, dtt[k], self.DTs[c * 128:(c + 1) * 128, :])
            P.ld(q, acs[k], self.ACSs[c * 128:(c + 1) * 128, :])
            P.ld(q, ace[k], self.ACSE[c:c + 1, :].broadcast_to([128, 64]))

        order = list(range(self.NCH - 1, -1, -1))
        loads(order[0])
        for n, c in enumerate(order):
            k = c % NB
            if n + 1 < len(order):
                loads(order[n + 1])
            u = uw[k]
            if c % 16 == 0:
                P.ts("dve", u[:, :, 0:2], u[:, :, 0:2], self.flagc[:, 0:1], None, ALU.mult)
            if c % 16 == 15:
                P.ts("dve", u[:, :, 130:132], u[:, :, 130:132], self.flagc[:, 0:1], None, ALU.mult)
            for cc in range(20):
                if cc % 4 == 0:
                    ps = self.nextps()
                o = ps[:, (cc % 4) * 128:(cc % 4) * 128 + 128]
                for j in range(5):
                    P.mm(o, u[:, cc, j:j + 128], diag[:, j, cc, :], start=(j == 0), stop=False)
                P.mm(o, self.onesb, cbr[:, cc * 128:(cc + 1) * 128], start=False, stop=True)
                if cc % 4 == 3:
                    if cc < 16:
                        P.act(xtk[k][:, (cc // 4) * 512:(cc // 4 + 1) * 512], ps, AF.Silu)
                    else:
                        P.act(btk[k], ps, AF.Silu)
            for which, dst in ((16, bfm[k]), (20, cfm[k])):
                ps = self.nextps()
                for g in range(4):
                    cc = which + g
                    o = ps[:, g * 128:(g + 1) * 128]
                    for j in range(5):
                        P.mm(o, diag[:, j, cc, :], u[:, cc, j:j + 128], start=(j == 0), stop=(j == 4))
                for g in range(4):
                    cc = which + g
                    P.act(dst[:, g, :], ps[:, g * 128:(g + 1) * 128], AF.Silu, bias=cbc[:, cc:cc + 1])
            P.st(q, self.XSs[c * 128:(c + 1) * 128, :], xtk[k])
            P.st(q, self.BTs[c * 128:(c + 1) * 128, :], btk[k])
            P.st(q, self.BFs[:, :, c * 128:(c + 1) * 128].rearrange("g p t -> p g t"), bfm[k])
            P.st(q, self.CFs[:, :, c * 128:(c + 1) * 128].rearrange("g p t -> p g t"), cfm[k])
            if c % 16 == 15:
                P.ts("dve", Pb, Pb, self.flagc[:, 0:1], None, ALU.mult)
            P.cp("act", pbb[k], Pb)
            P.st(q, self.PBs[c], pbb[k])
            d1, d2, d3 = sm[0], sm[1], sm[2]
            P.tt("dve", d1, ace[k][:, 32:64], acs[k][:, 32:64], ALU.subtract)
            P.act(d1, d1, AF.Exp)
            P.tt("dve", d2, d1, dtt[k][:, 32:64], ALU.mult)
            P.act(d3, ace[k][:, 32:64], AF.Exp)
            P.tt("dve", xdd[k].re("p (h e) -> p h e", e=64), xtk[k].re("p (h e) -> p h e", e=64),
                 d2.re("p (h o) -> p h o", o=1).bc([128, 32, 64]), ALU.mult)
            for g in range(4):
                ps = self.nextps()
                P.mm(ps, btk[k][:, g * 128:(g + 1) * 128], xdd[k][:, g * 512:(g + 1) * 512])
                pv = Pb[:, g * 512:(g + 1) * 512]
                P.tt("dve", pv.re("p (h e) -> p h e", e=64), pv.re("p (h e) -> p h e", e=64),
                     d3[:, g * 8:(g + 1) * 8].re("p (h o) -> p h o", o=1).bc([128, 8, 64]), ALU.mult)
                P.tt("dve", pv, pv, ps, ALU.add)
        P.barrier()
        A.reset(m0)

    def stage2b(self, L):
        P, A, NT = self.P, self.A, self.NT
        m0 = A.mark()
        q = "sp"
        dsb = A.alloc([128, 32], F32)
        P.ld(q, dsb, self.dsk[L].broadcast_to([128, 32]))
        dskd = A.alloc([128, 32, 128], BF16, "dskd")
        for h in range(32):
            P.ts("dve", dskd[:, h, :], self.identf, dsb[:, h:h + 1], None, ALU.mult)
        nwb = A.alloc([128, 2048], F32, "nwb")
        P.ld(q, nwb, self.snw[L].broadcast_to([128, 2048]))
        NB = 2
        xtk = [A.alloc([128, 2048], BF16, f"xtk{i}") for i in range(NB)]
        btk = [A.alloc([128, 512], BF16, f"btk{i}") for i in range(NB)]
        bfm = [A.alloc([128, 4, 128], BF16, f"bfm{i}") for i in range(NB)]
        cfm = [A.alloc([128, 4, 128], BF16, f"cfm{i}") for i in range(NB)]
        dtt = [A.alloc([128, 64], F32, f"dtt{i}") for i in range(NB)]
        acs = [A.alloc([128, 64], F32, f"acs{i}") for i in range(NB)]
        ace = [A.alloc([128, 64], F32, f"ace{i}") for i in range(NB)]
        pbb = [A.alloc([128, 2048], BF16, f"pbb{i}") for i in range(NB)]
        zt = [A.alloc([128, 2048], BF16, f"zt{i}") for i in range(NB)]
        lrow = [[A.alloc([128, 32, 128], F32, f"lrow{d}{i}") for i in range(NB)] for d in range(2)]
        Pf = [A.alloc([128, 512], F32, f"Pf{g}") for g in range(4)]
        pfb = [A.alloc([128, 512], BF16, f"pfb{g}") for g in range(4)]
        for g in range(4):
            P.memset("pool", Pf[g], 0.0)
            P.memset("pool", pfb[g], 0.0)
        cbm = [A.alloc([128, 4, 128], BF16 if EMBF else F32, f"cbm{d}") for d in range(2)]
        eacs = A.alloc([128, 64], F32, "eacs")
        yo = [A.alloc([128, 2048], BF16, f"yo{d}") for d in range(2)]
        y = A.alloc([128, 2048], F32, "y")
        sst = y
        xdd = [A.alloc([128, 512], BF16, f"xdd{g}") for g in range(4)]
        junk = A.alloc([128, 512], F32, "junk")
        nacs = A.alloc([128, 64], F32, "nacs")
        Em = [A.alloc([128, 128], BF16 if EMBF else F32, f"Em{i}") for i in range(6)]
        GT = [A.alloc([128, 128], BF16, f"GT{i}") for i in range(64)]
        sm = [A.alloc([128, 32], F32, f"sm{i}") for i in range(4)]
        ssq = A.alloc([128, 4], F32, "ssq")
        rs = A.alloc([128, 4], F32, "rs")
        sT = [A.alloc([128, 16, 128], BF16, f"sT{i}") for i in range(2)]

        def loads(c):
            k = c % NB
            sl = slice(c * 128, (c + 1) * 128)
            for d in range(2):
                P.ld(q, lrow[d][k], self.ACSF[d * 32:(d + 1) * 32, sl].unsqueeze(0).broadcast_to([128, 32, 128]))
            P.ld(q, bfm[k], self.BFs[:, :, sl].rearrange("g p t -> p g t"))
            P.ld(q, cfm[k], self.CFs[:, :, sl].rearrange("g p t -> p g t"))
            P.ld(q, dtt[k], self.DTs[sl, :])
            P.ld(q, acs[k], self.ACSs[sl, :])
            P.ld(q, xtk[k], self.XSs[sl, :])
            P.ld(q, btk[k], self.BTs[sl, :])
            P.ld(q, ace[k], self.ACSE[c:c + 1, :].broadcast_to([128, 64]))
            P.ld(q, pbb[k], self.PBs[c])
            P.ld(q, zt[k], self.Zs[sl, :])

        def partA(c):
            k = c % NB
            pcb = self.nextps()
            for g in range(4):
                P.mm(pcb[:, g * 128:(g + 1) * 128], bfm[k][:, g, :], cfm[k][:, g, :])
            pcb3 = pcb.re("p (g t) -> p g t", g=4)
            P.tt("dve", cbm[0], pcb3, self.MF.re("p (o t) -> p o t", o=1).bc([128, 4, 128]), ALU.mult)
            P.tt("dve", cbm[1], pcb3, self.MB.re("p (o t) -> p o t", o=1).bc([128, 4, 128]), ALU.mult)
            P.act(nacs, dtt[k], AF.Ln)
            P.tt("dve", nacs, nacs, acs[k], ALU.subtract)
            gi = 0
            for g in range(4):
                for hh in range(8):
                    h = g * 8 + hh
                    for d in range(2):
                        hd = d * 32 + h
                        em = Em[gi % 6]
                        gi += 1
                        P.act(em, lrow[d][k][:, h, :], AF.Exp, bias=nacs[:, hd:hd + 1])
                        P.stt("dve", GT[h * 2 + d], em, 1e30, cbm[d][:, g, :], ALU.min, ALU.mult)

        loads(0)
        partA(0)
        for c in range(self.NCH):
            k = c % NB
            if c + 1 < self.NCH:
                loads(c + 1)
            if c % 16 == 0 and c > 0:
                for g in range(4):
                    P.ts("dve", Pf[g], Pf[g], self.flagc[:, 0:1], None, ALU.mult)
                    P.cp("pool", pfb[g], Pf[g])
            P.act(eacs, acs[k], AF.Exp)
            for d in range(2):
                for g in range(4):
                    st_ = pfb[g] if d == 0 else pbb[k][:, g * 512:(g + 1) * 512]
                    ps = self.nextps()
                    P.mm(ps, cfm[k][:, g, :], st_)
                    P.tt("dve", yo[d][:, g * 512:(g + 1) * 512].re("p (h e) -> p h e", e=64),
                         ps.re("p (h e) -> p h e", e=64),
                         eacs[:, d * 32 + g * 8:d * 32 + g * 8 + 8].re("p (h o) -> p h o", o=1).bc([128, 8, 64]),
                         ALU.mult)
            for g in range(4):
                psy = self.nextps()
                P.mm(psy, self.identb, yo[0][:, g * 512:(g + 1) * 512], start=True, stop=False)
                P.mm(psy, self.identb, yo[1][:, g * 512:(g + 1) * 512], start=False, stop=False)
                for hh in range(8):
                    h = g * 8 + hh
                    o = psy[:, hh * 64:(hh + 1) * 64]
                    xh = xtk[k][:, h * 64:(h + 1) * 64]
                    P.mm(o, GT[h * 2], xh, start=False, stop=False)
                    P.mm(o, GT[h * 2 + 1], xh, start=False, stop=False)
                    P.mm(o, dskd[:, h, :], xh, start=False, stop=(hh == 7))
                P.tt("dve", y[:, g * 512:(g + 1) * 512], psy, zt[k][:, g * 512:(g + 1) * 512], ALU.mult)
            d1, d2, d3 = sm[0], sm[1], sm[2]
            P.tt("dve", d1, ace[k][:, 0:32], acs[k][:, 0:32], ALU.subtract)
            P.act(d1, d1, AF.Exp)
            P.tt("dve", d2, d1, dtt[k][:, 0:32], ALU.mult)
            P.act(d3, ace[k][:, 0:32], AF.Exp)
            for g in range(4):
                P.tt("pool", xdd[g].re("p (h e) -> p h e", e=64),
                     xtk[k][:, g * 512:(g + 1) * 512].re("p (h e) -> p h e", e=64),
                     d2[:, g * 8:(g + 1) * 8].re("p (h o) -> p h o", o=1).bc([128, 8, 64]), ALU.mult)
                ps = self.nextps()
                P.mm(ps, btk[k][:, g * 128:(g + 1) * 128], xdd[g])
                P.tt("pool", Pf[g].re("p (h e) -> p h e", e=64), Pf[g].re("p (h e) -> p h e", e=64),
                     d3[:, g * 8:(g + 1) * 8].re("p (h o) -> p h o", o=1).bc([128, 8, 64]), ALU.mult)
                P.tt("dve", Pf[g], Pf[g], ps, ALU.add)
                P.cp("pool", pfb[g], Pf[g])
            if c + 1 < self.NCH:
                partA(c + 1)
            P.memset("pool", ssq, 0.0)
            for g in range(4):
                P.act(junk, y[:, g * 512:(g + 1) * 512], AF.Square, accum=ssq[:, g:g + 1])
            P.act(rs, ssq, AF.Ln, bias=EPS, scale=1.0 / 512.0)
            P.act(rs, rs, AF.Exp, scale=-0.5)
            for g in range(4):
                P.stt("dve", sst[:, g * 512:(g + 1) * 512], y[:, g * 512:(g + 1) * 512],
                      rs[:, g:g + 1], nwb[:, g * 512:(g + 1) * 512], ALU.mult, ALU.mult)
            so = sT[c % 2]
            for cc in range(16):
                if cc % 4 == 0:
                    ps = self.nextps()
                P.tr(ps[:, (cc % 4) * 128:(cc % 4) * 128 + 128], sst[:, cc * 128:(cc + 1) * 128], self.identf)
                if cc % 4 == 3:
                    P.cp("act" if (cc // 4) % 2 else "dve", so[:, cc - 3:cc + 1, :], ps.re("p (k t) -> p k t", k=4))
            P.st(q, self.SSMs[:, :, c * 128:(c + 1) * 128].rearrange("k p t -> p k t"), so)
        P.barrier()
        A.reset(m0)

    def stage2c(self, L):
        P, A, NT, R = self.P, self.A, self.NT, self.R
        m0 = A.mark()
        q = "sp"
        rr = A.alloc([120, 62], F32)
        P.ld(q, rr, self.relrow[L][0].rearrange("(p f) -> p f", p=120))
        qk = A.alloc([1, 128], F32)
        P.ld(q, qk, self.qkrow[L])
        mx = A.alloc([1, 4], F32)
        rmx = A.alloc([120, 1], F32)
        rmt = A.alloc([1, 120], F32)

        def amax(o, i):
            oa, ia = o.ap, i.ap
            P.op("dve", lambda e: e.tensor_reduce(out=oa, in_=ia, axis=AX.X, op=ALU.max, apply_absolute_value=True), [i], [o])

        amax(rmx, rr)
        ptm = self.nextps()
        P.tr(ptm[0:1, 0:120], rmx, self.identf[0:120, 0:120])
        P.cp("dve", rmt, ptm[0:1, 0:120])
        amax(mx[:, 0:1], rmt)
        amax(mx[:, 1:2], qk[:, 0:64])
        amax(mx[:, 2:3], qk[:, 64:128])
        P.tt("dve", mx[:, 3:4], mx[:, 1:2], mx[:, 2:3], ALU.mult)
        P.stt("dve", mx[:, 3:4], mx[:, 3:4], -8.0, mx[:, 0:1], ALU.mult, ALU.subtract)
        pm = self.nextps()
        P.mm(pm[:, 0:1], self._ones_row(), mx[:, 3:4])
        negM = A.alloc([128, 1], F32, "negM")
        P.cp("dve", negM, pm[:, 0:1])
        tab = A.alloc([128, NH, NTAB * 64], BF16, "tab")
        m1 = A.mark()
        for h0 in range(0, NH, 2):
            t32 = A.alloc([128, 2 * NTAB * 64], F32)
            P.ld(q, t32, self.tabf[L][:, h0 * NTAB * 64:(h0 + 2) * NTAB * 64])
            P.cp("dve" if (h0 // 2) % 2 else "pool", tab[:, h0:h0 + 2, :].re("p h n -> p (h n)"), t32)
            if h0 % 4 == 2:
                P.barrier()
                A.reset(m1)
        A.reset(m1)
        NB = 2
        qz = [[A.alloc([128, 8, 256], BF16, f"qz{par}{i}") for i in range(NB)] for par in range(2)]
        kz = [[A.alloc([128, 8, 768], BF16, f"kz{par}{i}") for i in range(NB)] for par in range(2)]
        for par in range(2):
            a0 = 64 if par == 0 else 0
            for i in range(NB):
                P.memset("pool", qz[par][i], 0.0)
                P.memset("pool", kz[par][i], 0.0)
                P.cp("dve", kz[par][i][a0:a0 + 12, :, :],
                     self.ind[a0:a0 + 12, :].re("p (o t) -> p o t", o=1).bc([12, 8, 768]))
        va = [A.alloc([128, 6, NH * 65], BF16, f"va{i}") for i in range(NB)]
        PT = [A.alloc([128, 6, 256], BF16, f"PT{i}") for i in range(3)]
        atk = A.alloc([128, 2, D], F32, "atk")
        aT = [A.alloc([128, 8, 256], BF16, f"aT{i}") for i in range(2)]
        rc = [A.alloc([128, 2], F32, f"rc{i}") for i in range(2)]

        def sG(G):
            return min(max(4 * G - 4, 0), R - 12)

        def loads(G):
            k = G % NB
            t0 = G * 256
            k0 = sG(G) * 64
            for par in range(2):
                d0 = par * 64
                a0 = 64 if par == 0 else 0
                P.ld(q, qz[par][k][d0:d0 + 64, :, :], self.Qs[:, d0:d0 + 64, t0:t0 + 256].rearrange("c p t -> p c t"))
                P.ld(q, kz[par][k][d0:d0 + 64, :, :], self.Ks[:, d0:d0 + 64, k0:k0 + 768].rearrange("c p t -> p c t"))
                P.cp("dve", qz[par][k][a0:a0 + 12, :, :],
                     self.rowmb[a0:a0 + 12, G * 256:(G + 1) * 256].re("p (o t) -> p o t", o=1).bc([12, 8, 256]))
            P.ld(q, va[k], self.VAs[k0:k0 + 768, :].rearrange("(b p) f -> p b f", p=128))

        loads(0)
        hi = 0
        for G in range(self.NG):
            k = G % NB
            if G + 1 < self.NG:
                loads(G + 1)
            s_g = sG(G)
            def scores(h):
                ch = h // 2
                par = h % 2
                pt = PT[(G * NH + h) % 3]
                for kb in range(6):
                    if kb % 2 == 0:
                        psb = self.nextps()
                    o = psb[:, (kb % 2) * 256:(kb % 2) * 256 + 256]
                    P.mm(o, kz[par][k][:, ch, kb * 128:(kb + 1) * 128], qz[par][k][:, ch, :], start=True, stop=False)
                    b0 = 10 - (s_g + 2 * kb - 4 * G)
                    P.mm(o, self.identb, tab[:, h, b0 * 64:(b0 + 4) * 64], start=False, stop=True)
                    if kb % 2 == 1:
                        P.act(pt[:, kb - 1:kb + 1, :].re("p a b -> p (a b)"), psb, AF.Exp, bias=negM[:, 0:1])
                return pt

            def pv(h, pt):
                pso = self.nextps()
                for qh in range(2):
                    for kb in range(6):
                        P.mm(pso[:, qh * 65:qh * 65 + 65], pt[:, kb, qh * 128:(qh + 1) * 128],
                             va[k][:, kb, h * 65:(h + 1) * 65], start=(kb == 0), stop=(kb == 5))
                r_ = rc[h % 2]
                P.recip(r_, pso[:, 0:130].re("p (a b) -> p a b", b=65)[:, :, 64])
                for qh in range(2):
                    P.ts("dve", atk[:, qh, h * 64:(h + 1) * 64], pso[:, qh * 65:qh * 65 + 64], r_[:, qh:qh + 1], None, ALU.mult)

            prev = None
            for h in range(NH):
                pt = scores(h)
                if not PIPE:
                    pv(h, pt)
                    continue
                if prev is not None:
                    pv(*prev)
                prev = (h, pt)
            if PIPE:
                pv(*prev)
            ao = aT[G % 2]
            for qh in range(2):
                for cc in range(8):
                    if cc % 4 == 0:
                        ps = self.nextps()
                    P.tr(ps[:, (cc % 4) * 128:(cc % 4) * 128 + 128], atk[:, qh, cc * 128:(cc + 1) * 128], self.identf)
                    if cc % 4 == 3:
                        P.cp("act" if (cc // 4) % 2 else "dve", ao[:, cc - 3:cc + 1, qh * 128:(qh + 1) * 128],
                             ps.re("p (k t) -> p k t", k=4))
            P.st(q, self.ATTs[:, :, G * 256:(G + 1) * 256].rearrange("c p t -> p c t"), ao)
        P.barrier()
        A.reset(m0)

    def _ones_row(self):
        return self.MF[0:1, :]

    def stage3(self, L, last):
        P, A, NT = self.P, self.A, self.NT
        m0 = A.mark()
        qa = "pool"
        plan = []
        for i in range(self.NTL):
            plan += [self.wblk(L, B_ATT + b) for b in range(2)]
            plan += [self.wblk(L, B_SSM + b) for b in range(4)]
            plan += [self.wblk(L, B_OUT + b) for b in range(2)]
            plan += [self.wblk(L, B_UP2 + b) for b in range(11)]
            plan += [self.wblk(L, B_DN2 + b, 22 * 128) for b in range(8)]
        ws = self.WStream(self, plan)
        xT = A.alloc([128, 8, 512], F32, "xT")
        aT = A.alloc([128, 8, 512], BF16, "aTl")
        sT = A.alloc([128, 16, 512], BF16, "sTl")
        gt = A.alloc([128, 16, 512], BF16, "gt")
        t1 = A.alloc([128, 8, 512], F32, "t1")
        mg = A.alloc([128, 8, 512], BF16, "mg")
        h = A.alloc([128, 8, 512], BF16, "h")
        sq = A.alloc([128, 8, 512], BF16, "sq")
        actb = A.alloc([128, 22, 512], BF16, "actb")
        tmps = [A.alloc([128, 512], F32, f"tmp{i}") for i in range(4)]
        rst = A.alloc([128, 512], F32, "rst")
        xo = t1.re("p k t -> p (k t)").re("p (a d) -> p a d", a=4) if last else None
        g3 = self.lnpc[:, (L * 3 + 2) * 8:(L * 3 + 2) * 8 + 8]
        for i in range(self.NTL):
            t0 = i * 512
            P.ld(qa, aT, self.ATTs[:, :, t0:t0 + 512].rearrange("c p t -> p c t"))
            P.ld(qa, sT, self.SSMs[:, :, t0:t0 + 512].rearrange("c p t -> p c t"))
            P.ld(qa, gt, self.Gs[:, :, t0:t0 + 512].rearrange("c p t -> p c t"))
            P.ld(qa, xT, self.X1[i].rearrange("p (k t) -> p k t", k=8))
            for m in range(8):
                if m % 4 == 0:
                    blk = ws.next().re("p (k n) -> p k n", k=8)
                pa = self.nextps()
                for kc in range(8):
                    P.mm(pa, blk[:, kc, (m % 4) * 128:(m % 4) * 128 + 128], aT[:, kc, :], start=(kc == 0), stop=(kc == 7))
                P.tt("dve", t1[:, m, :], pa, gt[:, m, :], ALU.mult)
            for m in range(8):
                if m % 2 == 0:
                    blk = ws.next().re("p (k n) -> p k n", k=16)
                ps = self.nextps()
                for kc in range(16):
                    P.mm(ps, blk[:, kc, (m % 2) * 128:(m % 2) * 128 + 128], sT[:, kc, :], start=(kc == 0), stop=(kc == 15))
                t = tmps[m % 4]
                P.tt("dve", t, ps, gt[:, 8 + m, :], ALU.mult)
                P.tt("pool", mg[:, m, :], t, t1[:, m, :], ALU.add)
            for m in range(8):
                if m % 4 == 0:
                    blk = ws.next().re("p (k n) -> p k n", k=8)
                po = self.nextps()
                for kc in range(8):
                    P.mm(po, blk[:, kc, (m % 4) * 128:(m % 4) * 128 + 128], mg[:, kc, :], start=(kc == 0), stop=(kc == 7))
                P.tt("dve", xT[:, m, :], po, xT[:, m, :], ALU.add)
            self.rmsnorm(xT, g3, h, sq, rst)
            self.ffn(ws, xT, h, actb, tmps)
            if not last:
                P.st(qa, self.XL[i].rearrange("p (k t) -> p k t", k=8), xT)
            else:
                for sub in range(4):
                    for kc in range(8):
                        if kc % 4 == 0:
                            ps = self.nextps()
                        P.tr(ps[:, (kc % 4) * 128:(kc % 4) * 128 + 128], xT[:, kc, sub * 128:(sub + 1) * 128], self.identf)
                        if kc % 4 == 3:
                            P.cp("act" if (kc // 4) % 2 else "dve", xo[:, sub, (kc - 3) * 128:(kc + 1) * 128], ps)
                ev = P.st(qa, self.y_out[t0:t0 + 512, :].rearrange("(s p) d -> p s d", p=128), xo)
                self.final.append(ev)
        P.barrier()
        A.reset(m0)

    def build(self):
        self.persistent()
        self.cast_weights()
        for L in range(self.nlayers):
            last = (L == self.nlayers - 1)
            self.stage1(L)
            self.stage2a(L)
            self.stage2b(L)
            self.stage2c(L)
            self.stage3(L, last)
        self.P.emit(self.final)
        return self.nc


def _tables(inp):
    f = np.float32
    t = {}
    lnp = np.zeros((128, 48), f)
    for l in range(2):
        for w, nm in enumerate(("ln_ffn1", "ln_mix", "ln_ffn2")):
            lnp[:, (l * 3 + w) * 8:(l * 3 + w) * 8 + 8] = np.asarray(inp[nm][l], f).reshape(8, 128).T
    t["lnp"] = lnp
    qkn = np.zeros((128, 4), f)
    for l in range(2):
        qkn[:, l * 2] = np.tile(np.asarray(inp["q_norm"][l], f), 2)
        qkn[:, l * 2 + 1] = np.tile(np.asarray(inp["k_norm"][l], f), 2)
    t["qkn"] = qkn
    t["qkrow"] = np.stack([np.concatenate([inp["q_norm"][l], inp["k_norm"][l]]) for l in range(2)]).astype(f).reshape(2, 1, 128)
    rel = np.asarray(inp["rel_bias"], f)
    t["relrow"] = rel.reshape(2, 1, -1)
    cols = np.arange(64)
    cstart = np.clip(cols - 8, 0, 48)
    cvalid = (cols[None, :] >= cstart[:, None]) & (cols[None, :] < cstart[:, None] + 16)
    cidx = np.clip(cols[None, :] - cols[:, None] + 15, 0, 30)
    tab = np.zeros((2, 128, NH, NTAB, 64), f)
    for kl in range(2):
        for b in range(NTAB):
            delta = 17 - b + kl
            if 0 <= delta <= 14:
                blk = rel[:, :, delta, :][:, :, cidx]
                blk = np.where(cvalid[None, None], blk, f(NEG))
                tab[:, kl * 64:(kl + 1) * 64, :, b, :] = blk.transpose(0, 3, 1, 2)
    t["tabf"] = tab.reshape(2, 128, NH * NTAB * 64)
    cw = np.asarray(inp["conv_w"], f)
    t["convw"] = cw.reshape(2, 5, 24, 128).transpose(0, 3, 1, 2).reshape(2, 128, 120).copy()
    cb = np.asarray(inp["conv_b"], f)
    t["convbc"] = cb.reshape(2, 24, 128).transpose(0, 2, 1).copy()
    t["convbr"] = cb.reshape(2, 1, 3072)
    t["dtb"] = np.concatenate([inp["dt_bias_fwd"], inp["dt_bias_bwd"]], axis=1).astype(f).reshape(2, 1, 64)
    t["alog"] = np.concatenate([inp["a_log_fwd"], inp["a_log_bwd"]], axis=1).astype(f).reshape(2, 1, 64)
    t["dsk"] = np.asarray(inp["d_skip"], f).reshape(2, 1, 32)
    t["snw"] = np.asarray(inp["ssm_norm"], f).reshape(2, 1, 2048)
    cst = np.zeros((128, 640), f)
    cst[:, 0:128] = np.eye(128)
    cst[:, 128:256] = np.triu(np.ones((128, 128)))
    cst[:, 256:384] = np.tril(np.ones((128, 128)))
    cst[0:64, 384:448] = 1.0
    cst[64:128, 448:512] = 1.0
    cst[:, 512:640] = 1.0
    t["cst"] = cst
    ind = np.zeros((12, 768), f)
    for c in range(12):
        ind[c, c * 64:(c + 1) * 64] = 1.0
    t["indm"] = ind
    for nm in ("w_ffn1_up", "w_ffn1_down", "w_in", "w_attn_proj", "w_ssm_proj", "w_out", "w_ffn2_up", "w_ffn2_down"):
        t[nm] = np.ascontiguousarray(inp[nm], dtype=f)
    return t


def _rowmask(NT, SR):
    R, NG = NT // 64, NT // 256
    rm = np.zeros((12, NG, 4, 64), np.float32)
    for G in range(NG):
        s = min(max(4 * G - 4, 0), R - 12)
        for c in range(12):
            kap = s + c
            for rl in range(4):
                rho = 4 * G + rl
                r = rho % SR
                r0 = min(max(r - 4, 0), SR - 8)
                ok = (kap // SR == rho // SR) and (r0 <= kap % SR <= r0 + 7)
                rm[c, G, rl, :] = 0.0 if ok else NEG
    return rm.reshape(12, NG * 256)


def run_cores(inp, xs, seq_rows, NT, nlayers=2):
    bld = Builder(NT, nlayers=nlayers)
    nc = bld.build()
    t = _tables(inp)
    in_maps = []
    for x, SR in zip(xs, seq_rows):
        m = dict(t)
        m["x"] = np.ascontiguousarray(x, dtype=np.float32)
        m["rowm"] = _rowmask(NT, SR)
        m["flag"] = np.full((128, 1), 1.0 if SR * 64 == NT else 0.0, np.float32)
        in_maps.append(m)
    res = run_bass_kernel_spmd(nc, in_maps, core_ids=list(range(len(xs))))
    return [r["y"] for r in res.results]


def kernel(**inputs):
    inp = {k: np.asarray(v) for k, v in inputs.items()}
    xp = inp["x_prompt"]
    xsm = inp["x_sample"]
    xs = [xp[i] for i in range(4)] + [xsm[4 * j:4 * j + 4].reshape(8192, 1024) for j in range(4)]
    ys = run_cores(inp, xs, [128] * 4 + [32] * 4, 8192)
    y_prompt = np.stack(ys[0:4]).astype(np.float32)
    y_sample = np.concatenate([ys[4 + j].reshape(4, 2048, 1024) for j in range(4)], axis=0).astype(np.float32)
    return (y_prompt, y_sample)
```

```python
import numpy as np
import ml_dtypes
import concourse.bass as bass
import concourse.mybir as mybir
from concourse.bass_utils import run_bass_kernel_spmd

F32 = mybir.dt.float32
BF16 = mybir.dt.bfloat16
AF = mybir.ActivationFunctionType
ALU = mybir.AluOpType
AX = mybir.AxisListType


class Trk:
    __slots__ = ("name", "lw", "rd", "dsem", "dgen")

    def __init__(self, name):
        self.name = name
        self.lw = None
        self.rd = {}
        self.dsem = None
        self.dgen = -1


class V:
    __slots__ = ("t", "ap")

    def __init__(self, t, ap):
        self.t = t
        self.ap = ap

    def __getitem__(self, idx):
        return V(self.t, self.ap[idx])

    def re(self, s, **kw):
        return V(self.t, self.ap.rearrange(s, **kw))

    def bc(self, shape):
        return V(self.t, self.ap.broadcast_to(shape))


class DSem:
    __slots__ = ("idx", "count", "last")

    def __init__(self, idx):
        self.idx = idx
        self.count = 0
        self.last = None


class Prog:
    STREAMS = ("sp", "act", "dve", "pool", "pe")

    def __init__(self, nc):
        self.nc = nc
        self.ops = {s: [] for s in self.STREAMS}
        self.dsems = []
        self.free_ds = []
        self.gen = 0
        self.rr = 0
        self.nbuf = 0

    def sb(self, shape, dt, name=None):
        self.nbuf += 1
        name = name or f"sb{self.nbuf}"
        h = self.nc.alloc_sbuf_tensor(name, list(shape), dt)
        return V(Trk(name), h.ap() if hasattr(h, "ap") else h[:])

    def ps(self, name):
        h = self.nc.alloc_psum_tensor(name, [128, 512], F32)
        return V(Trk(name), h.ap() if hasattr(h, "ap") else h[:])

    def dram(self, name, shape, dt, kind="Internal"):
        h = self.nc.dram_tensor(name, list(shape), dt, kind=kind)
        return h.ap()

    def _deps(self, stream, reads, writes):
        deps = set()
        for r in reads:
            if r.lw is not None:
                deps.add(r.lw)
        for w in writes:
            if w.lw is not None:
                deps.add(w.lw)
            for ev in w.rd.values():
                deps.add(ev)
        return deps

    def _commit(self, ev, key, reads, writes):
        for w in writes:
            w.lw = ev
            w.rd = {}
        for r in reads:
            if r not in writes:
                r.rd[key] = ev

    def op(self, stream, fn, reads, writes):
        reads = [x.t if isinstance(x, V) else x for x in reads if x is not None]
        writes = [x.t if isinstance(x, V) else x for x in writes if x is not None]
        deps = self._deps(stream, reads, writes)
        lst = self.ops[stream]
        ev = ("E", stream, len(lst))
        if stream == "pe":
            deps = {d for d in deps if not (d[0] == "E" and d[1] == "pe")}
        else:
            raw = {r.lw for r in reads if r.lw is not None}
            deps = {d for d in deps if not (d[0] == "E" and d[1] == stream and d not in raw)}
        lst.append(["c", fn, deps, ev, False])
        self._commit(ev, stream, reads, writes)

    def dma(self, stream, out, in_, reads, writes, sem_owner=None):
        reads = [x.t if isinstance(x, V) else x for x in reads if x is not None]
        writes = [x.t if isinstance(x, V) else x for x in writes if x is not None]
        owner = sem_owner if sem_owner is not None else (writes[0] if writes else reads[0])
        if isinstance(owner, V):
            owner = owner.t
        if owner.dsem is None or owner.dgen != self.gen:
            if self.free_ds:
                owner.dsem = self.free_ds.pop()
            elif len(self.dsems) < 80:
                owner.dsem = DSem(len(self.dsems))
                self.dsems.append(owner.dsem)
            else:
                owner.dsem = self.dsems[self.rr % len(self.dsems)]
                self.rr += 1
            owner.dgen = self.gen
        ds = owner.dsem
        deps = self._deps(stream, reads, writes)
        if ds.last is not None:
            deps.add(ds.last)
        ds.count += 16
        ev = ("D", ds.idx, ds.count)
        ds.last = ev
        self.ops[stream].append(["d", (out, in_), deps, ev, True])
        self._commit(ev, ("D", ds.idx), reads, writes)
        return ev

    def emit(self, final_events):
        nc = self.nc
        for s in self.STREAMS:
            for o in self.ops[s]:
                for d in o[2]:
                    if d[0] == "E":
                        self.ops[d[1]][d[2]][4] = True
        for d in final_events:
            if d[0] == "E":
                self.ops[d[1]][d[2]][4] = True
        cnt = {}
        for s in self.STREAMS:
            c = 0
            arr = []
            for o in self.ops[s]:
                if o[0] == "c" and o[4]:
                    c += 1
                arr.append(c)
            cnt[s] = arr
        esem = {s: nc.alloc_semaphore(f"es_{s}") for s in self.STREAMS}
        dsem = [nc.alloc_semaphore(f"ds_{i}") for i in range(len(self.dsems))]

        def resolve(d):
            if d[0] == "E":
                return ("E", d[1]), esem[d[1]], cnt[d[1]][d[2]]
            return ("D", d[1]), dsem[d[1]], d[2]

        engmap = {"sp": "sync", "act": "scalar", "dve": "vector", "pool": "gpsimd", "pe": "tensor"}

        def run_stream(s, eng, extra_final=None):
            waited = {}
            def do_waits(deps):
                need = {}
                for d in deps:
                    k, sem, val = resolve(d)
                    if val > need.get(k, (None, 0))[1]:
                        need[k] = (sem, val)
                for k, (sem, val) in need.items():
                    if waited.get(k, 0) >= val:
                        continue
                    eng.wait_ge(sem, val)
                    waited[k] = val
            for o in self.ops[s]:
                do_waits(o[2])
                if o[0] == "c":
                    ins = o[1](eng)
                    if o[4]:
                        ins.then_inc(esem[s], 1)
                else:
                    out, in_ = o[1]
                    eng.dma_start(out=out, in_=in_).then_inc(dsem[o[3][1]], 16)
            if extra_final:
                do_waits(extra_final)

        with nc.Block() as block:
            @block.sync
            def _(e):
                run_stream("sp", e, final_events)

            @block.scalar
            def _(e):
                run_stream("act", e)

            @block.vector
            def _(e):
                run_stream("dve", e)

            @block.gpsimd
            def _(e):
                run_stream("pool", e)

            @block.tensor
            def _(e):
                run_stream("pe", e)

    def barrier(self):
        evs = set()
        for s in self.STREAMS:
            if self.ops[s]:
                evs.add(self.ops[s][-1][3])
        for ds in self.dsems:
            if ds.last is not None:
                evs.add(ds.last)
        for s in self.STREAMS:
            deps = {d for d in evs if not (d[0] == "E" and d[1] == s)}
            lst = self.ops[s]
            lst.append(["c", (lambda e: e.nop()), deps, ("E", s, len(lst)), False])
        self.gen += 1
        self.free_ds = list(self.dsems)

    @staticmethod
    def _a(x):
        return x.ap if isinstance(x, V) else x

    def _eng(self, e, name):
        return e

    def mm(self, out, lhsT, rhs, start=True, stop=True):
        o, l, r = out.ap, lhsT.ap, rhs.ap
        self.op("pe", lambda e: e.matmul(o, lhsT=l, rhs=r, start=start, stop=stop), [lhsT, rhs], [out])

    def tr(self, out, in_, ident):
        o, i, d = out.ap, in_.ap, ident.ap
        self.op("pe", lambda e: e.transpose(o, i, d), [in_, ident], [out])

    def act(self, out, in_, func, bias=0.0, scale=1.0, accum=None):
        o, i, b, s = out.ap, in_.ap, self._a(bias), self._a(scale)
        ac = accum.ap if accum is not None else None
        rd = [in_] + [x for x in (bias, scale) if isinstance(x, V)]
        wr = [out] + ([accum] if accum is not None else [])
        if ac is None:
            self.op("act", lambda e: e.activation(out=o, in_=i, func=func, bias=b, scale=s), rd, wr)
        else:
            self.op("act", lambda e: e.activation(out=o, in_=i, func=func, bias=b, scale=s, accum_out=ac), rd, wr)

    def tt(self, eng, out, in0, in1, op):
        o, a, b = out.ap, in0.ap, in1.ap
        self.op(eng, lambda e: e.tensor_tensor(out=o, in0=a, in1=b, op=op), [in0, in1], [out])

    def ts(self, eng, out, in0, s1, s2, op0, op1=None):
        o, a, x1, x2 = out.ap, in0.ap, self._a(s1), self._a(s2)
        rd = [in0] + [x for x in (s1, s2) if isinstance(x, V)]
        if op1 is None:
            self.op(eng, lambda e: e.tensor_scalar(out=o, in0=a, scalar1=x1, scalar2=None, op0=op0), rd, [out])
        else:
            self.op(eng, lambda e: e.tensor_scalar(out=o, in0=a, scalar1=x1, scalar2=x2, op0=op0, op1=op1), rd, [out])

    def stt(self, eng, out, in0, scalar, in1, op0, op1):
        o, a, sc, b = out.ap, in0.ap, self._a(scalar), in1.ap
        rd = [in0, in1] + ([scalar] if isinstance(scalar, V) else [])
        self.op(eng, lambda e: e.scalar_tensor_tensor(out=o, in0=a, scalar=sc, in1=b, op0=op0, op1=op1), rd, [out])

    def cp(self, eng, out, in_):
        o, i = out.ap, in_.ap
        if eng == "act":
            self.op("act", lambda e: e.activation(out=o, in_=i, func=AF.Copy), [in_], [out])
        else:
            self.op(eng, lambda e: e.tensor_copy(out=o, in_=i), [in_], [out])

    def recip(self, out, in_):
        o, i = out.ap, in_.ap
        self.op("dve", lambda e: e.reciprocal(out=o, in_=i), [in_], [out])

    def memset(self, eng, out, val):
        o = out.ap
        self.op(eng, lambda e: e.memset(o, val), [], [out])

    def ld(self, q, dst, src_ap):
        return self.dma(q, dst.ap, src_ap, [], [dst])

    def st(self, q, dst_ap, src):
        return self.dma(q, dst_ap, src.ap, [src], [], sem_owner=src)


class Arena:
    def __init__(self, nc, nbytes):
        self.h = nc.alloc_sbuf_tensor("arena", [128, nbytes // 2], BF16)
        self.ap = self.h.ap()
        self.nbytes = nbytes
        self.off = 0
        self.n = 0

    def alloc(self, shape, dt, name=None):
        esz = mybir.dt.size(dt)
        n = int(np.prod(shape[1:]))
        nb = (n * esz + 31) // 32 * 32
        assert self.off + nb <= self.nbytes, f"arena overflow {self.off}+{nb} > {self.nbytes}"
        a = self.ap[0:shape[0], self.off // 2:(self.off + n * esz) // 2]
        if dt != BF16:
            a = a.bitcast(dt)
        if len(shape) > 2:
            names = " ".join(f"d{i}" for i in range(len(shape) - 1))
            kw = {f"d{i}": shape[i + 1] for i in range(len(shape) - 2)}
            a = a.rearrange(f"p ({names}) -> p {names}", **kw)
        self.off += nb
        self.n += 1
        return V(Trk(name or f"a{self.n}"), a)

    def mark(self):
        return self.off

    def reset(self, m):
        self.off = m


D = 1024
DFF = 2816
NH = 16
SH = 32
INC = 10304
EPS = 1e-6
NEG = -30000.0
NBLK = 67
B_UP1, B_DN1, B_WIN, B_ATT, B_SSM, B_OUT, B_UP2, B_DN2 = 0, 11, 19, 40, 42, 46, 48, 59
NTAB = 22
LN8 = -2.0794415416798357
import os
PIPE = os.environ.get("K_PIPE", "1") == "1"
EMBF = os.environ.get("K_EMBF", "1") == "1"


class Builder:
    def __init__(self, NT, nlayers=2, dbg=None, stop_after=None):
        self.NT = NT
        self.NTL = NT // 512
        self.NCH = NT // 128
        self.NG = NT // 256
        self.R = NT // 64
        self.nlayers = nlayers
        self.dbg = dbg or []
        self.stop_after = stop_after
        nc = self.nc = bass.Bass("TRN2", target_bir_lowering=False)
        P = self.P = Prog(nc)
        dr = P.dram
        EI = "ExternalInput"
        self.x_in = dr("x", [NT, D], F32, EI)
        self.w_up = [dr("w_ffn1_up", [2, D, 2 * DFF], F32, EI), dr("w_ffn2_up", [2, D, 2 * DFF], F32, EI)]
        self.w_dn = [dr("w_ffn1_down", [2, DFF, D], F32, EI), dr("w_ffn2_down", [2, DFF, D], F32, EI)]
        self.w_in = dr("w_in", [2, D, INC], F32, EI)
        self.w_ap = dr("w_attn_proj", [2, D, D], F32, EI)
        self.w_sp = dr("w_ssm_proj", [2, 2 * D, D], F32, EI)
        self.w_o = dr("w_out", [2, D, D], F32, EI)
        self.lnp = dr("lnp", [128, 2 * 3 * 8], F32, EI)
        self.qkn = dr("qkn", [128, 4], F32, EI)
        self.tabf = dr("tabf", [2, 128, NH * NTAB * 64], F32, EI)
        self.relrow = dr("relrow", [2, 1, NH * 15 * 31], F32, EI)
        self.qkrow = dr("qkrow", [2, 1, 128], F32, EI)
        self.convw = dr("convw", [2, 128, 120], F32, EI)
        self.convbc = dr("convbc", [2, 128, 24], F32, EI)
        self.convbr = dr("convbr", [2, 1, 3072], F32, EI)
        self.dtb = dr("dtb", [2, 1, 64], F32, EI)
        self.alog = dr("alog", [2, 1, 64], F32, EI)
        self.dsk = dr("dsk", [2, 1, 32], F32, EI)
        self.snw = dr("snw", [2, 1, 2048], F32, EI)
        self.cst = dr("cst", [128, 5 * 128], F32, EI)
        self.indm = dr("indm", [12, 768], F32, EI)
        self.rowm = dr("rowm", [12, self.NG * 256], F32, EI)
        self.flag = dr("flag", [128, 1], F32, EI)
        self.y_out = dr("y", [NT, D], F32, "ExternalOutput")
        kind = "Internal"
        self.Wb = dr("Wb", [2 * NBLK, 128, 4096], BF16, kind)
        self.X1 = dr("X1", [self.NTL, 128, 8 * 512], F32, kind)
        self.XL = dr("XL", [self.NTL, 128, 8 * 512], F32, kind)
        self.Qs = dr("Qs", [8, 128, NT], BF16, kind)
        self.Ks = dr("Ks", [8, 128, NT], BF16, kind)
        self.VAs = dr("VAs", [NT, 1040], BF16, kind)
        self.Zs = dr("Zs", [NT, 2048], BF16, kind)
        self.Us = dr("Us", [24, 128, NT + 4], BF16, kind)
        self.Gs = dr("Gs", [16, 128, NT], BF16, kind)
        self.DTs = dr("DTs", [NT, 64], F32, kind)
        self.ACSs = dr("ACSs", [NT, 64], F32, kind)
        self.ACSF = dr("ACSF", [64, NT], F32, kind)
        self.ACSE = dr("ACSE", [self.NCH, 64], F32, kind)
        self.XSs = dr("XSs", [NT, 2048], BF16, kind)
        self.BTs = dr("BTs", [NT, 512], BF16, kind)
        self.BFs = dr("BFs", [4, 128, NT], BF16, kind)
        self.CFs = dr("CFs", [4, 128, NT], BF16, kind)
        self.PBs = dr("PBs", [self.NCH, 128, 2048], BF16, kind)
        self.SSMs = dr("SSMs", [16, 128, NT], BF16, kind)
        self.ATTs = dr("ATTs", [8, 128, NT], BF16, kind)
        self.dbg_out = {}
        self.A = Arena(nc, 209920)
        self.psb = [P.ps(f"psb{i}") for i in range(8)]
        self.psi = 0
        self.final = []

    def nextps(self):
        p = self.psb[self.psi % 8]
        self.psi += 1
        return p

    def persistent(self):
        P, A = self.P, self.A
        c32 = A.alloc([128, 640], F32, "c32")
        P.ld("sp", c32, self.cst)
        self.identf = c32[:, 0:128]
        self.MF = c32[:, 128:256]
        self.MB = c32[:, 256:384]
        cb = A.alloc([128, 640], BF16, "cbf")
        P.cp("dve", cb, c32)
        self.identb = cb[:, 0:128]
        self.blkones = cb[:, 384:512]
        self.onesb = cb[:, 512:640]
        ind32 = A.alloc([128, 768], F32)
        self.ind = A.alloc([128, 768], BF16, "ind")
        for p0 in (0, 64):
            P.ld("sp", ind32[p0:p0 + 12, :], self.indm)
            P.cp("dve", self.ind[p0:p0 + 12, :], ind32[p0:p0 + 12, :])
        self.lnpc = A.alloc([128, 48], F32, "lnp")
        P.ld("sp", self.lnpc, self.lnp)
        self.qknc = A.alloc([128, 4], F32, "qkn")
        P.ld("sp", self.qknc, self.qkn)
        self.flagc = A.alloc([128, 1], F32, "flag")
        P.ld("sp", self.flagc, self.flag)
        self.rowmb = A.alloc([128, self.NG * 256], BF16, "rowm")
        m = A.mark()
        for g0 in range(0, self.NG, 8):
            n = min(8, self.NG - g0)
            t = A.alloc([128, n * 256], F32)
            for p0 in (0, 64):
                P.ld("sp", t[p0:p0 + 12, :], self.rowm[:, g0 * 256:(g0 + n) * 256])
                P.cp("dve", self.rowmb[p0:p0 + 12, g0 * 256:(g0 + n) * 256], t[p0:p0 + 12, :])
        z = A.alloc([128, 24, 2], BF16)
        P.memset("pool", z, 0.0)
        P.st("sp", self.Us[:, :, 0:2].rearrange("k p t -> p k t"), z)
        P.st("sp", self.Us[:, :, self.NT + 2:self.NT + 4].rearrange("k p t -> p k t"), z)
        P.barrier()
        A.reset(m)
        self.base = A.mark()

    def cast_weights(self):
        P, A = self.P, self.A
        m = A.mark()
        NBW = 4
        st32 = [A.alloc([128, 4096], F32) for _ in range(NBW)]
        stb = [A.alloc([128, 4096], BF16) for _ in range(NBW)]
        engs = ["dve", "act", "pool"]
        n = 0
        for L in range(self.nlayers):
            jobs = []
            for f, (bu, bd) in enumerate(((B_UP1, B_DN1), (B_UP2, B_DN2))):
                wu = self.w_up[f][L].rearrange("(kc p) n -> p kc n", p=128)
                wd = self.w_dn[f][L].rearrange("(kc p) n -> p kc n", p=128)
                for b in range(11):
                    jobs.append((bu + b, 8, 512, [(0, 256, wu[:, :, b * 256:(b + 1) * 256]),
                                                  (256, 512, wu[:, :, DFF + b * 256:DFF + (b + 1) * 256])]))
                for mm_ in range(8):
                    jobs.append((bd + mm_, 22, 128, [(0, 128, wd[:, :, mm_ * 128:(mm_ + 1) * 128])]))
            wi = self.w_in[L].rearrange("(kc p) n -> p kc n", p=128)
            for b in range(16):
                jobs.append((B_WIN + b, 8, 512, [(0, 512, wi[:, :, b * 512:(b + 1) * 512])]))
            for b in range(4):
                jobs.append((B_WIN + 16 + b, 8, 512, [(0, 512, wi[:, :, 8256 + b * 512:8256 + (b + 1) * 512])]))
            jobs.append((B_WIN + 20, 8, 64, [(0, 64, wi[:, :, 8192:8256])]))
            wa = self.w_ap[L].rearrange("(kc p) n -> p kc n", p=128)
            ws_ = self.w_sp[L].rearrange("(kc p) n -> p kc n", p=128)
            wo = self.w_o[L].rearrange("(kc p) n -> p kc n", p=128)
            for b in range(2):
                jobs.append((B_ATT + b, 8, 512, [(0, 512, wa[:, :, b * 512:(b + 1) * 512])]))
            for b in range(4):
                jobs.append((B_SSM + b, 16, 256, [(0, 256, ws_[:, :, b * 256:(b + 1) * 256])]))
            for b in range(2):
                jobs.append((B_OUT + b, 8, 512, [(0, 512, wo[:, :, b * 512:(b + 1) * 512])]))
            for (bi, kc, nb, parts) in jobs:
                s32 = st32[n % NBW]
                sb_ = stb[n % NBW]
                v32 = s32[:, 0:kc * nb].re("p (k n) -> p k n", k=kc)
                for (c0, c1, src) in parts:
                    P.dma("sp" if n % 2 == 0 else "act", v32.ap[:, :, c0:c1], src, [], [s32])
                P.cp(engs[n % 3], sb_[:, 0:kc * nb], s32[:, 0:kc * nb])
                P.dma("pool", self.Wb[L * NBLK + bi][:, 0:kc * nb], sb_.ap[:, 0:kc * nb], [sb_], [], sem_owner=sb_)
                n += 1
        P.barrier()
        A.reset(m)

    class WStream:
        def __init__(self, bld, plan, R=4, q="sp"):
            self.b = bld
            self.plan = plan
            self.R = R
            self.ahead = R - 1
            self.q = q
            self.slots = [bld.A.alloc([128, 4096], BF16, f"wslot{i}") for i in range(R)]
            self.issued = 0
            self.cur = 0

        def next(self):
            i = self.cur
            lim = min(len(self.plan), i + self.ahead + 1)
            while self.issued < lim:
                j = self.issued
                ap, n = self.plan[j]
                s = self.slots[j % self.R]
                self.b.P.dma(self.q, s.ap[:, 0:n], ap[:, 0:n], [], [s])
                self.issued += 1
            self.cur += 1
            return self.slots[i % self.R]

    def wblk(self, L, bi, n=4096):
        return (self.Wb[L * NBLK + bi], n)

    def rmsnorm(self, xT, gcol, h, sq, tmp):
        P = self.P
        P.act(sq, xT, AF.Square)
        ps = self.nextps()
        for kc in range(8):
            P.mm(ps, self.onesb, sq[:, kc, :], start=(kc == 0), stop=(kc == 7))
        P.act(tmp, ps, AF.Ln, bias=EPS, scale=1.0 / D)
        P.act(tmp, tmp, AF.Exp, scale=-0.5)
        for kc in range(8):
            P.stt("dve", h[:, kc, :], xT[:, kc, :], gcol[:, kc:kc + 1], tmp, ALU.mult, ALU.mult)

    def ffn(self, ws, xT, h, actb, tmps):
        P = self.P
        for j in range(22):
            if j % 2 == 0:
                blk = ws.next()[:, 0:4096].re("p (k n) -> p k n", k=8)
            ca = (j % 2) * 128
            pa, pb = self.nextps(), self.nextps()
            for kc in range(8):
                P.mm(pa, blk[:, kc, ca:ca + 128], h[:, kc, :], start=(kc == 0), stop=(kc == 7))
            for kc in range(8):
                P.mm(pb, blk[:, kc, 256 + ca:256 + ca + 128], h[:, kc, :], start=(kc == 0), stop=(kc == 7))
            t = tmps[j % len(tmps)]
            P.act(t, pa, AF.Silu)
            P.tt("dve", actb[:, j, :], t, pb, ALU.mult)
        for m in range(8):
            blk = ws.next()[:, 0:22 * 128].re("p (k n) -> p k n", k=22)
            po = self.nextps()
            for j in range(22):
                P.mm(po, blk[:, j, :], actb[:, j, :], start=(j == 0), stop=(j == 21))
            P.stt("dve", xT[:, m, :], po, 0.5, xT[:, m, :], ALU.mult, ALU.add)

    def stage1(self, L):
        P, A, NT = self.P, self.A, self.NT
        m0 = A.mark()
        qa = "pool"
        plan = []
        for i in range(self.NTL):
            plan += [self.wblk(L, B_UP1 + b) for b in range(11)]
            plan += [self.wblk(L, B_DN1 + b, 22 * 128) for b in range(8)]
            plan += [self.wblk(L, B_WIN + b) for b in range(20)]
            plan += [self.wblk(L, B_WIN + 20, 512)]
        ws = self.WStream(self, plan)
        xTs = [A.alloc([128, 8, 512], F32, f"xT{i}") for i in range(2)]
        xtok = A.alloc([128, 4, 1024], F32, "xtok") if L == 0 else None
        h1s = [A.alloc([128, 8, 512], BF16, f"h1{i}") for i in range(2)]
        h = A.alloc([128, 8, 512], BF16, "h")
        sq = A.alloc([128, 8, 512], BF16, "sq")
        actb = A.alloc([128, 22, 512], BF16, "actb")
        tmps = [A.alloc([128, 512], F32, f"tmp{i}") for i in range(4)]
        rst = A.alloc([128, 512], F32, "rst")
        sqt = [A.alloc([128, 512], BF16, f"sqt{i}") for i in range(2)]
        stg = [A.alloc([128, 4, 512], BF16, f"stg{i}") for i in range(4)]
        vst = A.alloc([128, 4, NH * 65], BF16, "vst")
        dtst = A.alloc([128, 4, 64], F32, "dtst")
        acst = A.alloc([128, 4, 64], F32, "acst")
        acsf = A.alloc([64, 512], F32, "acsf")
        dsm = [A.alloc([128, 64], F32, f"dsm{i}") for i in range(4)]
        dtbb = A.alloc([128, 64], F32, "dtbb")
        ab = A.alloc([128, 64], F32, "ab")
        P.ld(qa, dtbb, self.dtb[L].broadcast_to([128, 64]))
        P.ld(qa, ab, self.alog[L].broadcast_to([128, 64]))
        P.act(ab, ab, AF.Exp)
        P.ts("dve", ab, ab, -1.0, None, ALU.mult)
        P.memset("pool", vst, 1.0)
        g1 = self.lnpc[:, (L * 3 + 0) * 8:(L * 3 + 0) * 8 + 8]
        g2 = self.lnpc[:, (L * 3 + 1) * 8:(L * 3 + 1) * 8 + 8]
        gq = self.qknc[:, L * 2:L * 2 + 1]
        gk = self.qknc[:, L * 2 + 1:L * 2 + 2]
        si = 0

        def prologue(i):
            t0_ = i * 512
            xT_ = xTs[i % 2]
            if L == 0:
                P.ld(qa, xtok, self.x_in[t0_:t0_ + 512, :].rearrange("(s p) d -> p s d", p=128))
                for sub in range(4):
                    for kc in range(8):
                        if kc % 4 == 0:
                            ps = self.nextps()
                        P.tr(ps[:, (kc % 4) * 128:(kc % 4) * 128 + 128], xtok[:, sub, kc * 128:(kc + 1) * 128], self.identf)
                        if kc % 4 == 3:
                            P.cp("act" if (kc // 4) % 2 else "dve",
                                 xT_[:, kc - 3:kc + 1, sub * 128:(sub + 1) * 128],
                                 ps.re("p (k t) -> p k t", k=4))
            else:
                P.ld(qa, xT_, self.XL[i].rearrange("p (k t) -> p k t", k=8))
            self.rmsnorm(xT_, g1, h1s[i % 2], sq, rst)

        prologue(0)
        for i in range(self.NTL):
            t0 = i * 512
            xT = xTs[i % 2]
            self.ffn(ws, xT, h1s[i % 2], actb, tmps)
            P.st(qa, self.X1[i].rearrange("p (k t) -> p k t", k=8), xT)
            self.rmsnorm(xT, g2, h, sq, rst)
            pend = None

            def finish(pd_):
                pq_, s__, c_, isq_, sg_, b_ = pd_
                pst = self.nextps()
                P.mm(pst, self.blkones, s__)
                t = tmps[c_ % 4]
                P.act(t, pst, AF.Ln, bias=EPS, scale=1.0 / 64.0)
                P.act(t, t, AF.Exp, bias=(LN8 if isq_ else 0.0), scale=-0.5)
                P.stt("dve", sg_[:, c_, :], pq_, gq if isq_ else gk, t, ALU.mult, ALU.mult)
                if c_ == 3:
                    dst = (self.Qs if isq_ else self.Ks)[(b_ % 2) * 4:(b_ % 2) * 4 + 4, :, t0:t0 + 512].rearrange("c p t -> p c t")
                    P.st(qa, dst, sg_)

            for b in range(4):
                blk = ws.next().re("p (k n) -> p k n", k=8)
                isq = b < 2
                sg = stg[si % 4]; si += 1
                for c in range(4):
                    pq = self.nextps()
                    for kc in range(8):
                        P.mm(pq, blk[:, kc, c * 128:(c + 1) * 128], h[:, kc, :], start=(kc == 0), stop=(kc == 7))
                    s_ = sqt[c % 2]
                    P.act(s_, pq, AF.Square)
                    if pend is not None:
                        finish(pend)
                    pend = (pq, s_, c, isq, sg, b)
            finish(pend)
            if i + 1 < self.NTL:
                prologue(i + 1)
            for b in range(2):
                blk = ws.next().re("p (k n) -> p k n", k=8)
                for sub in range(4):
                    pv = self.nextps()
                    for kc in range(8):
                        P.mm(pv, h[:, kc, sub * 128:(sub + 1) * 128], blk[:, kc, :], start=(kc == 0), stop=(kc == 7))
                    dstv = vst[:, sub, :].re("p (h e) -> p h e", e=65)[:, b * 8:(b + 1) * 8, 0:64]
                    P.cp("act" if sub % 2 else "dve", dstv, pv.re("p (h e) -> p h e", e=64))
            P.st(qa, self.VAs[t0:t0 + 512, :].rearrange("(s p) f -> p s f", p=128), vst)
            for b in range(4):
                blk = ws.next().re("p (k n) -> p k n", k=8)
                sg = stg[si % 4]; si += 1
                for sub in range(4):
                    pz = self.nextps()
                    for kc in range(8):
                        P.mm(pz, h[:, kc, sub * 128:(sub + 1) * 128], blk[:, kc, :], start=(kc == 0), stop=(kc == 7))
                    P.act(sg[:, sub, :], pz, AF.Silu)
                P.st(qa, self.Zs[t0:t0 + 512, b * 512:(b + 1) * 512].rearrange("(s p) f -> p s f", p=128), sg)
            for b in range(6):
                blk = ws.next().re("p (k n) -> p k n", k=8)
                sg = stg[si % 4]; si += 1
                for c in range(4):
                    pu = self.nextps()
                    for kc in range(8):
                        P.mm(pu, blk[:, kc, c * 128:(c + 1) * 128], h[:, kc, :], start=(kc == 0), stop=(kc == 7))
                    P.cp("act" if c % 2 else "dve", sg[:, c, :], pu)
                P.st(qa, self.Us[b * 4:b * 4 + 4, :, 2 + t0:2 + t0 + 512].rearrange("c p t -> p c t"), sg)
            for b in range(4):
                blk = ws.next().re("p (k n) -> p k n", k=8)
                sg = stg[si % 4]; si += 1
                for c in range(4):
                    pg = self.nextps()
                    for kc in range(8):
                        P.mm(pg, blk[:, kc, c * 128:(c + 1) * 128], h[:, kc, :], start=(kc == 0), stop=(kc == 7))
                    P.act(sg[:, c, :], pg, AF.Sigmoid)
                P.st(qa, self.Gs[b * 4:b * 4 + 4, :, t0:t0 + 512].rearrange("c p t -> p c t"), sg)
            blk = ws.next()[:, 0:512].re("p (k n) -> p k n", k=8)
            for sub in range(4):
                pd = self.nextps()
                for kc in range(8):
                    P.mm(pd[:, 0:64], h[:, kc, sub * 128:(sub + 1) * 128], blk[:, kc, :], start=(kc == 0), stop=(kc == 7))
                t1, t2 = dsm[0], dsm[1]
                P.tt("dve", t1, pd[:, 0:64], dtbb, ALU.add)
                P.act(t1, t1, AF.Exp)
                P.act(dtst[:, sub, :], t1, AF.Ln, bias=1.0)
                P.tt("dve", t2, dtst[:, sub, :], ab, ALU.mult)
                pc = self.nextps()
                P.mm(pc[:, 0:32], self.MF, t2[:, 0:32])
                P.mm(pc[:, 32:64], self.MB, t2[:, 32:64])
                P.cp("dve", acst[:, sub, :], pc[:, 0:64])
                pf = self.nextps()
                P.mm(pf[0:32, 0:128], t2[:, 0:32], self.MF)
                P.mm(pf[32:64, 0:128], t2[:, 32:64], self.MB)
                P.cp("act", acsf[:, sub * 128:(sub + 1) * 128], pf[0:64, 0:128])
            P.st(qa, self.DTs[t0:t0 + 512, :].rearrange("(s p) f -> p s f", p=128), dtst)
            P.st(qa, self.ACSs[t0:t0 + 512, :].rearrange("(s p) f -> p s f", p=128), acst)
            P.st(qa, self.ACSF[:, t0:t0 + 512], acsf)
            c0 = i * 4
            P.st(qa, self.ACSE[c0:c0 + 4, 0:32].unsqueeze(0), acst[127:128, :, 0:32])
            P.st(qa, self.ACSE[c0:c0 + 4, 32:64].unsqueeze(0), acst[0:1, :, 32:64])
        P.barrier()
        A.reset(m0)

    def stage2a(self, L):
        P, A, NT = self.P, self.A, self.NT
        m0 = A.mark()
        q = "sp"
        cw = A.alloc([128, 120], F32)
        P.ld(q, cw, self.convw[L])
        cbc = A.alloc([128, 24], F32, "cbc")
        P.ld(q, cbc, self.convbc[L])
        cbr32 = A.alloc([1, 3072], F32)
        P.ld(q, cbr32, self.convbr[L])
        cbr = A.alloc([128, 2560], BF16, "cbr")
        P.memset("pool", cbr, 0.0)
        P.cp("dve", cbr[0:1, :], cbr32[0:1, 0:2560])
        diag = A.alloc([128, 5, 24, 128], BF16, "diag")
        for j in range(5):
            for c in range(24):
                P.ts("pool" if (j * 24 + c) % 2 else "dve", diag[:, j, c, :], self.identf,
                     cw[:, j * 24 + c:j * 24 + c + 1], None, ALU.mult)
        NB = 2
        uw = [A.alloc([128, 24, 132], BF16, f"uw{i}") for i in range(NB)]
        dtt = [A.alloc([128, 64], F32, f"dtt{i}") for i in range(NB)]
        acs = [A.alloc([128, 64], F32, f"acs{i}") for i in range(NB)]
        ace = [A.alloc([128, 64], F32, f"ace{i}") for i in range(NB)]
        xtk = [A.alloc([128, 2048], BF16, f"xtk{i}") for i in range(NB)]
        btk = [A.alloc([128, 512], BF16, f"btk{i}") for i in range(NB)]
        bfm = [A.alloc([128, 4, 128], BF16, f"bfm{i}") for i in range(NB)]
        cfm = [A.alloc([128, 4, 128], BF16, f"cfm{i}") for i in range(NB)]
        xdd = [A.alloc([128, 2048], BF16, f"xdd{i}") for i in range(NB)]
        pbb = [A.alloc([128, 2048], BF16, f"pbb{i}") for i in range(NB)]
        sm = [A.alloc([128, 32], F32, f"sm{i}") for i in range(4)]
        Pb = A.alloc([128, 2048], F32, "Pb")
        P.memset("pool", Pb, 0.0)

        def loads(c):
            k = c % NB
            P.ld(q, uw[k], self.Us[:, :, c * 128:c * 128 + 132].rearrange("k p t -> p k t"))
            P.ld(q, dtt[k], self.DTs[c * 128:(c + 1) * 128, :])
            P.ld(q, acs[k], self.ACSs[c * 128:(c + 1) * 128, :])
            P.ld(q, ace[k], self.ACSE[c:c + 1, :].broadcast_to([128, 64]))

        order = list(range(self.NCH - 1, -1, -1))
        loads(order[0])
        for n, c in enumerate(order):
            k = c % NB
            if n + 1 < len(order):
                loads(order[n + 1])
            u = uw[k]
            if c % 16 == 0:
                P.ts("dve", u[:, :, 0:2], u[:, :, 0:2], self.flagc[:, 0:1], None, ALU.mult)
            if c % 16 == 15:
                P.ts("dve", u[:, :, 130:132], u[:, :, 130:132], self.flagc[:, 0:1], None, ALU.mult)
            for cc in range(20):
                if cc % 4 == 0:
                    ps = self.nextps()
                o = ps[:, (cc % 4) * 128:(cc % 4) * 128 + 128]
                for j in range(5):
                    P.mm(o, u[:, cc, j:j + 128], diag[:, j, cc, :], start=(j == 0), stop=False)
                P.mm(o, self.onesb, cbr[:, cc * 128:(cc + 1) * 128], start=False, stop=True)
                if cc % 4 == 3:
                    if cc < 16:
                        P.act(xtk[k][:, (cc // 4) * 512:(cc // 4 + 1) * 512], ps, AF.Silu)
                    else:
                        P.act(btk[k], ps, AF.Silu)
            for which, dst in ((16, bfm[k]), (20, cfm[k])):
                ps = self.nextps()
                for g in range(4):
                    cc = which + g
                    o = ps[:, g * 128:(g + 1) * 128]
                    for j in range(5):
                        P.mm(o, diag[:, j, cc, :], u[:, cc, j:j + 128], start=(j == 0), stop=(j == 4))
                for g in range(4):
                    cc = which + g
                    P.act(dst[:, g, :], ps[:, g * 128:(g + 1) * 128], AF.Silu, bias=cbc[:, cc:cc + 1])
            P.st(q, self.XSs[c * 128:(c + 1) * 128, :], xtk[k])
            P.st(q, self.BTs[c * 128:(c + 1) * 128, :], btk[k])
            P.st(q, self.BFs[:, :, c * 128:(c + 1) * 128].rearrange("g p t -> p g t"), bfm[k])
            P.st(q, self.CFs[:, :, c * 128:(c + 1) * 128].rearrange("g p t -> p g t"), cfm[k])
            if c % 16 == 15:
                P.ts("dve", Pb, Pb, self.flagc[:, 0:1], None, ALU.mult)
            P.cp("act", pbb[k], Pb)
            P.st(q, self.PBs[c], pbb[k])
            d1, d2, d3 = sm[0], sm[1], sm[2]
            P.tt("dve", d1, ace[k][:, 32:64], acs[k][:, 32:64], ALU.subtract)
            P.act(d1, d1, AF.Exp)
            P.tt("dve", d2, d1, dtt[k][:, 32:64], ALU.mult)
            P.act(d3, ace[k][:, 32:64], AF.Exp)
            P.tt("dve", xdd[k].re("p (h e) -> p h e", e=64), xtk[k].re("p (h e) -> p h e", e=64),
                 d2.re("p (h o) -> p h o", o=1).bc([128, 32, 64]), ALU.mult)
            for g in range(4):
                ps = self.nextps()
                P.mm(ps, btk[k][:, g * 128:(g + 1) * 128], xdd[k][:, g * 512:(g + 1) * 512])
                pv = Pb[:, g * 512:(g + 1) * 512]
                P.tt("dve", pv.re("p (h e) -> p h e", e=64), pv.re("p (h e) -> p h e", e=64),
                     d3[:, g * 8:(g + 1) * 8].re("p (h o) -> p h o", o=1).bc([128, 8, 64]), ALU.mult)
                P.tt("dve", pv, pv, ps, ALU.add)
        P.barrier()
        A.reset(m0)

    def stage2b(self, L):
        P, A, NT = self.P, self.A, self.NT
        m0 = A.mark()
        q = "sp"
        dsb = A.alloc([128, 32], F32)
        P.ld(q, dsb, self.dsk[L].broadcast_to([128, 32]))
        dskd = A.alloc([128, 32, 128], BF16, "dskd")
        for h in range(32):
            P.ts("dve", dskd[:, h, :], self.identf, dsb[:, h:h + 1], None, ALU.mult)
        nwb = A.alloc([128, 2048], F32, "nwb")
        P.ld(q, nwb, self.snw[L].broadcast_to([128, 2048]))
        NB = 2
        xtk = [A.alloc([128, 2048], BF16, f"xtk{i}") for i in range(NB)]
        btk = [A.alloc([128, 512], BF16, f"btk{i}") for i in range(NB)]
        bfm = [A.alloc([128, 4, 128], BF16, f"bfm{i}") for i in range(NB)]
        cfm = [A.alloc([128, 4, 128], BF16, f"cfm{i}") for i in range(NB)]
        dtt = [A.alloc([128, 64], F32, f"dtt{i}") for i in range(NB)]
        acs = [A.alloc([128, 64], F32, f"acs{i}") for i in range(NB)]
        ace = [A.alloc([128, 64], F32, f"ace{i}") for i in range(NB)]
        pbb = [A.alloc([128, 2048], BF16, f"pbb{i}") for i in range(NB)]
        zt = [A.alloc([128, 2048], BF16, f"zt{i}") for i in range(NB)]
        lrow = [[A.alloc([128, 32, 128], F32, f"lrow{d}{i}") for i in range(NB)] for d in range(2)]
        Pf = [A.alloc([128, 512], F32, f"Pf{g}") for g in range(4)]
        pfb = [A.alloc([128, 512], BF16, f"pfb{g}") for g in range(4)]
        for g in range(4):
            P.memset("pool", Pf[g], 0.0)
            P.memset("pool", pfb[g], 0.0)
        cbm = [A.alloc([128, 4, 128], BF16 if EMBF else F32, f"cbm{d}") for d in range(2)]
        eacs = A.alloc([128, 64], F32, "eacs")
        yo = [A.alloc([128, 2048], BF16, f"yo{d}") for d in range(2)]
        y = A.alloc([128, 2048], F32, "y")
        sst = y
        xddg = [A.alloc([128, 512], BF16, f"xddg{g}") for g in range(4)]
        junk = A.alloc([128, 512], F32, "junk")
        nacs = A.alloc([128, 64], F32, "nacs")
        Em = [A.alloc([128, 128], BF16 if EMBF else F32, f"Em{i}") for i in range(6)]
        GT = [A.alloc([128, 128], BF16, f"GT{i}") for i in range(8)]
        sm = [A.alloc([128, 32], F32, f"sm{i}") for i in range(4)]
        ssq = A.alloc([128, 4], F32, "ssq")
        rs = A.alloc([128, 4], F32, "rs")
        sT = [A.alloc([128, 16, 128], BF16, f"sT{i}") for i in range(2)]

        def loads(c):
            k = c % NB
            sl = slice(c * 128, (c + 1) * 128)
            P.ld(q, xtk[k], self.XSs[sl, :])
            P.ld(q, btk[k], self.BTs[sl, :])
            P.ld(q, bfm[k], self.BFs[:, :, sl].rearrange("g p t -> p g t"))
            P.ld(q, cfm[k], self.CFs[:, :, sl].rearrange("g p t -> p g t"))
            P.ld(q, dtt[k], self.DTs[sl, :])
            P.ld(q, acs[k], self.ACSs[sl, :])
            P.ld(q, ace[k], self.ACSE[c:c + 1, :].broadcast_to([128, 64]))
            P.ld(q, pbb[k], self.PBs[c])
            P.ld(q, zt[k], self.Zs[sl, :])
            for d in range(2):
                P.ld(q, lrow[d][k], self.ACSF[d * 32:(d + 1) * 32, sl].unsqueeze(0).broadcast_to([128, 32, 128]))

        def tail(c):
            P.memset("pool", ssq, 0.0)
            for g in range(4):
                P.act(junk, y[:, g * 512:(g + 1) * 512], AF.Square, accum=ssq[:, g:g + 1])
            P.act(rs, ssq, AF.Ln, bias=EPS, scale=1.0 / 512.0)
            P.act(rs, rs, AF.Exp, scale=-0.5)
            for g in range(4):
                P.stt("dve", sst[:, g * 512:(g + 1) * 512], y[:, g * 512:(g + 1) * 512],
                      rs[:, g:g + 1], nwb[:, g * 512:(g + 1) * 512], ALU.mult, ALU.mult)
            so = sT[c % 2]
            for cc in range(16):
                if cc % 4 == 0:
                    ps = self.nextps()
                P.tr(ps[:, (cc % 4) * 128:(cc % 4) * 128 + 128], sst[:, cc * 128:(cc + 1) * 128], self.identf)
                if cc % 4 == 3:
                    P.cp("act" if (cc // 4) % 2 else "dve", so[:, cc - 3:cc + 1, :], ps.re("p (k t) -> p k t", k=4))
            P.st(q, self.SSMs[:, :, c * 128:(c + 1) * 128].rearrange("k p t -> p k t"), so)

        loads(0)
        gi = 0
        for c in range(self.NCH):
            k = c % NB
            if c + 1 < self.NCH:
                loads(c + 1)
            if c % 16 == 0 and c > 0:
                for g in range(4):
                    P.ts("dve", Pf[g], Pf[g], self.flagc[:, 0:1], None, ALU.mult)
                    P.cp("act", pfb[g], Pf[g])
            P.act(eacs, acs[k], AF.Exp)
            for d in range(2):
                for g in range(4):
                    st_ = pfb[g] if d == 0 else pbb[k][:, g * 512:(g + 1) * 512]
                    ps = self.nextps()
                    P.mm(ps, cfm[k][:, g, :], st_)
                    P.tt("dve", yo[d][:, g * 512:(g + 1) * 512].re("p (h e) -> p h e", e=64),
                         ps.re("p (h e) -> p h e", e=64),
                         eacs[:, d * 32 + g * 8:d * 32 + g * 8 + 8].re("p (h o) -> p h o", o=1).bc([128, 8, 64]),
                         ALU.mult)
            if c > 0:
                tail(c - 1)
            d1, d2, d3 = sm[0], sm[1], sm[2]
            P.tt("dve", d1, ace[k][:, 0:32], acs[k][:, 0:32], ALU.subtract)
            P.act(d1, d1, AF.Exp)
            P.tt("dve", d2, d1, dtt[k][:, 0:32], ALU.mult)
            P.act(d3, ace[k][:, 0:32], AF.Exp)
            for g in range(4):
                P.tt("pool", xddg[g].re("p (h e) -> p h e", e=64),
                     xtk[k][:, g * 512:(g + 1) * 512].re("p (h e) -> p h e", e=64),
                     d2[:, g * 8:(g + 1) * 8].re("p (h o) -> p h o", o=1).bc([128, 8, 64]), ALU.mult)
                P.tt("pool", Pf[g].re("p (h e) -> p h e", e=64), Pf[g].re("p (h e) -> p h e", e=64),
                     d3[:, g * 8:(g + 1) * 8].re("p (h o) -> p h o", o=1).bc([128, 8, 64]), ALU.mult)
            pcb = self.nextps()
            for g in range(4):
                P.mm(pcb[:, g * 128:(g + 1) * 128], bfm[k][:, g, :], cfm[k][:, g, :])
            pcb3 = pcb.re("p (g t) -> p g t", g=4)
            P.tt("dve", cbm[0], pcb3, self.MF.re("p (o t) -> p o t", o=1).bc([128, 4, 128]), ALU.mult)
            P.tt("dve", cbm[1], pcb3, self.MB.re("p (o t) -> p o t", o=1).bc([128, 4, 128]), ALU.mult)
            P.act(nacs, dtt[k], AF.Ln)
            P.tt("dve", nacs, nacs, acs[k], ALU.subtract)
            for g in range(4):
                psy = self.nextps()
                P.mm(psy, self.identb, yo[0][:, g * 512:(g + 1) * 512], start=True, stop=False)
                P.mm(psy, self.identb, yo[1][:, g * 512:(g + 1) * 512], start=False, stop=False)
                for hh in range(8):
                    h = g * 8 + hh
                    o = psy[:, hh * 64:(hh + 1) * 64]
                    xh = xtk[k][:, h * 64:(h + 1) * 64]
                    for d in range(2):
                        hd = d * 32 + h
                        em, gt = Em[gi % 6], GT[gi % 8]
                        gi += 1
                        P.act(em, lrow[d][k][:, h, :], AF.Exp, bias=nacs[:, hd:hd + 1])
                        P.stt("dve", gt, em, 1e30, cbm[d][:, g, :], ALU.min, ALU.mult)
                        P.mm(o, gt, xh, start=False, stop=False)
                    P.mm(o, dskd[:, h, :], xh, start=False, stop=(hh == 7))
                P.tt("dve", y[:, g * 512:(g + 1) * 512], psy, zt[k][:, g * 512:(g + 1) * 512], ALU.mult)
                ps = self.nextps()
                P.mm(ps, btk[k][:, g * 128:(g + 1) * 128], xddg[g])
                P.tt("dve", Pf[g], Pf[g], ps, ALU.add)
                P.cp("pool", pfb[g], Pf[g])
        tail(self.NCH - 1)
        P.barrier()
        A.reset(m0)

    def stage2c(self, L):
        P, A, NT, R = self.P, self.A, self.NT, self.R
        m0 = A.mark()
        q = "sp"
        rr = A.alloc([120, 62], F32)
        P.ld(q, rr, self.relrow[L][0].rearrange("(p f) -> p f", p=120))
        qk = A.alloc([1, 128], F32)
        P.ld(q, qk, self.qkrow[L])
        mx = A.alloc([1, 4], F32)
        rmx = A.alloc([120, 1], F32)
        rmt = A.alloc([1, 120], F32)

        def amax(o, i):
            oa, ia = o.ap, i.ap
            P.op("dve", lambda e: e.tensor_reduce(out=oa, in_=ia, axis=AX.X, op=ALU.max, apply_absolute_value=True), [i], [o])

        amax(rmx, rr)
        ptm = self.nextps()
        P.tr(ptm[0:1, 0:120], rmx, self.identf[0:120, 0:120])
        P.cp("dve", rmt, ptm[0:1, 0:120])
        amax(mx[:, 0:1], rmt)
        amax(mx[:, 1:2], qk[:, 0:64])
        amax(mx[:, 2:3], qk[:, 64:128])
        P.tt("dve", mx[:, 3:4], mx[:, 1:2], mx[:, 2:3], ALU.mult)
        P.stt("dve", mx[:, 3:4], mx[:, 3:4], -8.0, mx[:, 0:1], ALU.mult, ALU.subtract)
        pm = self.nextps()
        P.mm(pm[:, 0:1], self._ones_row(), mx[:, 3:4])
        negM = A.alloc([128, 1], F32, "negM")
        P.cp("dve", negM, pm[:, 0:1])
        tab = A.alloc([128, NH, NTAB * 64], BF16, "tab")
        m1 = A.mark()
        for h0 in range(0, NH, 2):
            t32 = A.alloc([128, 2 * NTAB * 64], F32)
            P.ld(q, t32, self.tabf[L][:, h0 * NTAB * 64:(h0 + 2) * NTAB * 64])
            P.cp("dve" if (h0 // 2) % 2 else "pool", tab[:, h0:h0 + 2, :].re("p h n -> p (h n)"), t32)
            if h0 % 4 == 2:
                P.barrier()
                A.reset(m1)
        A.reset(m1)
        NB = 2
        qz = [[A.alloc([128, 8, 256], BF16, f"qz{par}{i}") for i in range(NB)] for par in range(2)]
        kz = [[A.alloc([128, 8, 768], BF16, f"kz{par}{i}") for i in range(NB)] for par in range(2)]
        for par in range(2):
            a0 = 64 if par == 0 else 0
            for i in range(NB):
                P.memset("pool", qz[par][i], 0.0)
                P.memset("pool", kz[par][i], 0.0)
                P.cp("dve", kz[par][i][a0:a0 + 12, :, :],
                     self.ind[a0:a0 + 12, :].re("p (o t) -> p o t", o=1).bc([12, 8, 768]))
        va = [A.alloc([128, 6, NH * 65], BF16, f"va{i}") for i in range(NB)]
        PT = [A.alloc([128, 6, 256], BF16, f"PT{i}") for i in range(3)]
        atk = A.alloc([128, 2, D], F32, "atk")
        aT = [A.alloc([128, 8, 256], BF16, f"aT{i}") for i in range(2)]
        rc = [A.alloc([128, 2], F32, f"rc{i}") for i in range(2)]

        def sG(G):
            return min(max(4 * G - 4, 0), R - 12)

        def loads(G):
            k = G % NB
            t0 = G * 256
            k0 = sG(G) * 64
            for par in range(2):
                d0 = par * 64
                a0 = 64 if par == 0 else 0
                P.ld(q, qz[par][k][d0:d0 + 64, :, :], self.Qs[:, d0:d0 + 64, t0:t0 + 256].rearrange("c p t -> p c t"))
                P.ld(q, kz[par][k][d0:d0 + 64, :, :], self.Ks[:, d0:d0 + 64, k0:k0 + 768].rearrange("c p t -> p c t"))
                P.cp("dve", qz[par][k][a0:a0 + 12, :, :],
                     self.rowmb[a0:a0 + 12, G * 256:(G + 1) * 256].re("p (o t) -> p o t", o=1).bc([12, 8, 256]))
            P.ld(q, va[k], self.VAs[k0:k0 + 768, :].rearrange("(b p) f -> p b f", p=128))

        loads(0)
        hi = 0
        for G in range(self.NG):
            k = G % NB
            if G + 1 < self.NG:
                loads(G + 1)
            s_g = sG(G)
            def scores(h):
                ch = h // 2
                par = h % 2
                pt = PT[(G * NH + h) % 3]
                for kb in range(6):
                    if kb % 2 == 0:
                        psb = self.nextps()
                    o = psb[:, (kb % 2) * 256:(kb % 2) * 256 + 256]
                    P.mm(o, kz[par][k][:, ch, kb * 128:(kb + 1) * 128], qz[par][k][:, ch, :], start=True, stop=False)
                    b0 = 10 - (s_g + 2 * kb - 4 * G)
                    P.mm(o, self.identb, tab[:, h, b0 * 64:(b0 + 4) * 64], start=False, stop=True)
                    if kb % 2 == 1:
                        P.act(pt[:, kb - 1:kb + 1, :].re("p a b -> p (a b)"), psb, AF.Exp, bias=negM[:, 0:1])
                return pt

            def pv(h, pt):
                pso = self.nextps()
                for qh in range(2):
                    for kb in range(6):
                        P.mm(pso[:, qh * 65:qh * 65 + 65], pt[:, kb, qh * 128:(qh + 1) * 128],
                             va[k][:, kb, h * 65:(h + 1) * 65], start=(kb == 0), stop=(kb == 5))
                r_ = rc[h % 2]
                P.recip(r_, pso[:, 0:130].re("p (a b) -> p a b", b=65)[:, :, 64])
                for qh in range(2):
                    P.ts("dve", atk[:, qh, h * 64:(h + 1) * 64], pso[:, qh * 65:qh * 65 + 64], r_[:, qh:qh + 1], None, ALU.mult)

            prev = None
            for h in range(NH):
                pt = scores(h)
                if not PIPE:
                    pv(h, pt)
                    continue
                if prev is not None:
                    pv(*prev)
                prev = (h, pt)
            if PIPE:
                pv(*prev)
            ao = aT[G % 2]
            for qh in range(2):
                for cc in range(8):
                    if cc % 4 == 0:
                        ps = self.nextps()
                    P.tr(ps[:, (cc % 4) * 128:(cc % 4) * 128 + 128], atk[:, qh, cc * 128:(cc + 1) * 128], self.identf)
                    if cc % 4 == 3:
                        P.cp("act" if (cc // 4) % 2 else "dve", ao[:, cc - 3:cc + 1, qh * 128:(qh + 1) * 128],
                             ps.re("p (k t) -> p k t", k=4))
            P.st(q, self.ATTs[:, :, G * 256:(G + 1) * 256].rearrange("c p t -> p c t"), ao)
        P.barrier()
        A.reset(m0)

    def _ones_row(self):
        return self.MF[0:1, :]

    def stage3(self, L, last):
        P, A, NT = self.P, self.A, self.NT
        m0 = A.mark()
        qa = "pool"
        plan = []
        for i in range(self.NTL):
            plan += [self.wblk(L, B_ATT + b) for b in range(2)]
            plan += [self.wblk(L, B_SSM + b) for b in range(4)]
            plan += [self.wblk(L, B_OUT + b) for b in range(2)]
            plan += [self.wblk(L, B_UP2 + b) for b in range(11)]
            plan += [self.wblk(L, B_DN2 + b, 22 * 128) for b in range(8)]
        ws = self.WStream(self, plan)
        xT = A.alloc([128, 8, 512], F32, "xT")
        aT = A.alloc([128, 8, 512], BF16, "aTl")
        sT = A.alloc([128, 16, 512], BF16, "sTl")
        gt = A.alloc([128, 16, 512], BF16, "gt")
        t1 = A.alloc([128, 8, 512], F32, "t1")
        mg = A.alloc([128, 8, 512], BF16, "mg")
        h = A.alloc([128, 8, 512], BF16, "h")
        sq = A.alloc([128, 8, 512], BF16, "sq")
        actb = A.alloc([128, 22, 512], BF16, "actb")
        tmps = [A.alloc([128, 512], F32, f"tmp{i}") for i in range(4)]
        rst = A.alloc([128, 512], F32, "rst")
        xo = t1.re("p k t -> p (k t)").re("p (a d) -> p a d", a=4) if last else None
        g3 = self.lnpc[:, (L * 3 + 2) * 8:(L * 3 + 2) * 8 + 8]
        for i in range(self.NTL):
            t0 = i * 512
            P.ld(qa, aT, self.ATTs[:, :, t0:t0 + 512].rearrange("c p t -> p c t"))
            P.ld(qa, sT, self.SSMs[:, :, t0:t0 + 512].rearrange("c p t -> p c t"))
            P.ld(qa, gt, self.Gs[:, :, t0:t0 + 512].rearrange("c p t -> p c t"))
            P.ld(qa, xT, self.X1[i].rearrange("p (k t) -> p k t", k=8))
            for m in range(8):
                if m % 4 == 0:
                    blk = ws.next().re("p (k n) -> p k n", k=8)
                pa = self.nextps()
                for kc in range(8):
                    P.mm(pa, blk[:, kc, (m % 4) * 128:(m % 4) * 128 + 128], aT[:, kc, :], start=(kc == 0), stop=(kc == 7))
                P.tt("dve", t1[:, m, :], pa, gt[:, m, :], ALU.mult)
            for m in range(8):
                if m % 2 == 0:
                    blk = ws.next().re("p (k n) -> p k n", k=16)
                ps = self.nextps()
                for kc in range(16):
                    P.mm(ps, blk[:, kc, (m % 2) * 128:(m % 2) * 128 + 128], sT[:, kc, :], start=(kc == 0), stop=(kc == 15))
                t = tmps[m % 4]
                P.tt("dve", t, ps, gt[:, 8 + m, :], ALU.mult)
                P.tt("pool", mg[:, m, :], t, t1[:, m, :], ALU.add)
            for m in range(8):
                if m % 4 == 0:
                    blk = ws.next().re("p (k n) -> p k n", k=8)
                po = self.nextps()
                for kc in range(8):
                    P.mm(po, blk[:, kc, (m % 4) * 128:(m % 4) * 128 + 128], mg[:, kc, :], start=(kc == 0), stop=(kc == 7))
                P.tt("dve", xT[:, m, :], po, xT[:, m, :], ALU.add)
            self.rmsnorm(xT, g3, h, sq, rst)
            self.ffn(ws, xT, h, actb, tmps)
            if not last:
                P.st(qa, self.XL[i].rearrange("p (k t) -> p k t", k=8), xT)
            else:
                for sub in range(4):
                    for kc in range(8):
                        if kc % 4 == 0:
                            ps = self.nextps()
                        P.tr(ps[:, (kc % 4) * 128:(kc % 4) * 128 + 128], xT[:, kc, sub * 128:(sub + 1) * 128], self.identf)
                        if kc % 4 == 3:
                            P.cp("act" if (kc // 4) % 2 else "dve", xo[:, sub, (kc - 3) * 128:(kc + 1) * 128], ps)
                ev = P.st(qa, self.y_out[t0:t0 + 512, :].rearrange("(s p) d -> p s d", p=128), xo)
                self.final.append(ev)
        P.barrier()
        A.reset(m0)

    def build(self):
        self.persistent()
        self.cast_weights()
        for L in range(self.nlayers):
            last = (L == self.nlayers - 1)
            self.stage1(L)
            self.stage2a(L)
            self.stage2b(L)
            self.stage2c(L)
            self.stage3(L, last)
        self.P.emit(self.final)
        return self.nc


def _tables(inp):
    f = np.float32
    t = {}
    lnp = np.zeros((128, 48), f)
    for l in range(2):
        for w, nm in enumerate(("ln_ffn1", "ln_mix", "ln_ffn2")):
            lnp[:, (l * 3 + w) * 8:(l * 3 + w) * 8 + 8] = np.asarray(inp[nm][l], f).reshape(8, 128).T
    t["lnp"] = lnp
    qkn = np.zeros((128, 4), f)
    for l in range(2):
        qkn[:, l * 2] = np.tile(np.asarray(inp["q_norm"][l], f), 2)
        qkn[:, l * 2 + 1] = np.tile(np.asarray(inp["k_norm"][l], f), 2)
    t["qkn"] = qkn
    t["qkrow"] = np.stack([np.concatenate([inp["q_norm"][l], inp["k_norm"][l]]) for l in range(2)]).astype(f).reshape(2, 1, 128)
    rel = np.asarray(inp["rel_bias"], f)
    t["relrow"] = rel.reshape(2, 1, -1)
    cols = np.arange(64)
    cstart = np.clip(cols - 8, 0, 48)
    cvalid = (cols[None, :] >= cstart[:, None]) & (cols[None, :] < cstart[:, None] + 16)
    cidx = np.clip(cols[None, :] - cols[:, None] + 15, 0, 30)
    tab = np.zeros((2, 128, NH, NTAB, 64), f)
    for kl in range(2):
        for b in range(NTAB):
            delta = 17 - b + kl
            if 0 <= delta <= 14:
                blk = rel[:, :, delta, :][:, :, cidx]
                blk = np.where(cvalid[None, None], blk, f(NEG))
                tab[:, kl * 64:(kl + 1) * 64, :, b, :] = blk.transpose(0, 3, 1, 2)
    t["tabf"] = tab.reshape(2, 128, NH * NTAB * 64)
    cw = np.asarray(inp["conv_w"], f)
    t["convw"] = cw.reshape(2, 5, 24, 128).transpose(0, 3, 1, 2).reshape(2, 128, 120).copy()
    cb = np.asarray(inp["conv_b"], f)
    t["convbc"] = cb.reshape(2, 24, 128).transpose(0, 2, 1).copy()
    t["convbr"] = cb.reshape(2, 1, 3072)
    t["dtb"] = np.concatenate([inp["dt_bias_fwd"], inp["dt_bias_bwd"]], axis=1).astype(f).reshape(2, 1, 64)
    t["alog"] = np.concatenate([inp["a_log_fwd"], inp["a_log_bwd"]], axis=1).astype(f).reshape(2, 1, 64)
    t["dsk"] = np.asarray(inp["d_skip"], f).reshape(2, 1, 32)
    t["snw"] = np.asarray(inp["ssm_norm"], f).reshape(2, 1, 2048)
    cst = np.zeros((128, 640), f)
    cst[:, 0:128] = np.eye(128)
    cst[:, 128:256] = np.triu(np.ones((128, 128)))
    cst[:, 256:384] = np.tril(np.ones((128, 128)))
    cst[0:64, 384:448] = 1.0
    cst[64:128, 448:512] = 1.0
    cst[:, 512:640] = 1.0
    t["cst"] = cst
    ind = np.zeros((12, 768), f)
    for c in range(12):
        ind[c, c * 64:(c + 1) * 64] = 1.0
    t["indm"] = ind
    for nm in ("w_ffn1_up", "w_ffn1_down", "w_in", "w_attn_proj", "w_ssm_proj", "w_out", "w_ffn2_up", "w_ffn2_down"):
        t[nm] = np.ascontiguousarray(inp[nm], dtype=f)
    return t


def _rowmask(NT, SR):
    R, NG = NT // 64, NT // 256
    rm = np.zeros((12, NG, 4, 64), np.float32)
    for G in range(NG):
        s = min(max(4 * G - 4, 0), R - 12)
        for c in range(12):
            kap = s + c
            for rl in range(4):
                rho = 4 * G + rl
                r = rho % SR
                r0 = min(max(r - 4, 0), SR - 8)
                ok = (kap // SR == rho // SR) and (r0 <= kap % SR <= r0 + 7)
                rm[c, G, rl, :] = 0.0 if ok else NEG
    return rm.reshape(12, NG * 256)


def run_cores(inp, xs, seq_rows, NT, nlayers=2):
    bld = Builder(NT, nlayers=nlayers)
    nc = bld.build()
    t = _tables(inp)
    in_maps = []
    for x, SR in zip(xs, seq_rows):
        m = dict(t)
        m["x"] = np.ascontiguousarray(x, dtype=np.float32)
        m["rowm"] = _rowmask(NT, SR)
        m["flag"] = np.full((128, 1), 1.0 if SR * 64 == NT else 0.0, np.float32)
        in_maps.append(m)
    res = run_bass_kernel_spmd(nc, in_maps, core_ids=list(range(len(xs))))
    return [r["y"] for r in res.results]


def kernel(**inputs):
    inp = {k: np.asarray(v) for k, v in inputs.items()}
    xp = inp["x_prompt"]
    xsm = inp["x_sample"]
    xs = [xp[i] for i in range(4)] + [xsm[4 * j:4 * j + 4].reshape(8192, 1024) for j in range(4)]
    ys = run_cores(inp, xs, [128] * 4 + [32] * 4, 8192)
    y_prompt = np.stack(ys[0:4]).astype(np.float32)
    y_sample = np.concatenate([ys[4 + j].reshape(4, 2048, 1024) for j in range(4)], axis=0).astype(np.float32)
    return (y_prompt, y_sample)
```

```python
import numpy as np
import ml_dtypes
import concourse.bass as bass
import concourse.mybir as mybir
from concourse.bass_utils import run_bass_kernel_spmd

F32 = mybir.dt.float32
BF16 = mybir.dt.bfloat16
AF = mybir.ActivationFunctionType
ALU = mybir.AluOpType
AX = mybir.AxisListType


class Trk:
    __slots__ = ("name", "lw", "rd", "dsem", "dgen")

    def __init__(self, name):
        self.name = name
        self.lw = None
        self.rd = {}
        self.dsem = None
        self.dgen = -1


class V:
    __slots__ = ("t", "ap")

    def __init__(self, t, ap):
        self.t = t
        self.ap = ap

    def __getitem__(self, idx):
        return V(self.t, self.ap[idx])

    def re(self, s, **kw):
        return V(self.t, self.ap.rearrange(s, **kw))

    def bc(self, shape):
        return V(self.t, self.ap.broadcast_to(shape))


class DSem:
    __slots__ = ("idx", "count", "last")

    def __init__(self, idx):
        self.idx = idx
        self.count = 0
        self.last = None


class Prog:
    STREAMS = ("sp", "act", "dve", "pool", "pe")

    def __init__(self, nc):
        self.nc = nc
        self.ops = {s: [] for s in self.STREAMS}
        self.dsems = []
        self.free_ds = []
        self.gen = 0
        self.rr = 0
        self.nbuf = 0

    def sb(self, shape, dt, name=None):
        self.nbuf += 1
        name = name or f"sb{self.nbuf}"
        h = self.nc.alloc_sbuf_tensor(name, list(shape), dt)
        return V(Trk(name), h.ap() if hasattr(h, "ap") else h[:])

    def ps(self, name):
        h = self.nc.alloc_psum_tensor(name, [128, 512], F32)
        return V(Trk(name), h.ap() if hasattr(h, "ap") else h[:])

    def dram(self, name, shape, dt, kind="Internal"):
        h = self.nc.dram_tensor(name, list(shape), dt, kind=kind)
        return h.ap()

    def _deps(self, stream, reads, writes):
        deps = set()
        for r in reads:
            if r.lw is not None:
                deps.add(r.lw)
        for w in writes:
            if w.lw is not None:
                deps.add(w.lw)
            for ev in w.rd.values():
                deps.add(ev)
        return deps

    def _commit(self, ev, key, reads, writes):
        for w in writes:
            w.lw = ev
            w.rd = {}
        for r in reads:
            if r not in writes:
                r.rd[key] = ev

    def op(self, stream, fn, reads, writes):
        reads = [x.t if isinstance(x, V) else x for x in reads if x is not None]
        writes = [x.t if isinstance(x, V) else x for x in writes if x is not None]
        deps = self._deps(stream, reads, writes)
        lst = self.ops[stream]
        ev = ("E", stream, len(lst))
        if stream == "pe":
            deps = {d for d in deps if not (d[0] == "E" and d[1] == "pe")}
        else:
            raw = {r.lw for r in reads if r.lw is not None}
            deps = {d for d in deps if not (d[0] == "E" and d[1] == stream and d not in raw)}
        lst.append(["c", fn, deps, ev, False])
        self._commit(ev, stream, reads, writes)

    def dma(self, stream, out, in_, reads, writes, sem_owner=None):
        reads = [x.t if isinstance(x, V) else x for x in reads if x is not None]
        writes = [x.t if isinstance(x, V) else x for x in writes if x is not None]
        owner = sem_owner if sem_owner is not None else (writes[0] if writes else reads[0])
        if isinstance(owner, V):
            owner = owner.t
        if owner.dsem is None or owner.dgen != self.gen:
            if self.free_ds:
                owner.dsem = self.free_ds.pop()
            elif len(self.dsems) < 80:
                owner.dsem = DSem(len(self.dsems))
                self.dsems.append(owner.dsem)
            else:
                owner.dsem = self.dsems[self.rr % len(self.dsems)]
                self.rr += 1
            owner.dgen = self.gen
        ds = owner.dsem
        deps = self._deps(stream, reads, writes)
        if ds.last is not None:
            deps.add(ds.last)
        ds.count += 16
        ev = ("D", ds.idx, ds.count)
        ds.last = ev
        self.ops[stream].append(["d", (out, in_), deps, ev, True])
        self._commit(ev, ("D", ds.idx), reads, writes)
        return ev

    def emit(self, final_events):
        nc = self.nc
        for s in self.STREAMS:
            for o in self.ops[s]:
                for d in o[2]:
                    if d[0] == "E":
                        self.ops[d[1]][d[2]][4] = True
        for d in final_events:
            if d[0] == "E":
                self.ops[d[1]][d[2]][4] = True
        cnt = {}
        for s in self.STREAMS:
            c = 0
            arr = []
            for o in self.ops[s]:
                if o[0] == "c" and o[4]:
                    c += 1
                arr.append(c)
            cnt[s] = arr
        esem = {s: nc.alloc_semaphore(f"es_{s}") for s in self.STREAMS}
        dsem = [nc.alloc_semaphore(f"ds_{i}") for i in range(len(self.dsems))]

        def resolve(d):
            if d[0] == "E":
                return ("E", d[1]), esem[d[1]], cnt[d[1]][d[2]]
            return ("D", d[1]), dsem[d[1]], d[2]

        engmap = {"sp": "sync", "act": "scalar", "dve": "vector", "pool": "gpsimd", "pe": "tensor"}

        def run_stream(s, eng, extra_final=None):
            waited = {}
            def do_waits(deps):
                need = {}
                for d in deps:
                    k, sem, val = resolve(d)
                    if val > need.get(k, (None, 0))[1]:
                        need[k] = (sem, val)
                for k, (sem, val) in need.items():
                    if waited.get(k, 0) >= val:
                        continue
                    eng.wait_ge(sem, val)
                    waited[k] = val
            for o in self.ops[s]:
                do_waits(o[2])
                if o[0] == "c":
                    ins = o[1](eng)
                    if o[4]:
                        ins.then_inc(esem[s], 1)
                else:
                    out, in_ = o[1]
                    eng.dma_start(out=out, in_=in_).then_inc(dsem[o[3][1]], 16)
            if extra_final:
                do_waits(extra_final)

        with nc.Block() as block:
            @block.sync
            def _(e):
                run_stream("sp", e, final_events)

            @block.scalar
            def _(e):
                run_stream("act", e)

            @block.vector
            def _(e):
                run_stream("dve", e)

            @block.gpsimd
            def _(e):
                run_stream("pool", e)

            @block.tensor
            def _(e):
                run_stream("pe", e)

    def barrier(self):
        evs = set()
        for s in self.STREAMS:
            if self.ops[s]:
                evs.add(self.ops[s][-1][3])
        for ds in self.dsems:
            if ds.last is not None:
                evs.add(ds.last)
        for s in self.STREAMS:
            deps = {d for d in evs if not (d[0] == "E" and d[1] == s)}
            lst = self.ops[s]
            lst.append(["c", (lambda e: e.nop()), deps, ("E", s, len(lst)), False])
        self.gen += 1
        self.free_ds = list(self.dsems)

    @staticmethod
    def _a(x):
        return x.ap if isinstance(x, V) else x

    def _eng(self, e, name):
        return e

    def mm(self, out, lhsT, rhs, start=True, stop=True):
        o, l, r = out.ap, lhsT.ap, rhs.ap
        self.op("pe", lambda e: e.matmul(o, lhsT=l, rhs=r, start=start, stop=stop), [lhsT, rhs], [out])

    def tr(self, out, in_, ident):
        o, i, d = out.ap, in_.ap, ident.ap
        self.op("pe", lambda e: e.transpose(o, i, d), [in_, ident], [out])

    def act(self, out, in_, func, bias=0.0, scale=1.0, accum=None):
        o, i, b, s = out.ap, in_.ap, self._a(bias), self._a(scale)
        ac = accum.ap if accum is not None else None
        rd = [in_] + [x for x in (bias, scale) if isinstance(x, V)]
        wr = [out] + ([accum] if accum is not None else [])
        if ac is None:
            self.op("act", lambda e: e.activation(out=o, in_=i, func=func, bias=b, scale=s), rd, wr)
        else:
            self.op("act", lambda e: e.activation(out=o, in_=i, func=func, bias=b, scale=s, accum_out=ac), rd, wr)

    def tt(self, eng, out, in0, in1, op):
        o, a, b = out.ap, in0.ap, in1.ap
        self.op(eng, lambda e: e.tensor_tensor(out=o, in0=a, in1=b, op=op), [in0, in1], [out])

    def ts(self, eng, out, in0, s1, s2, op0, op1=None):
        o, a, x1, x2 = out.ap, in0.ap, self._a(s1), self._a(s2)
        rd = [in0] + [x for x in (s1, s2) if isinstance(x, V)]
        if op1 is None:
            self.op(eng, lambda e: e.tensor_scalar(out=o, in0=a, scalar1=x1, scalar2=None, op0=op0), rd, [out])
        else:
            self.op(eng, lambda e: e.tensor_scalar(out=o, in0=a, scalar1=x1, scalar2=x2, op0=op0, op1=op1), rd, [out])

    def stt(self, eng, out, in0, scalar, in1, op0, op1):
        o, a, sc, b = out.ap, in0.ap, self._a(scalar), in1.ap
        rd = [in0, in1] + ([scalar] if isinstance(scalar, V) else [])
        self.op(eng, lambda e: e.scalar_tensor_tensor(out=o, in0=a, scalar=sc, in1=b, op0=op0, op1=op1), rd, [out])

    def cp(self, eng, out, in_):
        o, i = out.ap, in_.ap
        if eng == "act":
            self.op("act", lambda e: e.activation(out=o, in_=i, func=AF.Copy), [in_], [out])
        else:
            self.op(eng, lambda e: e.tensor_copy(out=o, in_=i), [in_], [out])

    def recip(self, out, in_):
        o, i = out.ap, in_.ap
        self.op("dve", lambda e: e.reciprocal(out=o, in_=i), [in_], [out])

    def memset(self, eng, out, val):
        o = out.ap
        self.op(eng, lambda e: e.memset(o, val), [], [out])

    def ld(self, q, dst, src_ap):
        return self.dma(q, dst.ap, src_ap, [], [dst])

    def st(self, q, dst_ap, src):
        return self.dma(q, dst_ap, src.ap, [src], [], sem_owner=src)


class Arena:
    def __init__(self, nc, nbytes):
        self.h = nc.alloc_sbuf_tensor("arena", [128, nbytes // 2], BF16)
        self.ap = self.h.ap()
        self.nbytes = nbytes
        self.off = 0
        self.n = 0

    def alloc(self, shape, dt, name=None):
        esz = mybir.dt.size(dt)
        n = int(np.prod(shape[1:]))
        nb = (n * esz + 31) // 32 * 32
        assert self.off + nb <= self.nbytes, f"arena overflow {self.off}+{nb} > {self.nbytes}"
        a = self.ap[0:shape[0], self.off // 2:(self.off + n * esz) // 2]
        if dt != BF16:
            a = a.bitcast(dt)
        if len(shape) > 2:
            names = " ".join(f"d{i}" for i in range(len(shape) - 1))
            kw = {f"d{i}": shape[i + 1] for i in range(len(shape) - 2)}
            a = a.rearrange(f"p ({names}) -> p {names}", **kw)
        self.off += nb
        self.n += 1
        return V(Trk(name or f"a{self.n}"), a)

    def mark(self):
        return self.off

    def reset(self, m):
        self.off = m


D = 1024
DFF = 2816
NH = 16
SH = 32
INC = 10304
EPS = 1e-6
NEG = -30000.0
NBLK = 67
B_UP1, B_DN1, B_WIN, B_ATT, B_SSM, B_OUT, B_UP2, B_DN2 = 0, 11, 19, 40, 42, 46, 48, 59
NTAB = 22
LN8 = -2.0794415416798357
import os
PIPE = os.environ.get("K_PIPE", "1") == "1"
EMBF = os.environ.get("K_EMBF", "1") == "1"


class Builder:
    def __init__(self, NT, nlayers=2, dbg=None, stop_after=None):
        self.NT = NT
        self.NTL = NT // 512
        self.NCH = NT // 128
        self.NG = NT // 256
        self.R = NT // 64
        self.nlayers = nlayers
        self.dbg = dbg or []
        self.stop_after = stop_after
        nc = self.nc = bass.Bass("TRN2", target_bir_lowering=False)
        P = self.P = Prog(nc)
        dr = P.dram
        EI = "ExternalInput"
        self.x_in = dr("x", [NT, D], F32, EI)
        self.w_up = [dr("w_ffn1_up", [2, D, 2 * DFF], F32, EI), dr("w_ffn2_up", [2, D, 2 * DFF], F32, EI)]
        self.w_dn = [dr("w_ffn1_down", [2, DFF, D], F32, EI), dr("w_ffn2_down", [2, DFF, D], F32, EI)]
        self.w_in = dr("w_in", [2, D, INC], F32, EI)
        self.w_ap = dr("w_attn_proj", [2, D, D], F32, EI)
        self.w_sp = dr("w_ssm_proj", [2, 2 * D, D], F32, EI)
        self.w_o = dr("w_out", [2, D, D], F32, EI)
        self.lnp = dr("lnp", [128, 2 * 3 * 8], F32, EI)
        self.qkn = dr("qkn", [128, 4], F32, EI)
        self.tabf = dr("tabf", [2, 128, NH * NTAB * 64], F32, EI)
        self.relrow = dr("relrow", [2, 1, NH * 15 * 31], F32, EI)
        self.qkrow = dr("qkrow", [2, 1, 128], F32, EI)
        self.convw = dr("convw", [2, 128, 120], F32, EI)
        self.convbc = dr("convbc", [2, 128, 24], F32, EI)
        self.convbr = dr("convbr", [2, 1, 3072], F32, EI)
        self.dtb = dr("dtb", [2, 1, 64], F32, EI)
        self.alog = dr("alog", [2, 1, 64], F32, EI)
        self.dsk = dr("dsk", [2, 1, 32], F32, EI)
        self.snw = dr("snw", [2, 1, 2048], F32, EI)
        self.cst = dr("cst", [128, 5 * 128], F32, EI)
        self.indm = dr("indm", [12, 768], F32, EI)
        self.rowm = dr("rowm", [12, self.NG * 256], F32, EI)
        self.flag = dr("flag", [128, 1], F32, EI)
        self.y_out = dr("y", [NT, D], F32, "ExternalOutput")
        kind = "Internal"
        self.Wb = dr("Wb", [2 * NBLK, 128, 4096], BF16, kind)
        self.X1 = dr("X1", [self.NTL, 128, 8 * 512], F32, kind)
        self.XL = dr("XL", [self.NTL, 128, 8 * 512], F32, kind)
        self.Qs = dr("Qs", [8, 128, NT], BF16, kind)
        self.Ks = dr("Ks", [8, 128, NT], BF16, kind)
        self.VAs = dr("VAs", [NT, 1040], BF16, kind)
        self.Zs = dr("Zs", [NT, 2048], BF16, kind)
        self.Us = dr("Us", [24, 128, NT + 4], BF16, kind)
        self.Gs = dr("Gs", [16, 128, NT], BF16, kind)
        self.DTs = dr("DTs", [NT, 64], F32, kind)
        self.ACSs = dr("ACSs", [NT, 64], F32, kind)
        self.ACSF = dr("ACSF", [64, NT], F32, kind)
        self.ACSE = dr("ACSE", [self.NCH, 64], F32, kind)
        self.XSs = dr("XSs", [NT, 2048], BF16, kind)
        self.BTs = dr("BTs", [NT, 512], BF16, kind)
        self.BFs = dr("BFs", [4, 128, NT], BF16, kind)
        self.CFs = dr("CFs", [4, 128, NT], BF16, kind)
        self.PBs = dr("PBs", [self.NCH, 128, 2048], BF16, kind)
        self.SSMs = dr("SSMs", [16, 128, NT], BF16, kind)
        self.ATTs = dr("ATTs", [8, 128, NT], BF16, kind)
        self.dbg_out = {}
        self.A = Arena(nc, 209920)
        self.psb = [P.ps(f"psb{i}") for i in range(8)]
        self.psi = 0
        self.final = []

    def nextps(self):
        p = self.psb[self.psi % 8]
        self.psi += 1
        return p

    def persistent(self):
        P, A = self.P, self.A
        c32 = A.alloc([128, 640], F32, "c32")
        P.ld("sp", c32, self.cst)
        self.identf = c32[:, 0:128]
        self.MF = c32[:, 128:256]
        self.MB = c32[:, 256:384]
        cb = A.alloc([128, 640], BF16, "cbf")
        P.cp("dve", cb, c32)
        self.identb = cb[:, 0:128]
        self.blkones = cb[:, 384:512]
        self.onesb = cb[:, 512:640]
        ind32 = A.alloc([128, 768], F32)
        self.ind = A.alloc([128, 768], BF16, "ind")
        for p0 in (0, 64):
            P.ld("sp", ind32[p0:p0 + 12, :], self.indm)
            P.cp("dve", self.ind[p0:p0 + 12, :], ind32[p0:p0 + 12, :])
        self.lnpc = A.alloc([128, 48], F32, "lnp")
        P.ld("sp", self.lnpc, self.lnp)
        self.qknc = A.alloc([128, 4], F32, "qkn")
        P.ld("sp", self.qknc, self.qkn)
        self.flagc = A.alloc([128, 1], F32, "flag")
        P.ld("sp", self.flagc, self.flag)
        self.rowmb = A.alloc([128, self.NG * 256], BF16, "rowm")
        m = A.mark()
        for g0 in range(0, self.NG, 8):
            n = min(8, self.NG - g0)
            t = A.alloc([128, n * 256], F32)
            for p0 in (0, 64):
                P.ld("sp", t[p0:p0 + 12, :], self.rowm[:, g0 * 256:(g0 + n) * 256])
                P.cp("dve", self.rowmb[p0:p0 + 12, g0 * 256:(g0 + n) * 256], t[p0:p0 + 12, :])
        z = A.alloc([128, 24, 2], BF16)
        P.memset("pool", z, 0.0)
        P.st("sp", self.Us[:, :, 0:2].rearrange("k p t -> p k t"), z)
        P.st("sp", self.Us[:, :, self.NT + 2:self.NT + 4].rearrange("k p t -> p k t"), z)
        P.barrier()
        A.reset(m)
        self.base = A.mark()

    def cast_weights(self):
        P, A = self.P, self.A
        m = A.mark()
        NBW = 4
        st32 = [A.alloc([128, 4096], F32) for _ in range(NBW)]
        stb = [A.alloc([128, 4096], BF16) for _ in range(NBW)]
        engs = ["dve", "act", "pool"]
        n = 0
        for L in range(self.nlayers):
            jobs = []
            for f, (bu, bd) in enumerate(((B_UP1, B_DN1), (B_UP2, B_DN2))):
                wu = self.w_up[f][L].rearrange("(kc p) n -> p kc n", p=128)
                wd = self.w_dn[f][L].rearrange("(kc p) n -> p kc n", p=128)
                for b in range(11):
                    jobs.append((bu + b, 8, 512, [(0, 256, wu[:, :, b * 256:(b + 1) * 256]),
                                                  (256, 512, wu[:, :, DFF + b * 256:DFF + (b + 1) * 256])]))
                for mm_ in range(8):
                    jobs.append((bd + mm_, 22, 128, [(0, 128, wd[:, :, mm_ * 128:(mm_ + 1) * 128])]))
            wi = self.w_in[L].rearrange("(kc p) n -> p kc n", p=128)
            for b in range(16):
                jobs.append((B_WIN + b, 8, 512, [(0, 512, wi[:, :, b * 512:(b + 1) * 512])]))
            for b in range(4):
                jobs.append((B_WIN + 16 + b, 8, 512, [(0, 512, wi[:, :, 8256 + b * 512:8256 + (b + 1) * 512])]))
            jobs.append((B_WIN + 20, 8, 64, [(0, 64, wi[:, :, 8192:8256])]))
            wa = self.w_ap[L].rearrange("(kc p) n -> p kc n", p=128)
            ws_ = self.w_sp[L].rearrange("(kc p) n -> p kc n", p=128)
            wo = self.w_o[L].rearrange("(kc p) n -> p kc n", p=128)
            for b in range(2):
                jobs.append((B_ATT + b, 8, 512, [(0, 512, wa[:, :, b * 512:(b + 1) * 512])]))
            for b in range(4):
                jobs.append((B_SSM + b, 16, 256, [(0, 256, ws_[:, :, b * 256:(b + 1) * 256])]))
            for b in range(2):
                jobs.append((B_OUT + b, 8, 512, [(0, 512, wo[:, :, b * 512:(b + 1) * 512])]))
            for (bi, kc, nb, parts) in jobs:
                s32 = st32[n % NBW]
                sb_ = stb[n % NBW]
                v32 = s32[:, 0:kc * nb].re("p (k n) -> p k n", k=kc)
                for (c0, c1, src) in parts:
                    P.dma("sp" if n % 2 == 0 else "act", v32.ap[:, :, c0:c1], src, [], [s32])
                P.cp(engs[n % 3], sb_[:, 0:kc * nb], s32[:, 0:kc * nb])
                P.dma("pool", self.Wb[L * NBLK + bi][:, 0:kc * nb], sb_.ap[:, 0:kc * nb], [sb_], [], sem_owner=sb_)
                n += 1
        P.barrier()
        A.reset(m)

    class WStream:
        def __init__(self, bld, plan, R=4, q="sp"):
            self.b = bld
            self.plan = plan
            self.R = R
            self.ahead = R - 1
            self.q = q
            self.slots = [bld.A.alloc([128, 4096], BF16, f"wslot{i}") for i in range(R)]
            self.issued = 0
            self.cur = 0

        def next(self):
            i = self.cur
            lim = min(len(self.plan), i + self.ahead + 1)
            while self.issued < lim:
                j = self.issued
                ap, n = self.plan[j]
                s = self.slots[j % self.R]
                self.b.P.dma(self.q, s.ap[:, 0:n], ap[:, 0:n], [], [s])
                self.issued += 1
            self.cur += 1
            return self.slots[i % self.R]

    def wblk(self, L, bi, n=4096):
        return (self.Wb[L * NBLK + bi], n)

    def rmsnorm(self, xT, gcol, h, sq, tmp):
        P = self.P
        P.act(sq, xT, AF.Square)
        ps = self.nextps()
        for kc in range(8):
            P.mm(ps, self.onesb, sq[:, kc, :], start=(kc == 0), stop=(kc == 7))
        P.act(tmp, ps, AF.Ln, bias=EPS, scale=1.0 / D)
        P.act(tmp, tmp, AF.Exp, scale=-0.5)
        for kc in range(8):
            P.stt("dve", h[:, kc, :], xT[:, kc, :], gcol[:, kc:kc + 1], tmp, ALU.mult, ALU.mult)

    def ffn(self, ws, xT, h, actb, tmps):
        P = self.P
        for j in range(22):
            if j % 2 == 0:
                blk = ws.next()[:, 0:4096].re("p (k n) -> p k n", k=8)
            ca = (j % 2) * 128
            pa, pb = self.nextps(), self.nextps()
            for kc in range(8):
                P.mm(pa, blk[:, kc, ca:ca + 128], h[:, kc, :], start=(kc == 0), stop=(kc == 7))
            for kc in range(8):
                P.mm(pb, blk[:, kc, 256 + ca:256 + ca + 128], h[:, kc, :], start=(kc == 0), stop=(kc == 7))
            t = tmps[j % len(tmps)]
            P.act(t, pa, AF.Silu)
            P.tt("dve", actb[:, j, :], t, pb, ALU.mult)
        for m in range(8):
            blk = ws.next()[:, 0:22 * 128].re("p (k n) -> p k n", k=22)
            po = self.nextps()
            for j in range(22):
                P.mm(po, blk[:, j, :], actb[:, j, :], start=(j == 0), stop=(j == 21))
            P.stt("dve", xT[:, m, :], po, 0.5, xT[:, m, :], ALU.mult, ALU.add)

    def stage1(self, L):
        P, A, NT = self.P, self.A, self.NT
        m0 = A.mark()
        qa = "pool"
        plan = []
        for i in range(self.NTL):
            plan += [self.wblk(L, B_UP1 + b) for b in range(11)]
            plan += [self.wblk(L, B_DN1 + b, 22 * 128) for b in range(8)]
            plan += [self.wblk(L, B_WIN + b) for b in range(20)]
            plan += [self.wblk(L, B_WIN + 20, 512)]
        ws = self.WStream(self, plan)
        xTs = [A.alloc([128, 8, 512], F32, f"xT{i}") for i in range(2)]
        xtok = A.alloc([128, 4, 1024], F32, "xtok") if L == 0 else None
        h1s = [A.alloc([128, 8, 512], BF16, f"h1{i}") for i in range(2)]
        h = A.alloc([128, 8, 512], BF16, "h")
        sq = A.alloc([128, 8, 512], BF16, "sq")
        actb = A.alloc([128, 22, 512], BF16, "actb")
        tmps = [A.alloc([128, 512], F32, f"tmp{i}") for i in range(4)]
        rst = A.alloc([128, 512], F32, "rst")
        sqt = [A.alloc([128, 512], BF16, f"sqt{i}") for i in range(2)]
        stg = [A.alloc([128, 4, 512], BF16, f"stg{i}") for i in range(4)]
        vst = A.alloc([128, 4, NH * 65], BF16, "vst")
        dtst = A.alloc([128, 4, 64], F32, "dtst")
        acst = A.alloc([128, 4, 64], F32, "acst")
        acsf = A.alloc([64, 512], F32, "acsf")
        dsm = [A.alloc([128, 64], F32, f"dsm{i}") for i in range(4)]
        dtbb = A.alloc([128, 64], F32, "dtbb")
        ab = A.alloc([128, 64], F32, "ab")
        P.ld(qa, dtbb, self.dtb[L].broadcast_to([128, 64]))
        P.ld(qa, ab, self.alog[L].broadcast_to([128, 64]))
        P.act(ab, ab, AF.Exp)
        P.ts("dve", ab, ab, -1.0, None, ALU.mult)
        P.memset("pool", vst, 1.0)
        g1 = self.lnpc[:, (L * 3 + 0) * 8:(L * 3 + 0) * 8 + 8]
        g2 = self.lnpc[:, (L * 3 + 1) * 8:(L * 3 + 1) * 8 + 8]
        gq = self.qknc[:, L * 2:L * 2 + 1]
        gk = self.qknc[:, L * 2 + 1:L * 2 + 2]
        si = 0

        def prologue(i):
            t0_ = i * 512
            xT_ = xTs[i % 2]
            if L == 0:
                P.ld(qa, xtok, self.x_in[t0_:t0_ + 512, :].rearrange("(s p) d -> p s d", p=128))
                for sub in range(4):
                    for kc in range(8):
                        if kc % 4 == 0:
                            ps = self.nextps()
                        P.tr(ps[:, (kc % 4) * 128:(kc % 4) * 128 + 128], xtok[:, sub, kc * 128:(kc + 1) * 128], self.identf)
                        if kc % 4 == 3:
                            P.cp("act" if (kc // 4) % 2 else "dve",
                                 xT_[:, kc - 3:kc + 1, sub * 128:(sub + 1) * 128],
                                 ps.re("p (k t) -> p k t", k=4))
            else:
                P.ld(qa, xT_, self.XL[i].rearrange("p (k t) -> p k t", k=8))
            self.rmsnorm(xT_, g1, h1s[i % 2], sq, rst)

        prologue(0)
        for i in range(self.NTL):
            t0 = i * 512
            xT = xTs[i % 2]
            self.ffn(ws, xT, h1s[i % 2], actb, tmps)
            P.st(qa, self.X1[i].rearrange("p (k t) -> p k t", k=8), xT)
            self.rmsnorm(xT, g2, h, sq, rst)
            pend = None

            def finish(pd_):
                pq_, s__, c_, isq_, sg_, b_ = pd_
                pst = self.nextps()
                P.mm(pst, self.blkones, s__)
                t = tmps[c_ % 4]
                P.act(t, pst, AF.Ln, bias=EPS, scale=1.0 / 64.0)
                P.act(t, t, AF.Exp, bias=(LN8 if isq_ else 0.0), scale=-0.5)
                P.stt("dve", sg_[:, c_, :], pq_, gq if isq_ else gk, t, ALU.mult, ALU.mult)
                if c_ == 3:
                    dst = (self.Qs if isq_ else self.Ks)[(b_ % 2) * 4:(b_ % 2) * 4 + 4, :, t0:t0 + 512].rearrange("c p t -> p c t")
                    P.st(qa, dst, sg_)

            for b in range(4):
                blk = ws.next().re("p (k n) -> p k n", k=8)
                isq = b < 2
                sg = stg[si % 4]; si += 1
                for c in range(4):
                    pq = self.nextps()
                    for kc in range(8):
                        P.mm(pq, blk[:, kc, c * 128:(c + 1) * 128], h[:, kc, :], start=(kc == 0), stop=(kc == 7))
                    s_ = sqt[c % 2]
                    P.act(s_, pq, AF.Square)
                    if pend is not None:
                        finish(pend)
                    pend = (pq, s_, c, isq, sg, b)
            finish(pend)
            if i + 1 < self.NTL:
                prologue(i + 1)
            for b in range(2):
                blk = ws.next().re("p (k n) -> p k n", k=8)
                for sub in range(4):
                    pv = self.nextps()
                    for kc in range(8):
                        P.mm(pv, h[:, kc, sub * 128:(sub + 1) * 128], blk[:, kc, :], start=(kc == 0), stop=(kc == 7))
                    dstv = vst[:, sub, :].re("p (h e) -> p h e", e=65)[:, b * 8:(b + 1) * 8, 0:64]
                    P.cp("act" if sub % 2 else "dve", dstv, pv.re("p (h e) -> p h e", e=64))
            P.st(qa, self.VAs[t0:t0 + 512, :].rearrange("(s p) f -> p s f", p=128), vst)
            for b in range(4):
                blk = ws.next().re("p (k n) -> p k n", k=8)
                sg = stg[si % 4]; si += 1
                for sub in range(4):
                    pz = self.nextps()
                    for kc in range(8):
                        P.mm(pz, h[:, kc, sub * 128:(sub + 1) * 128], blk[:, kc, :], start=(kc == 0), stop=(kc == 7))
                    P.act(sg[:, sub, :], pz, AF.Silu)
                P.st(qa, self.Zs[t0:t0 + 512, b * 512:(b + 1) * 512].rearrange("(s p) f -> p s f", p=128), sg)
            for b in range(6):
                blk = ws.next().re("p (k n) -> p k n", k=8)
                sg = stg[si % 4]; si += 1
                for c in range(4):
                    pu = self.nextps()
                    for kc in range(8):
                        P.mm(pu, blk[:, kc, c * 128:(c + 1) * 128], h[:, kc, :], start=(kc == 0), stop=(kc == 7))
                    P.cp("act" if c % 2 else "dve", sg[:, c, :], pu)
                P.st(qa, self.Us[b * 4:b * 4 + 4, :, 2 + t0:2 + t0 + 512].rearrange("c p t -> p c t"), sg)
            for b in range(4):
                blk = ws.next().re("p (k n) -> p k n", k=8)
                sg = stg[si % 4]; si += 1
                for c in range(4):
                    pg = self.nextps()
                    for kc in range(8):
                        P.mm(pg, blk[:, kc, c * 128:(c + 1) * 128], h[:, kc, :], start=(kc == 0), stop=(kc == 7))
                    P.act(sg[:, c, :], pg, AF.Sigmoid)
                P.st(qa, self.Gs[b * 4:b * 4 + 4, :, t0:t0 + 512].rearrange("c p t -> p c t"), sg)
            blk = ws.next()[:, 0:512].re("p (k n) -> p k n", k=8)
            for sub in range(4):
                pd = self.nextps()
                for kc in range(8):
                    P.mm(pd[:, 0:64], h[:, kc, sub * 128:(sub + 1) * 128], blk[:, kc, :], start=(kc == 0), stop=(kc == 7))
                t1, t2 = dsm[0], dsm[1]
                P.tt("dve", t1, pd[:, 0:64], dtbb, ALU.add)
                P.act(t1, t1, AF.Exp)
                P.act(dtst[:, sub, :], t1, AF.Ln, bias=1.0)
                P.tt("dve", t2, dtst[:, sub, :], ab, ALU.mult)
                pc = self.nextps()
                P.mm(pc[:, 0:32], self.MF, t2[:, 0:32])
                P.mm(pc[:, 32:64], self.MB, t2[:, 32:64])
                P.cp("dve", acst[:, sub, :], pc[:, 0:64])
                pf = self.nextps()
                P.mm(pf[0:32, 0:128], t2[:, 0:32], self.MF)
                P.mm(pf[32:64, 0:128], t2[:, 32:64], self.MB)
                P.cp("act", acsf[:, sub * 128:(sub + 1) * 128], pf[0:64, 0:128])
            P.st(qa, self.DTs[t0:t0 + 512, :].rearrange("(s p) f -> p s f", p=128), dtst)
            P.st(qa, self.ACSs[t0:t0 + 512, :].rearrange("(s p) f -> p s f", p=128), acst)
            P.st(qa, self.ACSF[:, t0:t0 + 512], acsf)
            c0 = i * 4
            P.st(qa, self.ACSE[c0:c0 + 4, 0:32].unsqueeze(0), acst[127:128, :, 0:32])
            P.st(qa, self.ACSE[c0:c0 + 4, 32:64].unsqueeze(0), acst[0:1, :, 32:64])
        P.barrier()
        A.reset(m0)

    def stage2a(self, L):
        P, A, NT = self.P, self.A, self.NT
        m0 = A.mark()
        q = "sp"
        cw = A.alloc([128, 120], F32)
        P.ld(q, cw, self.convw[L])
        cbc = A.alloc([128, 24], F32, "cbc")
        P.ld(q, cbc, self.convbc[L])
        cbr32 = A.alloc([1, 3072], F32)
        P.ld(q, cbr32, self.convbr[L])
        cbr = A.alloc([128, 2560], BF16, "cbr")
        P.memset("pool", cbr, 0.0)
        P.cp("dve", cbr[0:1, :], cbr32[0:1, 0:2560])
        diag = A.alloc([128, 5, 24, 128], BF16, "diag")
        for j in range(5):
            for c in range(24):
                P.ts("pool" if (j * 24 + c) % 2 else "dve", diag[:, j, c, :], self.identf,
                     cw[:, j * 24 + c:j * 24 + c + 1], None, ALU.mult)
        NB = 3
        uw = [A.alloc([128, 24, 132], BF16, f"uw{i}") for i in range(NB)]
        dtt = [A.alloc([128, 64], F32, f"dtt{i}") for i in range(NB)]
        acs = [A.alloc([128, 64], F32, f"acs{i}") for i in range(NB)]
        ace = [A.alloc([128, 64], F32, f"ace{i}") for i in range(NB)]
        xtk = [A.alloc([128, 2048], BF16, f"xtk{i}") for i in range(NB)]
        btk = [A.alloc([128, 512], BF16, f"btk{i}") for i in range(NB)]
        bfm = [A.alloc([128, 4, 128], BF16, f"bfm{i}") for i in range(NB)]
        cfm = [A.alloc([128, 4, 128], BF16, f"cfm{i}") for i in range(NB)]
        xdd = [A.alloc([128, 2048], BF16, f"xdd{i}") for i in range(NB)]
        pbb = [A.alloc([128, 2048], BF16, f"pbb{i}") for i in range(NB)]
        sm = [A.alloc([128, 32], F32, f"sm{i}") for i in range(4)]
        Pb = A.alloc([128, 2048], F32, "Pb")
        P.memset("pool", Pb, 0.0)

        def loads(c):
            k = c % NB
            P.ld(q, uw[k], self.Us[:, :, c * 128:c * 128 + 132].rearrange("k p t -> p k t"))
            P.ld(q, dtt[k], self.DTs[c * 128:(c + 1) * 128, :])
            P.ld(q, acs[k], self.ACSs[c * 128:(c + 1) * 128, :])
            P.ld(q, ace[k], self.ACSE[c:c + 1, :].broadcast_to([128, 64]))

        def conv(c):
            k = c % NB
            u = uw[k]
            if c % 16 == 0:
                P.ts("dve", u[:, :, 0:2], u[:, :, 0:2], self.flagc[:, 0:1], None, ALU.mult)
            if c % 16 == 15:
                P.ts("dve", u[:, :, 130:132], u[:, :, 130:132], self.flagc[:, 0:1], None, ALU.mult)
            for cc in range(20):
                if cc % 4 == 0:
                    ps = self.nextps()
                o = ps[:, (cc % 4) * 128:(cc % 4) * 128 + 128]
                for j in range(5):
                    P.mm(o, u[:, cc, j:j + 128], diag[:, j, cc, :], start=(j == 0), stop=False)
                P.mm(o, self.onesb, cbr[:, cc * 128:(cc + 1) * 128], start=False, stop=True)
                if cc % 4 == 3:
                    if cc < 16:
                        P.act(xtk[k][:, (cc // 4) * 512:(cc // 4 + 1) * 512], ps, AF.Silu)
                    else:
                        P.act(btk[k], ps, AF.Silu)
            for which, dst in ((16, bfm[k]), (20, cfm[k])):
                ps = self.nextps()
                for g in range(4):
                    cc = which + g
                    o = ps[:, g * 128:(g + 1) * 128]
                    for j in range(5):
                        P.mm(o, diag[:, j, cc, :], u[:, cc, j:j + 128], start=(j == 0), stop=(j == 4))
                for g in range(4):
                    cc = which + g
                    P.act(dst[:, g, :], ps[:, g * 128:(g + 1) * 128], AF.Silu, bias=cbc[:, cc:cc + 1])
            P.st(q, self.XSs[c * 128:(c + 1) * 128, :], xtk[k])
            P.st(q, self.BTs[c * 128:(c + 1) * 128, :], btk[k])
            P.st(q, self.BFs[:, :, c * 128:(c + 1) * 128].rearrange("g p t -> p g t"), bfm[k])
            P.st(q, self.CFs[:, :, c * 128:(c + 1) * 128].rearrange("g p t -> p g t"), cfm[k])

        def state(c):
            k = c % NB
            if c % 16 == 15:
                P.ts("dve", Pb, Pb, self.flagc[:, 0:1], None, ALU.mult)
            P.cp("act", pbb[k], Pb)
            P.st(q, self.PBs[c], pbb[k])
            d1, d2, d3 = sm[0], sm[1], sm[2]
            P.tt("dve", d1, ace[k][:, 32:64], acs[k][:, 32:64], ALU.subtract)
            P.act(d1, d1, AF.Exp)
            P.tt("dve", d2, d1, dtt[k][:, 32:64], ALU.mult)
            P.act(d3, ace[k][:, 32:64], AF.Exp)
            P.tt("dve", xdd[k].re("p (h e) -> p h e", e=64), xtk[k].re("p (h e) -> p h e", e=64),
                 d2.re("p (h o) -> p h o", o=1).bc([128, 32, 64]), ALU.mult)
            for g in range(4):
                ps = self.nextps()
                P.mm(ps, btk[k][:, g * 128:(g + 1) * 128], xdd[k][:, g * 512:(g + 1) * 512])
                pv = Pb[:, g * 512:(g + 1) * 512]
                P.tt("dve", pv.re("p (h e) -> p h e", e=64), pv.re("p (h e) -> p h e", e=64),
                     d3[:, g * 8:(g + 1) * 8].re("p (h o) -> p h o", o=1).bc([128, 8, 64]), ALU.mult)
                P.tt("dve", pv, pv, ps, ALU.add)

        order = list(range(self.NCH - 1, -1, -1))
        loads(order[0])
        for n, c in enumerate(order):
            if n + 1 < len(order):
                loads(order[n + 1])
            conv(c)
            if n > 0:
                state(order[n - 1])
        state(order[-1])
        P.barrier()
        A.reset(m0)

    def stage2b(self, L):
        P, A, NT = self.P, self.A, self.NT
        m0 = A.mark()
        q = "sp"
        dsb = A.alloc([128, 32], F32)
        P.ld(q, dsb, self.dsk[L].broadcast_to([128, 32]))
        dskd = A.alloc([128, 32, 128], BF16, "dskd")
        for h in range(32):
            P.ts("dve", dskd[:, h, :], self.identf, dsb[:, h:h + 1], None, ALU.mult)
        nwb = A.alloc([128, 2048], F32, "nwb")
        P.ld(q, nwb, self.snw[L].broadcast_to([128, 2048]))
        NB = 2
        xtk = [A.alloc([128, 2048], BF16, f"xtk{i}") for i in range(NB)]
        btk = [A.alloc([128, 512], BF16, f"btk{i}") for i in range(NB)]
        bfm = [A.alloc([128, 4, 128], BF16, f"bfm{i}") for i in range(NB)]
        cfm = [A.alloc([128, 4, 128], BF16, f"cfm{i}") for i in range(NB)]
        dtt = [A.alloc([128, 64], F32, f"dtt{i}") for i in range(NB)]
        acs = [A.alloc([128, 64], F32, f"acs{i}") for i in range(NB)]
        ace = [A.alloc([128, 64], F32, f"ace{i}") for i in range(NB)]
        pbb = [A.alloc([128, 2048], BF16, f"pbb{i}") for i in range(NB)]
        zt = [A.alloc([128, 2048], BF16, f"zt{i}") for i in range(NB)]
        lrow = [[A.alloc([128, 32, 128], F32, f"lrow{d}{i}") for i in range(NB)] for d in range(2)]
        Pf = [A.alloc([128, 512], F32, f"Pf{g}") for g in range(4)]
        pfb = [A.alloc([128, 512], BF16, f"pfb{g}") for g in range(4)]
        for g in range(4):
            P.memset("pool", Pf[g], 0.0)
            P.memset("pool", pfb[g], 0.0)
        cbm = [A.alloc([128, 4, 128], BF16 if EMBF else F32, f"cbm{d}") for d in range(2)]
        eacs = A.alloc([128, 64], F32, "eacs")
        yo = [A.alloc([128, 2048], BF16, f"yo{d}") for d in range(2)]
        y = A.alloc([128, 2048], F32, "y")
        sst = y
        xddg = [A.alloc([128, 512], BF16, f"xddg{g}") for g in range(4)]
        junk = A.alloc([128, 512], F32, "junk")
        nacs = A.alloc([128, 64], F32, "nacs")
        Em = [A.alloc([128, 128], BF16 if EMBF else F32, f"Em{i}") for i in range(6)]
        GT = [A.alloc([128, 128], BF16, f"GT{i}") for i in range(8)]
        sm = [A.alloc([128, 32], F32, f"sm{i}") for i in range(4)]
        ssq = A.alloc([128, 4], F32, "ssq")
        rs = A.alloc([128, 4], F32, "rs")
        sT = [A.alloc([128, 16, 128], BF16, f"sT{i}") for i in range(2)]

        def loads(c):
            k = c % NB
            sl = slice(c * 128, (c + 1) * 128)
            P.ld(q, xtk[k], self.XSs[sl, :])
            P.ld(q, btk[k], self.BTs[sl, :])
            P.ld(q, bfm[k], self.BFs[:, :, sl].rearrange("g p t -> p g t"))
            P.ld(q, cfm[k], self.CFs[:, :, sl].rearrange("g p t -> p g t"))
            P.ld(q, dtt[k], self.DTs[sl, :])
            P.ld(q, acs[k], self.ACSs[sl, :])
            P.ld(q, ace[k], self.ACSE[c:c + 1, :].broadcast_to([128, 64]))
            P.ld(q, pbb[k], self.PBs[c])
            P.ld(q, zt[k], self.Zs[sl, :])
            for d in range(2):
                P.ld(q, lrow[d][k], self.ACSF[d * 32:(d + 1) * 32, sl].unsqueeze(0).broadcast_to([128, 32, 128]))

        def tail(c):
            P.memset("pool", ssq, 0.0)
            for g in range(4):
                P.act(junk, y[:, g * 512:(g + 1) * 512], AF.Square, accum=ssq[:, g:g + 1])
            P.act(rs, ssq, AF.Ln, bias=EPS, scale=1.0 / 512.0)
            P.act(rs, rs, AF.Exp, scale=-0.5)
            for g in range(4):
                P.stt("dve", sst[:, g * 512:(g + 1) * 512], y[:, g * 512:(g + 1) * 512],
                      rs[:, g:g + 1], nwb[:, g * 512:(g + 1) * 512], ALU.mult, ALU.mult)
            so = sT[c % 2]
            for cc in range(16):
                if cc % 4 == 0:
                    ps = self.nextps()
                P.tr(ps[:, (cc % 4) * 128:(cc % 4) * 128 + 128], sst[:, cc * 128:(cc + 1) * 128], self.identf)
                if cc % 4 == 3:
                    P.cp("act" if (cc // 4) % 2 else "dve", so[:, cc - 3:cc + 1, :], ps.re("p (k t) -> p k t", k=4))
            P.st(q, self.SSMs[:, :, c * 128:(c + 1) * 128].rearrange("k p t -> p k t"), so)

        loads(0)
        gi = 0
        for c in range(self.NCH):
            k = c % NB
            if c + 1 < self.NCH:
                loads(c + 1)
            if c % 16 == 0 and c > 0:
                for g in range(4):
                    P.ts("dve", Pf[g], Pf[g], self.flagc[:, 0:1], None, ALU.mult)
                    P.cp("act", pfb[g], Pf[g])
            P.act(eacs, acs[k], AF.Exp)
            for d in range(2):
                for g in range(4):
                    st_ = pfb[g] if d == 0 else pbb[k][:, g * 512:(g + 1) * 512]
                    ps = self.nextps()
                    P.mm(ps, cfm[k][:, g, :], st_)
                    P.tt("dve", yo[d][:, g * 512:(g + 1) * 512].re("p (h e) -> p h e", e=64),
                         ps.re("p (h e) -> p h e", e=64),
                         eacs[:, d * 32 + g * 8:d * 32 + g * 8 + 8].re("p (h o) -> p h o", o=1).bc([128, 8, 64]),
                         ALU.mult)
            if c > 0:
                tail(c - 1)
            d1, d2, d3 = sm[0], sm[1], sm[2]
            P.tt("dve", d1, ace[k][:, 0:32], acs[k][:, 0:32], ALU.subtract)
            P.act(d1, d1, AF.Exp)
            P.tt("dve", d2, d1, dtt[k][:, 0:32], ALU.mult)
            P.act(d3, ace[k][:, 0:32], AF.Exp)
            for g in range(4):
                P.tt("pool", xddg[g].re("p (h e) -> p h e", e=64),
                     xtk[k][:, g * 512:(g + 1) * 512].re("p (h e) -> p h e", e=64),
                     d2[:, g * 8:(g + 1) * 8].re("p (h o) -> p h o", o=1).bc([128, 8, 64]), ALU.mult)
                P.tt("pool", Pf[g].re("p (h e) -> p h e", e=64), Pf[g].re("p (h e) -> p h e", e=64),
                     d3[:, g * 8:(g + 1) * 8].re("p (h o) -> p h o", o=1).bc([128, 8, 64]), ALU.mult)
            pcb = self.nextps()
            for g in range(4):
                P.mm(pcb[:, g * 128:(g + 1) * 128], bfm[k][:, g, :], cfm[k][:, g, :])
            pcb3 = pcb.re("p (g t) -> p g t", g=4)
            P.tt("dve", cbm[0], pcb3, self.MF.re("p (o t) -> p o t", o=1).bc([128, 4, 128]), ALU.mult)
            P.tt("dve", cbm[1], pcb3, self.MB.re("p (o t) -> p o t", o=1).bc([128, 4, 128]), ALU.mult)
            P.act(nacs, dtt[k], AF.Ln)
            P.tt("dve", nacs, nacs, acs[k], ALU.subtract)
            for g in range(4):
                psy = self.nextps()
                P.mm(psy, self.identb, yo[0][:, g * 512:(g + 1) * 512], start=True, stop=False)
                P.mm(psy, self.identb, yo[1][:, g * 512:(g + 1) * 512], start=False, stop=False)
                for hh in range(8):
                    h = g * 8 + hh
                    o = psy[:, hh * 64:(hh + 1) * 64]
                    xh = xtk[k][:, h * 64:(h + 1) * 64]
                    for d in range(2):
                        hd = d * 32 + h
                        em, gt = Em[gi % 6], GT[gi % 8]
                        gi += 1
                        P.act(em, lrow[d][k][:, h, :], AF.Exp, bias=nacs[:, hd:hd + 1])
                        P.stt("dve", gt, em, 1e30, cbm[d][:, g, :], ALU.min, ALU.mult)
                        P.mm(o, gt, xh, start=False, stop=False)
                    P.mm(o, dskd[:, h, :], xh, start=False, stop=(hh == 7))
                P.tt("dve", y[:, g * 512:(g + 1) * 512], psy, zt[k][:, g * 512:(g + 1) * 512], ALU.mult)
                ps = self.nextps()
                P.mm(ps, btk[k][:, g * 128:(g + 1) * 128], xddg[g])
                P.tt("dve", Pf[g], Pf[g], ps, ALU.add)
                P.cp("pool", pfb[g], Pf[g])
        tail(self.NCH - 1)
        P.barrier()
        A.reset(m0)

    def stage2c(self, L):
        P, A, NT, R = self.P, self.A, self.NT, self.R
        m0 = A.mark()
        q = "sp"
        rr = A.alloc([120, 62], F32)
        P.ld(q, rr, self.relrow[L][0].rearrange("(p f) -> p f", p=120))
        qk = A.alloc([1, 128], F32)
        P.ld(q, qk, self.qkrow[L])
        mx = A.alloc([1, 4], F32)
        rmx = A.alloc([120, 1], F32)
        rmt = A.alloc([1, 120], F32)

        def amax(o, i):
            oa, ia = o.ap, i.ap
            P.op("dve", lambda e: e.tensor_reduce(out=oa, in_=ia, axis=AX.X, op=ALU.max, apply_absolute_value=True), [i], [o])

        amax(rmx, rr)
        ptm = self.nextps()
        P.tr(ptm[0:1, 0:120], rmx, self.identf[0:120, 0:120])
        P.cp("dve", rmt, ptm[0:1, 0:120])
        amax(mx[:, 0:1], rmt)
        amax(mx[:, 1:2], qk[:, 0:64])
        amax(mx[:, 2:3], qk[:, 64:128])
        P.tt("dve", mx[:, 3:4], mx[:, 1:2], mx[:, 2:3], ALU.mult)
        P.stt("dve", mx[:, 3:4], mx[:, 3:4], -8.0, mx[:, 0:1], ALU.mult, ALU.subtract)
        pm = self.nextps()
        P.mm(pm[:, 0:1], self._ones_row(), mx[:, 3:4])
        negM = A.alloc([128, 1], F32, "negM")
        P.cp("dve", negM, pm[:, 0:1])
        tab = A.alloc([128, NH, NTAB * 64], BF16, "tab")
        m1 = A.mark()
        for h0 in range(0, NH, 2):
            t32 = A.alloc([128, 2 * NTAB * 64], F32)
            P.ld(q, t32, self.tabf[L][:, h0 * NTAB * 64:(h0 + 2) * NTAB * 64])
            P.cp("dve" if (h0 // 2) % 2 else "pool", tab[:, h0:h0 + 2, :].re("p h n -> p (h n)"), t32)
            if h0 % 4 == 2:
                P.barrier()
                A.reset(m1)
        A.reset(m1)
        NB = 2
        qz = [[A.alloc([128, 8, 256], BF16, f"qz{par}{i}") for i in range(NB)] for par in range(2)]
        kz = [[A.alloc([128, 8, 768], BF16, f"kz{par}{i}") for i in range(NB)] for par in range(2)]
        for par in range(2):
            a0 = 64 if par == 0 else 0
            for i in range(NB):
                P.memset("pool", qz[par][i], 0.0)
                P.memset("pool", kz[par][i], 0.0)
                P.cp("dve", kz[par][i][a0:a0 + 12, :, :],
                     self.ind[a0:a0 + 12, :].re("p (o t) -> p o t", o=1).bc([12, 8, 768]))
        va = [A.alloc([128, 6, NH * 65], BF16, f"va{i}") for i in range(NB)]
        PT = [A.alloc([128, 6, 256], BF16, f"PT{i}") for i in range(3)]
        atk = A.alloc([128, 2, D], F32, "atk")
        aT = [A.alloc([128, 8, 256], BF16, f"aT{i}") for i in range(2)]
        rc = [A.alloc([128, 2], F32, f"rc{i}") for i in range(2)]

        def sG(G):
            return min(max(4 * G - 4, 0), R - 12)

        def loads(G):
            k = G % NB
            t0 = G * 256
            k0 = sG(G) * 64
            for par in range(2):
                d0 = par * 64
                a0 = 64 if par == 0 else 0
                P.ld(q, qz[par][k][d0:d0 + 64, :, :], self.Qs[:, d0:d0 + 64, t0:t0 + 256].rearrange("c p t -> p c t"))
                P.ld(q, kz[par][k][d0:d0 + 64, :, :], self.Ks[:, d0:d0 + 64, k0:k0 + 768].rearrange("c p t -> p c t"))
                P.cp("dve", qz[par][k][a0:a0 + 12, :, :],
                     self.rowmb[a0:a0 + 12, G * 256:(G + 1) * 256].re("p (o t) -> p o t", o=1).bc([12, 8, 256]))
            P.ld(q, va[k], self.VAs[k0:k0 + 768, :].rearrange("(b p) f -> p b f", p=128))

        loads(0)
        hi = 0
        for G in range(self.NG):
            k = G % NB
            if G + 1 < self.NG:
                loads(G + 1)
            s_g = sG(G)
            def scores(h):
                ch = h // 2
                par = h % 2
                pt = PT[(G * NH + h) % 3]
                for kb in range(6):
                    if kb % 2 == 0:
                        psb = self.nextps()
                    o = psb[:, (kb % 2) * 256:(kb % 2) * 256 + 256]
                    P.mm(o, kz[par][k][:, ch, kb * 128:(kb + 1) * 128], qz[par][k][:, ch, :], start=True, stop=False)
                    b0 = 10 - (s_g + 2 * kb - 4 * G)
                    P.mm(o, self.identb, tab[:, h, b0 * 64:(b0 + 4) * 64], start=False, stop=True)
                    if kb % 2 == 1:
                        P.act(pt[:, kb - 1:kb + 1, :].re("p a b -> p (a b)"), psb, AF.Exp, bias=negM[:, 0:1])
                return pt

            def pv(h, pt):
                pso = self.nextps()
                for qh in range(2):
                    for kb in range(6):
                        P.mm(pso[:, qh * 65:qh * 65 + 65], pt[:, kb, qh * 128:(qh + 1) * 128],
                             va[k][:, kb, h * 65:(h + 1) * 65], start=(kb == 0), stop=(kb == 5))
                r_ = rc[h % 2]
                P.recip(r_, pso[:, 0:130].re("p (a b) -> p a b", b=65)[:, :, 64])
                for qh in range(2):
                    P.ts("dve", atk[:, qh, h * 64:(h + 1) * 64], pso[:, qh * 65:qh * 65 + 64], r_[:, qh:qh + 1], None, ALU.mult)

            prev = None
            for h in range(NH):
                pt = scores(h)
                if not PIPE:
                    pv(h, pt)
                    continue
                if prev is not None:
                    pv(*prev)
                prev = (h, pt)
            if PIPE:
                pv(*prev)
            ao = aT[G % 2]
            for qh in range(2):
                for cc in range(8):
                    if cc % 4 == 0:
                        ps = self.nextps()
                    P.tr(ps[:, (cc % 4) * 128:(cc % 4) * 128 + 128], atk[:, qh, cc * 128:(cc + 1) * 128], self.identf)
                    if cc % 4 == 3:
                        P.cp("act" if (cc // 4) % 2 else "dve", ao[:, cc - 3:cc + 1, qh * 128:(qh + 1) * 128],
                             ps.re("p (k t) -> p k t", k=4))
            P.st(q, self.ATTs[:, :, G * 256:(G + 1) * 256].rearrange("c p t -> p c t"), ao)
        P.barrier()
        A.reset(m0)

    def _ones_row(self):
        return self.MF[0:1, :]

    def stage3(self, L, last):
        P, A, NT = self.P, self.A, self.NT
        m0 = A.mark()
        qa = "pool"
        plan = []
        for i in range(self.NTL):
            plan += [self.wblk(L, B_ATT + b) for b in range(2)]
            plan += [self.wblk(L, B_SSM + b) for b in range(4)]
            plan += [self.wblk(L, B_OUT + b) for b in range(2)]
            plan += [self.wblk(L, B_UP2 + b) for b in range(11)]
            plan += [self.wblk(L, B_DN2 + b, 22 * 128) for b in range(8)]
        ws = self.WStream(self, plan)
        xT = A.alloc([128, 8, 512], F32, "xT")
        aT = A.alloc([128, 8, 512], BF16, "aTl")
        sT = A.alloc([128, 16, 512], BF16, "sTl")
        gt = A.alloc([128, 16, 512], BF16, "gt")
        t1 = A.alloc([128, 8, 512], F32, "t1")
        mg = A.alloc([128, 8, 512], BF16, "mg")
        h = A.alloc([128, 8, 512], BF16, "h")
        sq = A.alloc([128, 8, 512], BF16, "sq")
        actb = A.alloc([128, 22, 512], BF16, "actb")
        tmps = [A.alloc([128, 512], F32, f"tmp{i}") for i in range(4)]
        rst = A.alloc([128, 512], F32, "rst")
        xo = t1.re("p k t -> p (k t)").re("p (a d) -> p a d", a=4) if last else None
        g3 = self.lnpc[:, (L * 3 + 2) * 8:(L * 3 + 2) * 8 + 8]
        for i in range(self.NTL):
            t0 = i * 512
            P.ld(qa, aT, self.ATTs[:, :, t0:t0 + 512].rearrange("c p t -> p c t"))
            P.ld(qa, sT, self.SSMs[:, :, t0:t0 + 512].rearrange("c p t -> p c t"))
            P.ld(qa, gt, self.Gs[:, :, t0:t0 + 512].rearrange("c p t -> p c t"))
            P.ld(qa, xT, self.X1[i].rearrange("p (k t) -> p k t", k=8))
            for m in range(8):
                if m % 4 == 0:
                    blk = ws.next().re("p (k n) -> p k n", k=8)
                pa = self.nextps()
                for kc in range(8):
                    P.mm(pa, blk[:, kc, (m % 4) * 128:(m % 4) * 128 + 128], aT[:, kc, :], start=(kc == 0), stop=(kc == 7))
                P.tt("dve", t1[:, m, :], pa, gt[:, m, :], ALU.mult)
            for m in range(8):
                if m % 2 == 0:
                    blk = ws.next().re("p (k n) -> p k n", k=16)
                ps = self.nextps()
                for kc in range(16):
                    P.mm(ps, blk[:, kc, (m % 2) * 128:(m % 2) * 128 + 128], sT[:, kc, :], start=(kc == 0), stop=(kc == 15))
                t = tmps[m % 4]
                P.tt("dve", t, ps, gt[:, 8 + m, :], ALU.mult)
                P.tt("pool", mg[:, m, :], t, t1[:, m, :], ALU.add)
            for m in range(8):
                if m % 4 == 0:
                    blk = ws.next().re("p (k n) -> p k n", k=8)
                po = self.nextps()
                for kc in range(8):
                    P.mm(po, blk[:, kc, (m % 4) * 128:(m % 4) * 128 + 128], mg[:, kc, :], start=(kc == 0), stop=(kc == 7))
                P.tt("dve", xT[:, m, :], po, xT[:, m, :], ALU.add)
            self.rmsnorm(xT, g3, h, sq, rst)
            self.ffn(ws, xT, h, actb, tmps)
            if not last:
                P.st(qa, self.XL[i].rearrange("p (k t) -> p k t", k=8), xT)
            else:
                for sub in range(4):
                    for kc in range(8):
                        if kc % 4 == 0:
                            ps = self.nextps()
                        P.tr(ps[:, (kc % 4) * 128:(kc % 4) * 128 + 128], xT[:, kc, sub * 128:(sub + 1) * 128], self.identf)
                        if kc % 4 == 3:
                            P.cp("act" if (kc // 4) % 2 else "dve", xo[:, sub, (kc - 3) * 128:(kc + 1) * 128], ps)
                ev = P.st(qa, self.y_out[t0:t0 + 512, :].rearrange("(s p) d -> p s d", p=128), xo)
                self.final.append(ev)
        P.barrier()
        A.reset(m0)

    def build(self):
        self.persistent()
        self.cast_weights()
        for L in range(self.nlayers):
            last = (L == self.nlayers - 1)
            self.stage1(L)
            self.stage2a(L)
            self.stage2b(L)
            self.stage2c(L)
            self.stage3(L, last)
        self.P.emit(self.final)
        return self.nc


def _tables(inp):
    f = np.float32
    t = {}
    lnp = np.zeros((128, 48), f)
    for l in range(2):
        for w, nm in enumerate(("ln_ffn1", "ln_mix", "ln_ffn2")):
            lnp[:, (l * 3 + w) * 8:(l * 3 + w) * 8 + 8] = np.asarray(inp[nm][l], f).reshape(8, 128).T
    t["lnp"] = lnp
    qkn = np.zeros((128, 4), f)
    for l in range(2):
        qkn[:, l * 2] = np.tile(np.asarray(inp["q_norm"][l], f), 2)
        qkn[:, l * 2 + 1] = np.tile(np.asarray(inp["k_norm"][l], f), 2)
    t["qkn"] = qkn
    t["qkrow"] = np.stack([np.concatenate([inp["q_norm"][l], inp["k_norm"][l]]) for l in range(2)]).astype(f).reshape(2, 1, 128)
    rel = np.asarray(inp["rel_bias"], f)
    t["relrow"] = rel.reshape(2, 1, -1)
    cols = np.arange(64)
    cstart = np.clip(cols - 8, 0, 48)
    cvalid = (cols[None, :] >= cstart[:, None]) & (cols[None, :] < cstart[:, None] + 16)
    cidx = np.clip(cols[None, :] - cols[:, None] + 15, 0, 30)
    tab = np.zeros((2, 128, NH, NTAB, 64), f)
    for kl in range(2):
        for b in range(NTAB):
            delta = 17 - b + kl
            if 0 <= delta <= 14:
                blk = rel[:, :, delta, :][:, :, cidx]
                blk = np.where(cvalid[None, None], blk, f(NEG))
                tab[:, kl * 64:(kl + 1) * 64, :, b, :] = blk.transpose(0, 3, 1, 2)
    t["tabf"] = tab.reshape(2, 128, NH * NTAB * 64)
    cw = np.asarray(inp["conv_w"], f)
    t["convw"] = cw.reshape(2, 5, 24, 128).transpose(0, 3, 1, 2).reshape(2, 128, 120).copy()
    cb = np.asarray(inp["conv_b"], f)
    t["convbc"] = cb.reshape(2, 24, 128).transpose(0, 2, 1).copy()
    t["convbr"] = cb.reshape(2, 1, 3072)
    t["dtb"] = np.concatenate([inp["dt_bias_fwd"], inp["dt_bias_bwd"]], axis=1).astype(f).reshape(2, 1, 64)
    t["alog"] = np.concatenate([inp["a_log_fwd"], inp["a_log_bwd"]], axis=1).astype(f).reshape(2, 1, 64)
    t["dsk"] = np.asarray(inp["d_skip"], f).reshape(2, 1, 32)
    t["snw"] = np.asarray(inp["ssm_norm"], f).reshape(2, 1, 2048)
    cst = np.zeros((128, 640), f)
    cst[:, 0:128] = np.eye(128)
    cst[:, 128:256] = np.triu(np.ones((128, 128)))
    cst[:, 256:384] = np.tril(np.ones((128, 128)))
    cst[0:64, 384:448] = 1.0
    cst[64:128, 448:512] = 1.0
    cst[:, 512:640] = 1.0
    t["cst"] = cst
    ind = np.zeros((12, 768), f)
    for c in range(12):
        ind[c, c * 64:(c + 1) * 64] = 1.0
    t["indm"] = ind
    for nm in ("w_ffn1_up", "w_ffn1_down", "w_in", "w_attn_proj", "w_ssm_proj", "w_out", "w_ffn2_up", "w_ffn2_down"):
        t[nm] = np.ascontiguousarray(inp[nm], dtype=f)
    return t


def _rowmask(NT, SR):
    R, NG = NT // 64, NT // 256
    rm = np.zeros((12, NG, 4, 64), np.float32)
    for G in range(NG):
        s = min(max(4 * G - 4, 0), R - 12)
        for c in range(12):
            kap = s + c
            for rl in range(4):
                rho = 4 * G + rl
                r = rho % SR
                r0 = min(max(r - 4, 0), SR - 8)
                ok = (kap // SR == rho // SR) and (r0 <= kap % SR <= r0 + 7)
                rm[c, G, rl, :] = 0.0 if ok else NEG
    return rm.reshape(12, NG * 256)


def run_cores(inp, xs, seq_rows, NT, nlayers=2):
    bld = Builder(NT, nlayers=nlayers)
    nc = bld.build()
    t = _tables(inp)
    in_maps = []
    for x, SR in zip(xs, seq_rows):
        m = dict(t)
        m["x"] = np.ascontiguousarray(x, dtype=np.float32)
        m["rowm"] = _rowmask(NT, SR)
        m["flag"] = np.full((128, 1), 1.0 if SR * 64 == NT else 0.0, np.float32)
        in_maps.append(m)
    res = run_bass_kernel_spmd(nc, in_maps, core_ids=list(range(len(xs))))
    return [r["y"] for r in res.results]


def kernel(**inputs):
    inp = {k: np.asarray(v) for k, v in inputs.items()}
    xp = inp["x_prompt"]
    xsm = inp["x_sample"]
    xs = [xp[i] for i in range(4)] + [xsm[4 * j:4 * j + 4].reshape(8192, 1024) for j in range(4)]
    ys = run_cores(inp, xs, [128] * 4 + [32] * 4, 8192)
    y_prompt = np.stack(ys[0:4]).astype(np.float32)
    y_sample = np.concatenate([ys[4 + j].reshape(4, 2048, 1024) for j in range(4)], axis=0).astype(np.float32)
    return (y_prompt, y_sample)
```
